# Optimizing a Trainium2 kernel written in Bass

```python
import jax
import jax.numpy as jnp
from jax import lax
import numpy as np

D_MODEL = 4096
BATCH = 4
SEQ = 4096
DEPTH = 4

GRID_W = 64
CTX_LEN = 256
HEAD_DIM = 128
N_MIX_HEADS = D_MODEL // HEAD_DIM
A_HEADS = 3 * N_MIX_HEADS // 8
A_KV_HEADS = A_HEADS // 3
C_HEADS = 3 * N_MIX_HEADS // 8
B_CH = D_MODEL - (A_HEADS + C_HEADS) * HEAD_DIM
A_Q = A_HEADS * HEAD_DIM
A_KV = A_KV_HEADS * HEAD_DIM
C_W = C_HEADS * HEAD_DIM
IN_W = A_Q + 2 * A_KV + 2 * B_CH + 3 * C_W
IN_SPLITS = (A_Q, A_Q + A_KV, A_Q + 2 * A_KV, A_Q + 2 * A_KV + 2 * B_CH,
             A_Q + 2 * A_KV + 2 * B_CH + C_W, A_Q + 2 * A_KV + 2 * B_CH + 2 * C_W)
B_CONV = 31
NA_ROWS = 8
NA_COLS = 16
D_FF = 11 * D_MODEL // 8
FFN_CONV = 3
Q_BLOCK = 128
ROPE_THETA = 10000.0
EPS = 1e-6
N_MOD = 6
NEG_INF = -1e30

kernel_name = 'hybrid_gqa_conformer_natten_dit'


def rms_norm(x, g):
    xf = x.astype(jnp.float32)
    y = xf * lax.rsqrt(jnp.mean(xf * xf, axis=-1, keepdims=True) + EPS)
    return (y * g.astype(jnp.float32)).astype(x.dtype)


def layer_norm(x, g, b):
    xf = x.astype(jnp.float32)
    mu = jnp.mean(xf, axis=-1, keepdims=True)
    var = jnp.mean(jnp.square(xf - mu), axis=-1, keepdims=True)
    y = (xf - mu) * lax.rsqrt(var + EPS)
    return (y * g.astype(jnp.float32) + b.astype(jnp.float32)).astype(x.dtype)


def modulate(h, shift, scale):
    return h * (1 + scale) + shift


def heads(t, n):
    return t.reshape(t.shape[:-1] + (n, HEAD_DIM))


def axial_rope_tables(n_tokens):
    t = jnp.arange(n_tokens, dtype=jnp.int32)
    row = (t // GRID_W).astype(jnp.float32)
    col = (t % GRID_W).astype(jnp.float32)
    half = HEAD_DIM // 2
    inv = ROPE_THETA ** (-jnp.arange(0, half, 2, dtype=jnp.float32) / half)
    ang = jnp.concatenate([row[:, None] * inv, col[:, None] * inv], axis=-1)
    return jnp.cos(ang), jnp.sin(ang)


def apply_rope(x, cos, sin):
    xf = x.astype(jnp.float32)
    x1, x2 = xf[..., 0::2], xf[..., 1::2]
    c, s = cos[None, :, None, :], sin[None, :, None, :]
    out = jnp.stack([x1 * c - x2 * s, x1 * s + x2 * c], axis=-1).reshape(x.shape)
    return out.astype(x.dtype)


def depthwise_conv(x, w, b):
    y = lax.conv_general_dilated(x, w[:, None, :], window_strides=(1,), padding='SAME',
                                 dimension_numbers=('NWC', 'WIO', 'NWC'),
                                 feature_group_count=x.shape[-1])
    return y + b


def gqa_attend(q, k, v):
    B, Lq, H, Dh = q.shape
    Hkv = k.shape[2]
    qg = q.reshape(B, Lq, Hkv, H // Hkv, Dh)
    s = jnp.einsum('bqkgd,bskd->bkgqs', qg, k).astype(jnp.float32) * (Dh ** -0.5)
    p = jax.nn.softmax(s, axis=-1).astype(v.dtype)
    return jnp.einsum('bkgqs,bskd->bqkgd', p, v).reshape(B, Lq, H * Dh)


def gqa_blocked(q, k, v):
    B, S, H, Dh = q.shape
    nblk = S // Q_BLOCK
    qb = jnp.swapaxes(q.reshape(B, nblk, Q_BLOCK, H, Dh), 0, 1)
    o = lax.map(lambda qq: gqa_attend(qq, k, v), qb)
    return jnp.swapaxes(o, 0, 1).reshape(B, S, H * Dh)


def neighbourhood_attention(q, k, v, k_ctx, v_ctx, rpb):
    B, S, H, Dh = q.shape
    rows = S // GRID_W
    kh = min(NA_ROWS, rows)
    scale = Dh ** -0.5
    qg = jnp.moveaxis(q.reshape(B, rows, GRID_W, H, Dh), 1, 0)
    kg = k.reshape(B, rows, GRID_W, H, Dh)
    vg = v.reshape(B, rows, GRID_W, H, Dh)
    col = jnp.arange(GRID_W, dtype=jnp.int32)
    col_start = jnp.clip(col - NA_COLS // 2, 0, GRID_W - NA_COLS)
    col_mask = (col[None, :] >= col_start[:, None]) & (col[None, :] < col_start[:, None] + NA_COLS)
    col_idx = jnp.clip(col[None, :] - col[:, None] + NA_COLS - 1, 0, 2 * NA_COLS - 2)
    row = jnp.arange(rows, dtype=jnp.int32)
    row_start = jnp.clip(row - kh // 2, 0, rows - kh)
    n_nb = kh * GRID_W

    def one_row(args):
        q_row, r, r0 = args
        k_rows = lax.dynamic_slice_in_dim(kg, r0, kh, axis=1)
        v_rows = lax.dynamic_slice_in_dim(vg, r0, kh, axis=1)
        row_idx = r0 + jnp.arange(kh, dtype=jnp.int32) - r + NA_ROWS - 1
        bias = rpb[:, row_idx[:, None, None], col_idx[None, :, :]]
        bias = jnp.transpose(bias, (0, 2, 1, 3)).astype(jnp.float32)
        s_nb = jnp.einsum('bqhd,bjkhd->bhqjk', q_row, k_rows).astype(jnp.float32) * scale + bias[None]
        s_nb = jnp.where(col_mask[:, None, :], s_nb, NEG_INF)
        s_cx = jnp.einsum('bqhd,bchd->bhqc', q_row, k_ctx).astype(jnp.float32) * scale
        s_all = jnp.concatenate([s_nb.reshape(B, H, GRID_W, n_nb), s_cx], axis=-1)
        p = jax.nn.softmax(s_all, axis=-1).astype(v.dtype)
        p_nb = p[..., :n_nb].reshape(B, H, GRID_W, kh, GRID_W)
        return (jnp.einsum('bhqjk,bjkhd->bqhd', p_nb, v_rows)
                + jnp.einsum('bhqc,bchd->bqhd', p[..., n_nb:], v_ctx))

    o = lax.map(one_row, (qg, row, row_start))
    return jnp.moveaxis(o, 0, 1).reshape(B, S, H * Dh)


def conformer_conv(u, dw_w, dw_b, ln_g, ln_b, pw_w, pw_b):
    a, g = jnp.split(u, 2, axis=-1)
    z = a * jax.nn.sigmoid(g)
    z = depthwise_conv(z, dw_w, dw_b)
    z = jax.nn.silu(layer_norm(z, ln_g, ln_b))
    return z @ pw_w + pw_b


def conv_ffn(h, w_up, dw_w, dw_b, w_down):
    u = depthwise_conv(h @ w_up, dw_w, dw_b)
    gate, val = jnp.split(u, 2, axis=-1)
    return (jax.nn.silu(gate) * val) @ w_down


def setup_inputs(seed: int = 0) -> dict:
    key = jax.random.key(seed)
    keys = iter(jax.random.split(key, 32))

    def nrm(shape, scale):
        return jax.random.normal(next(keys), shape, jnp.float32) * scale

    L, D = DEPTH, D_MODEL
    return {
        'x': nrm((BATCH, SEQ, D), 1.0),
        'c': nrm((BATCH, D), 1.0),
        'ctx': nrm((BATCH, CTX_LEN, D), 1.0),
        'c_ctx': nrm((D,), 1.0),
        'ada_w': nrm((L, D, N_MOD * D), 0.5 * D ** -0.5),
        'ada_b': nrm((L, N_MOD * D), 0.02),
        'norm1_g': 1.0 + nrm((L, D), 0.02),
        'norm2_g': 1.0 + nrm((L, D), 0.02),
        'w_in': nrm((L, D, IN_W), D ** -0.5),
        'a_qn_g': 1.0 + nrm((L, HEAD_DIM), 0.02),
        'a_kn_g': 1.0 + nrm((L, HEAD_DIM), 0.02),
        'b_dw_w': nrm((L, B_CONV, B_CH), B_CONV ** -0.5),
        'b_dw_b': nrm((L, B_CH), 0.02),
        'b_ln_g': 1.0 + nrm((L, B_CH), 0.02),
        'b_ln_b': nrm((L, B_CH), 0.02),
        'b_pw_w': nrm((L, B_CH, B_CH), B_CH ** -0.5),
        'b_pw_b': nrm((L, B_CH), 0.02),
        'c_rpb': nrm((L, C_HEADS, 2 * NA_ROWS - 1, 2 * NA_COLS - 1), 0.1),
        'w_out': nrm((L, D, D), D ** -0.5),
        'ffn_w_up': nrm((L, D, 2 * D_FF), D ** -0.5),
        'ffn_dw_w': nrm((L, FFN_CONV, 2 * D_FF), FFN_CONV ** -0.5),
        'ffn_dw_b': nrm((L, 2 * D_FF), 0.02),
        'ffn_w_down': nrm((L, D_FF, D), D_FF ** -0.5),
        'final_g': 1.0 + nrm((D,), 0.02),
    }


def reference(x, c, ctx, c_ctx, ada_w, ada_b, norm1_g, norm2_g, w_in, a_qn_g, a_kn_g,
              b_dw_w, b_dw_b, b_ln_g, b_ln_b, b_pw_w, b_pw_b, c_rpb, w_out,
              ffn_w_up, ffn_dw_w, ffn_dw_b, ffn_w_down, final_g):
    S = x.shape[1]
    cos, sin = axial_rope_tables(S)
    silu_c = jax.nn.silu(c)
    silu_cc = jax.nn.silu(c_ctx)
    for l in range(DEPTH):
        ctx_needed = l < DEPTH - 1
        mod = (silu_c @ ada_w[l] + ada_b[l])[:, None, :]
        mod_c = silu_cc @ ada_w[l] + ada_b[l]
        sh1, sc1, g1, sh2, sc2, g2 = jnp.split(mod, N_MOD, axis=-1)
        sh1c, sc1c, g1c, sh2c, sc2c, g2c = jnp.split(mod_c, N_MOD, axis=-1)

        h = modulate(rms_norm(x, norm1_g[l]), sh1, sc1)
        hc = modulate(rms_norm(ctx, norm1_g[l]), sh1c, sc1c)
        qa, ka, va, bu, qn, kn, vn = jnp.split(h @ w_in[l], IN_SPLITS, axis=-1)
        qa_c, ka_c, va_c, bu_c, qn_c, kn_c, vn_c = jnp.split(hc @ w_in[l], IN_SPLITS, axis=-1)

        qa = apply_rope(rms_norm(heads(qa, A_HEADS), a_qn_g[l]), cos, sin)
        ka = apply_rope(rms_norm(heads(ka, A_KV_HEADS), a_kn_g[l]), cos, sin)
        va = heads(va, A_KV_HEADS)
        ka_c = rms_norm(heads(ka_c, A_KV_HEADS), a_kn_g[l])
        va_c = heads(va_c, A_KV_HEADS)
        o_a = gqa_blocked(qa, jnp.concatenate([ka_c, ka], axis=1), jnp.concatenate([va_c, va], axis=1))

        o_b = conformer_conv(bu, b_dw_w[l], b_dw_b[l], b_ln_g[l], b_ln_b[l], b_pw_w[l], b_pw_b[l])

        kn_c_h, vn_c_h = heads(kn_c, C_HEADS), heads(vn_c, C_HEADS)
        o_c = neighbourhood_attention(heads(qn, C_HEADS), heads(kn, C_HEADS), heads(vn, C_HEADS),
                                      kn_c_h, vn_c_h, c_rpb[l])

        x = x + g1 * (jnp.concatenate([o_a, o_b, o_c], axis=-1) @ w_out[l])

        h2 = modulate(rms_norm(x, norm2_g[l]), sh2, sc2)
        x = x + g2 * conv_ffn(h2, ffn_w_up[l], ffn_dw_w[l], ffn_dw_b[l], ffn_w_down[l])

        if ctx_needed:
            qa_c = rms_norm(heads(qa_c, A_HEADS), a_qn_g[l])
            oc_a = gqa_attend(qa_c, ka_c, va_c)
            oc_b = conformer_conv(bu_c, b_dw_w[l], b_dw_b[l], b_ln_g[l], b_ln_b[l], b_pw_w[l], b_pw_b[l])
            oc_c = gqa_attend(heads(qn_c, C_HEADS), kn_c_h, vn_c_h)
            ctx = ctx + g1c * (jnp.concatenate([oc_a, oc_b, oc_c], axis=-1) @ w_out[l])
            h2c = modulate(rms_norm(ctx, norm2_g[l]), sh2c, sc2c)
            ctx = ctx + g2c * conv_ffn(h2c, ffn_w_up[l], ffn_dw_w[l], ffn_dw_b[l], ffn_w_down[l])

    return rms_norm(x, final_g)
```

```python
import numpy as np
import ml_dtypes
from contextlib import ExitStack
import concourse.bass as bass
import concourse.mybir as mybir
from concourse.bass_utils import run_bass_kernel_spmd

F32 = mybir.dt.float32
BF16 = mybir.dt.bfloat16
ACT = mybir.ActivationFunctionType
ALU = mybir.AluOpType
NEG = -1e30
EPS = 1e-6
P = 128


class _Stop(Exception):
    pass


class Cfg:
    stop = None

    def __init__(s, D=4096, SEQ=4096, L=4):
        s.D, s.SEQ, s.L = D, SEQ, L
        s.B, s.CTX, s.GW = 4, 256, 64
        s.NT = D // P
        NH = D // P
        s.AH = 3 * NH // 8
        s.AKV = s.AH // 3
        s.CH = 3 * NH // 8
        s.BCH = D - (s.AH + s.CH) * P
        s.BT = s.BCH // P
        s.DFF = 11 * D // 8
        s.FT = s.DFF // P
        s.INW = s.AH * P + 2 * s.AKV * P + 2 * s.BCH + 3 * s.CH * P
        s.CTI = s.INW // P
        s.SL = SEQ // 2
        s.T = s.CTX + s.SL
        s.ROWS = SEQ // s.GW
        s.RL = s.SL // s.GW
        s.NBLK = s.SL // 512
        s.MC = 6 * D // 8
        s.o_qa = 0
        s.o_ka = s.o_qa + s.AH
        s.o_va = s.o_ka + s.AKV
        s.o_ba = s.o_va + s.AKV
        s.o_bg = s.o_ba + s.BT
        s.o_qn = s.o_bg + s.BT
        s.o_kn = s.o_qn + s.CH
        s.o_vn = s.o_kn + s.CH
        o = 0
        s.pv = {}
        for name, n in [("n1g", s.NT), ("n2g", s.NT), ("qg", 1), ("kg", 1), ("bdw", s.BT * 31), ("bdb", s.BT),
                        ("blg", s.BT), ("blb", s.BT), ("bpb", s.BT), ("fdw", 2 * s.FT * 3), ("fdb", 2 * s.FT)]:
            s.pv[name] = o
            o += n
        s.NP = o
        s.chunks = [(0, s.CTX)] + [(s.CTX + 512 * i, 512) for i in range(s.NBLK)]


class Sem:
    def __init__(s, h):
        s.h, s.count = h, 0


class Buf:
    def __init__(s, name):
        s.name, s.w, s.r = name, None, []


class Eng:
    def __init__(s, e, sems):
        s.e, s.sems, s.si, s.seen = e, sems, 0, {}
        s.sem = sems[0]

    def rotate(s):
        if s.sem.count >= 30000:
            s.si += 1
            s.sem = s.sems[s.si]


class K:
    def __init__(s, nc, stack, n_eng_sems=10, n_dma_sems=20):
        s.nc = nc
        mk = lambda nm: Sem(stack.enter_context(nc.semaphore(nm)))
        s.pe = Eng(nc.tensor, [mk(f"pe{i}") for i in range(5)])
        s.act = Eng(nc.scalar, [mk(f"ac{i}") for i in range(4)])
        s.dve = Eng(nc.vector, [mk(f"dv{i}") for i in range(6)])
        s.pool = Eng(nc.gpsimd, [mk(f"po{i}") for i in range(2)])
        s.sp = Eng(nc.sync, [mk("spx")])
        s.dsem = {id(s.sp): [mk(f"ds{i}") for i in range(16)],
                  id(s.pool): [mk(f"dp{i}") for i in range(28)]}
        s.dsi = {k: 0 for k in s.dsem}
        s.ccsem = [mk(f"cc{i}") for i in range(4)]
        s.cci = 0
        s.bufs = {}

    def B(s, *key):
        if key not in s.bufs:
            s.bufs[key] = Buf(str(key))
        return s.bufs[key]

    def _wait(s, eng, reads, writes, pe_acc=False):
        deps = {}

        def add(tok):
            if tok is None:
                return
            sem, val = tok
            if deps.get(id(sem), (None, 0))[1] < val:
                deps[id(sem)] = (sem, val)
        for b in reads:
            add(b.w)
        for b in writes:
            if not (pe_acc and b.w is not None and b.w[0] in eng.sems):
                add(b.w)
            for t in b.r:
                add(t)
        for sem, val in deps.values():
            if pe_acc and sem in eng.sems:
                continue
            if eng.seen.get(id(sem), 0) < val:
                eng.e.wait_ge(sem.h, val)
                eng.seen[id(sem)] = val

    def _done(s, tok, reads, writes):
        for b in reads:
            b.r.append(tok)
            if len(b.r) > 64:
                b.r = b.r[-64:] if False else b.r
        for b in writes:
            b.w, b.r = tok, []

    def op(s, eng, fn, reads=(), writes=(), inc=True, pe_acc=False):
        s._wait(eng, reads, writes, pe_acc)
        ins = fn()
        if inc:
            eng.sem.count += 1
            ins.then_inc(eng.sem.h, 1)
            tok = (eng.sem, eng.sem.count)
            s._done(tok, reads, writes)
            eng.rotate()
        else:
            tok = (eng.sem, eng.sem.count + 1)
            s._done(tok, reads, writes)
        return ins

    def dma(s, eng, out, in_, reads=(), writes=(), slow=False):
        s._wait(eng, reads, writes)
        pool = s.dsem[id(eng)]
        sem = pool[s.dsi[id(eng)] % len(pool)]
        s.dsi[id(eng)] += 1
        if sem.count and eng.seen.get(id(sem), 0) < sem.count:
            eng.e.wait_ge(sem.h, sem.count)
            eng.seen[id(sem)] = sem.count
        ins = eng.e.dma_start(out=out, in_=in_, allow_slow_non_contiguous=True) if slow else eng.e.dma_start(out=out, in_=in_)
        sem.count += 16
        ins.then_inc(sem.h, 16)
        s._done((sem, sem.count), reads, writes)

    def allgather(s, groups, in_ap, out_ap, reads=(), writes=()):
        eng = s.pool
        s._wait(eng, reads, writes)
        sem = s.ccsem[0]
        ins = eng.e.collective_compute("AllGather", ALU.bypass, replica_groups=groups,
                                       ins=[in_ap.opt()], outs=[out_ap.opt()])
        sem.count += 1
        ins.then_inc(sem.h)
        s._done((sem, sem.count), reads, writes)

    def wait_all(s, eng, bufs):
        s._wait(eng, bufs, ())

    def barrier(s):
        sems = []
        for e in (s.pe, s.act, s.dve, s.pool, s.sp):
            sems += e.sems
        for v in s.dsem.values():
            sems += v
        sems += s.ccsem
        for e in (s.pe, s.act, s.dve, s.pool, s.sp):
            for sem in sems:
                if sem.count and e.seen.get(id(sem), 0) < sem.count:
                    e.e.wait_ge(sem.h, sem.count)
                    e.seen[id(sem)] = sem.count


PAIRS = [[0, 1], [2, 3], [4, 5], [6, 7]]
ALL8 = [list(range(8))]
QUADS = [[0, 1, 2, 3], [4, 5, 6, 7]]
P4 = [[0, 4], [1, 5], [2, 6], [3, 7]]


def wblk(rows):
    b = min(128, rows)
    while rows % b:
        b -= 1
    return b


def build(cfg):
    try:
        return _build(cfg)
    except _Stop as e:
        return e.args[0]


def _build(cfg):
    c = cfg
    D, NT, T, SL, CTX, L = c.D, c.NT, c.T, c.SL, c.CTX, c.L
    nc = bass.Bass("TRN2", target_bir_lowering=False)
    din = lambda n, sh, dt=F32: nc.dram_tensor(n, sh, dt, kind="ExternalInput")
    dsc = lambda n, sh, dt=F32: nc.dram_tensor(n, sh, dt)
    xin = din("xin", [D, T])
    c5T = din("c5T", [P, NT * 5])
    ada_s = din("ada_s", [L * D, c.MC])
    adab5 = din("adab5", [5, L * c.MC])
    flags = din("flags", [P, 8])
    pvec = din("pvec", [L * P, c.NP])
    fing = din("fing", [P, NT])
    cosT = din("cosT", [P, T])
    sinT = din("sinT", [P, T])
    consts = din("consts", [P, 3 * P])
    nmask = din("nmask", [c.NBLK * P, 8 * 512], BF16)
    rpbT = din("rpbT", [L * c.CH * 64, 23 * 64])
    wspec = {"win": (D, c.INW), "wout": (D, D), "wup": (D, 2 * c.DFF), "wdn": (c.DFF, D), "wpw": (c.BCH, c.BCH)}
    wsh, wbf, wfull, wquad = {}, {}, {}, {}
    for n, (kk, nn) in wspec.items():
        rows = kk * nn // 8 // 1024
        wsh[n] = din(n + "_s", [L * rows, 1024])
        wbf[n] = [dsc(f"{n}_b{l}", [rows, 1024], BF16) for l in range(L)]
        wfull[n] = [dsc(f"{n}_f{l}", [8 * rows, 1024], BF16) for l in range(L)]
        wquad[n] = [dsc(f"{n}_q{l}", [4 * rows, 1024], BF16) for l in range(L)]
    yout = nc.dram_tensor("yout", [D, SL], F32, kind="ExternalOutput")
    xA = dsc("xA", [D, T]); xB_ = dsc("xB", [D, T])
    qaT = dsc("qaT", [c.AH * P, T], BF16)
    kaT = dsc("kaT", [c.AKV * P, T], BF16)
    va = dsc("va", [c.AKV * T, P], BF16)
    qnT = dsc("qnT", [c.CH * P, T], BF16)
    knT = dsc("knT", [c.CH * P, T], BF16)
    vn = dsc("vn", [c.CH * T, P], BF16)
    ZW = 15 + CTX + 15 + 15 + SL + 15
    zp = dsc("zp", [c.BCH, ZW])
    catT = dsc("catT", [D, T], BF16)
    kaS = dsc("kaS", [c.AKV * P, SL], BF16); kaG = dsc("kaG", [2 * c.AKV * P, SL], BF16)
    vaS = dsc("vaS", [c.AKV * SL, P], BF16); vaG = dsc("vaG", [2 * c.AKV * SL, P], BF16)
    knS = dsc("knS", [c.CH * P, 512], BF16); knG = dsc("knG", [2 * c.CH * P, 512], BF16)
    vnS = dsc("vnS", [c.CH * 512, P], BF16); vnG = dsc("vnG", [2 * c.CH * 512, P], BF16)
    zS = dsc("zS", [c.BCH, 32]); zG = dsc("zG", [2 * c.BCH, 32])
    xeS = dsc("xeS", [P, 2 * NT]); xeG = dsc("xeG", [2 * P, 2 * NT])
    modS = dsc("modS", [5, L * c.MC]); modG = dsc("modG", [8 * 5, L * c.MC]); modQ = dsc("modQ", [4 * 5, L * c.MC])

    with ExitStack() as st:
        k = K(nc, st)
        PE, AC, DV, PO, SP = k.pe, k.act, k.dve, k.pool, k.sp
        B = k.B

        uid = [0]

        class Scope:
            def __enter__(s):
                s.st = ExitStack()
                s.st.__enter__()
                uid[0] += 1
                u = uid[0]
                return lambda n, sh, dt=F32: s.st.enter_context(nc.sbuf_tensor(f"{n}_u{u}", sh, dt))

            def __exit__(s, *a):
                k.barrier()
                return s.st.__exit__(*a)

        sb = lambda n, sh, dt=F32: st.enter_context(nc.sbuf_tensor(n, sh, dt))
        psb = [st.enter_context(nc.psum_tensor(f"ps{i}", [P, 512], F32)) for i in range(7)]
        pst = st.enter_context(nc.psum_tensor("pst", [P, 4 * P], BF16))
        rot_i = [0]

        def ps_next():
            i = rot_i[0] % 4
            rot_i[0] += 1
            return psb[i], B("ps", i)
        PS_O, PS_DEN, PS_AUX = (psb[4], B("ps", 4)), (psb[5], B("ps", 5)), (psb[6], B("ps", 6))
        PST = (pst, B("pst"))
        ident_f = sb("ident_f", [P, P]); rot_f = sb("rot_f", [P, P]); ones_f = sb("ones_f", [P, P])
        ident_b = sb("ident_b", [P, P], BF16); ones_b = sb("ones_b", [P, P], BF16)
        flg = sb("flg", [P, 8]); fin_g = sb("fin_g", [P, NT])
        zero_f = sb("zero_f", [P, 16])
        msb = sb("msb", [P, L, 2, 6 * NT])
        pv = sb("pv", [P, L, c.NP])
        qgs = sb("qgs", [P, L])
        CB = B("consts")
        k.dma(SP, ident_f[:], consts[:, 0:P], writes=[CB])
        k.dma(SP, rot_f[:], consts[:, P:2 * P], writes=[CB])
        k.dma(SP, ones_f[:], consts[:, 2 * P:3 * P], writes=[CB])
        k.dma(PO, ident_b[:], consts[:, 0:P], writes=[CB])
        k.dma(PO, ones_b[:], consts[:, 2 * P:3 * P], writes=[CB])
        k.dma(SP, flg[:], flags[:, :], writes=[CB])
        k.dma(SP, fin_g[:], fing[:, :], writes=[CB])
        k.op(DV, lambda: nc.vector.memset(zero_f[:], 0.0), writes=[CB])

        def wpieces(l):
            for n, (kk, nn) in wspec.items():
                rows = kk * nn // 8 // 1024
                blk = wblk(rows)
                for pi in range(rows // blk):
                    yield n, rows, blk, pi
        for l in range(L):
            for n, rows, blk, pi in wpieces(l):
                r0 = pi * blk
                k.dma(PO, wbf[n][l][r0:r0 + blk, :], wsh[n][l * rows + r0:l * rows + r0 + blk, :], writes=[B("wbfp", n, l, pi)])
        for l in range(L):
            for n, rows, blk, pi in wpieces(l):
                r0, q0 = pi * blk, pi * 4 * blk
                k.allgather(QUADS, wbf[n][l][r0:r0 + blk, :], wquad[n][l][q0:q0 + 4 * blk, :],
                            reads=[B("wbfp", n, l, pi)], writes=[B("wq", n, l, pi)])
            for n, rows, blk, pi in wpieces(l):
                q0 = pi * 4 * blk
                k.allgather(P4, wquad[n][l][q0:q0 + 4 * blk, :], wfull[n][l][2 * q0:2 * q0 + 8 * blk, :],
                            reads=[B("wq", n, l, pi)], writes=[B("wf", n, l)])

        def ck(name):
            if c.stop == name:
                k.barrier()
                raise _Stop(nc)
        ck('weights')

        def wtile(n, l, ct, KT):
            v = wfull[n][l].ap().rearrange("r j -> (r j)").rearrange("(c p q) -> c p q", p=P, q=KT * P)
            return v[ct]

        ng = 1
        while c.MC % ng or c.MC // ng > 512:
            ng += 1
        gw = c.MC // ng
        assert ng <= 6
        NJ = c.MC // P
        NR = 8 * NJ
        with Scope() as ss:
            c5 = ss("c5", [P, NT * 5])
            adat = [ss(f"adat{i}", [P, c.MC]) for i in range(2)]
            modrow = ss("modrow", [5, c.MC]); adabs = ss("adabs", [5, c.MC])
            modT = ss("modT", [P, L, 5, NR])
            mrow = [ss(f"mrow{i}", [P, P]) for i in range(2)]
            k.dma(SP, c5[:], c5T[:, :], writes=[B("c5")])
            k.op(AC, lambda: nc.scalar.activation(out=c5[:], in_=c5[:], func=ACT.Silu), reads=[B("c5")], writes=[B("c5")])
            for l in range(L):
                k.dma(SP, adabs[:], adab5[:, l * c.MC:(l + 1) * c.MC], writes=[B("adabs")])
                for kt in range(NT):
                    at, ab = adat[kt % 2], B("adat", kt % 2)
                    k.dma(SP, at[:], ada_s[l * D + kt * P:l * D + (kt + 1) * P, :], writes=[ab])
                    for g in range(ng):
                        psg, pb = psb[g], B("ps", g)
                        k.op(PE, lambda psg=psg, at=at, g=g, kt=kt: nc.tensor.matmul(
                            psg[0:5, 0:gw], lhsT=c5[:, kt * 5:(kt + 1) * 5], rhs=at[:, g * gw:(g + 1) * gw],
                            start=(kt == 0), stop=(kt == NT - 1)),
                            reads=[ab, B("c5")], writes=[pb], pe_acc=(kt > 0))
                for g in range(ng):
                    psg, pb = psb[g], B("ps", g)
                    k.op(DV, lambda psg=psg, g=g: nc.vector.tensor_tensor(
                        out=modrow[:, g * gw:(g + 1) * gw], in0=psg[0:5, 0:gw], in1=adabs[:, g * gw:(g + 1) * gw], op=ALU.add),
                        reads=[pb, B("adabs")], writes=[B("modrow")])
                k.dma(PO, modS[:, l * c.MC:(l + 1) * c.MC], modrow[:], reads=[B("modrow")], writes=[B("modS")])
            k.allgather(QUADS, modS[:, :], modQ[:, :], reads=[B("modS")], writes=[B("modQ")])
            k.allgather(P4, modQ[:, :], modG[:, :], reads=[B("modQ")], writes=[B("modG")])
            cnt = 0
            mg = modG.ap().rearrange("(k r) (l j q) -> r l k j q", r=5, l=L, q=P)
            for l in range(L):
                for r in range(5):
                    for h0 in range(0, NR, P):
                        nrow = min(P, NR - h0)
                        mr, mb = mrow[cnt % 2], B("mrow", cnt % 2)
                        cnt += 1
                        for kk_ in range(8):
                            lo, hi = max(h0, kk_ * NJ), min(h0 + nrow, (kk_ + 1) * NJ)
                            if lo < hi:
                                k.dma(SP, mr[lo - h0:hi - h0, :], mg[r, l, kk_, lo - kk_ * NJ:hi - kk_ * NJ, :],
                                      reads=[B("modG")], writes=[mb])
                        k.op(PE, lambda mr=mr, nrow=nrow: nc.tensor.transpose(PS_AUX[0][:, 0:nrow], mr[0:nrow, :], ident_f[0:nrow, 0:nrow]),
                             reads=[mb, CB], writes=[PS_AUX[1]])
                        k.op(DV, lambda l=l, r=r, h0=h0, nrow=nrow: nc.vector.tensor_copy(out=modT[:, l, r, h0:h0 + nrow], in_=PS_AUX[0][:, 0:nrow]),
                             reads=[PS_AUX[1]], writes=[B("modT")])
            for l in range(L):
                k.dma(SP, pv[:, l, :], pvec[l * P:(l + 1) * P, :], writes=[B("pv")])
            for l in range(L):
                k.op(DV, lambda l=l: nc.vector.tensor_scalar(out=msb[:, l, 0, :], in0=modT[:, l, 0, :], scalar1=flg[:, 0:1], scalar2=None, op0=ALU.mult),
                     reads=[B("modT"), CB], writes=[B("msb")])
                for b in range(1, 4):
                    k.op(DV, lambda l=l, b=b: nc.vector.scalar_tensor_tensor(out=msb[:, l, 0, :], in0=modT[:, l, b, :], scalar=flg[:, b:b + 1],
                                                                            in1=msb[:, l, 0, :], op0=ALU.mult, op1=ALU.add),
                         reads=[B("modT"), B("msb"), CB], writes=[B("msb")])
                k.op(DV, lambda l=l: nc.vector.tensor_copy(out=msb[:, l, 1, :], in_=modT[:, l, 4, :]), reads=[B("modT"), B("msb")], writes=[B("msb")])
                for v in range(2):
                    for (mi, gname) in [(1, "n1g"), (4, "n2g")]:
                        k.op(DV, lambda l=l, v=v, mi=mi, gname=gname: nc.vector.scalar_tensor_tensor(
                            out=msb[:, l, v, mi * NT:(mi + 1) * NT], in0=msb[:, l, v, mi * NT:(mi + 1) * NT], scalar=1.0,
                            in1=pv[:, l, c.pv[gname]:c.pv[gname] + NT], op0=ALU.add, op1=ALU.mult),
                            reads=[B("msb"), B("pv")], writes=[B("msb")])
                k.op(DV, lambda l=l: nc.vector.tensor_scalar(out=qgs[:, l:l + 1], in0=pv[:, l, c.pv["qg"]:c.pv["qg"] + 1], scalar1=float(P) ** -0.5,
                                                            scalar2=None, op0=ALU.mult), reads=[B("pv")], writes=[B("qgs")])
        ck('mods')
        MS = B("msb")

        def mod(l, v, mi, t):
            return msb[:, l, v, mi * NT + t:mi * NT + t + 1]

        nch = len(c.chunks)
        lat = list(range(1, nch))
        XT = lambda nm, ci, t: B(nm, ci, t)
        XC = lambda nm, ci: [B(nm, ci, t) for t in range(NT)]
        for ci, (t0, n) in enumerate(c.chunks):
            k.dma(PO, xA[:, t0:t0 + n], xin[:, t0:t0 + n], writes=XC("xA", ci))
        for j in range(c.BT):
            k.dma(PO, zp[j * P:(j + 1) * P, 0:15], zero_f[:, 0:15], reads=[CB], writes=[B("zp", j, "pad")])
            k.dma(PO, zp[j * P:(j + 1) * P, 15 + CTX:15 + CTX + 15], zero_f[:, 0:15], reads=[CB], writes=[B("zp", j, "pad")])

        def xpiece(xt_, nm, ci, a, b):
            return (lambda gi, G: xt_[:, a:b].rearrange("(t p) n -> p t n", p=P)[:, gi:gi + G, :],
                    lambda gi, G: [B(nm, ci, t) for t in range(gi, gi + G)], b - a)

        def norm_mod(ss, pieces, W, l, v, mA, mS, hdst, wk, ydst=None):
            xt, sq, rstd, tmp = wk
            G = 2
            halves = [(0, W)] if W <= 512 else [(0, W // 2), (W // 2, W)]

            def load(gi):
                xb_, xbb = xt[(gi // G) % 2], B("xt", (gi // G) % 2)
                col = 0
                for (apf, bf, w) in pieces:
                    k.dma(SP, xb_[:, :, col:col + w], apf(gi, G), reads=bf(gi, G), writes=[xbb], slow=(w == 1))
                    col += w
                return xb_, xbb
            for gi in range(0, NT, G):
                xb_, xbb = load(gi)
                for tt in range(G):
                    t = gi + tt
                    s_, sbb = sq[t % 2], B("sq", t % 2)
                    k.op(AC, lambda xb_=xb_, tt=tt, s_=s_: nc.scalar.activation(out=s_[:, 0:W], in_=xb_[:, tt, 0:W], func=ACT.Square),
                         reads=[xbb], writes=[sbb])
                    for hi, (a, b) in enumerate(halves):
                        pp = [PS_AUX, PS_DEN][hi]
                        k.op(PE, lambda pp=pp, s_=s_, a=a, b=b, t=t: nc.tensor.matmul(pp[0][:, 0:b - a], lhsT=ones_b[:], rhs=s_[:, a:b],
                                                                                  start=(t == 0), stop=(t == NT - 1)),
                             reads=[sbb, CB], writes=[pp[1]], pe_acc=(t > 0))
            for hi, (a, b) in enumerate(halves):
                pp = [PS_AUX, PS_DEN][hi]
                k.op(DV, lambda pp=pp, a=a, b=b: nc.vector.tensor_scalar(out=rstd[:, a:b], in0=pp[0][:, 0:b - a], scalar1=1.0 / D, scalar2=EPS,
                                                                       op0=ALU.mult, op1=ALU.add), reads=[pp[1], B("rstd")], writes=[B("rstd")])
            k.op(AC, lambda: nc.scalar.sqrt(out=rstd[:, 0:W], in_=rstd[:, 0:W]), reads=[B("rstd")], writes=[B("rstd")])
            k.op(DV, lambda: nc.vector.reciprocal(out=rstd[:, 0:W], in_=rstd[:, 0:W]), reads=[B("rstd")], writes=[B("rstd")])
            for gi in range(0, NT, G):
                xb_, xbb = load(gi)
                for tt in range(G):
                    t = gi + tt
                    tm, tb = tmp()
                    k.op(DV, lambda xb_=xb_, tt=tt, tm=tm: nc.vector.tensor_tensor(out=tm[:, 0:W], in0=xb_[:, tt, 0:W], in1=rstd[:, 0:W], op=ALU.mult),
                         reads=[xbb, B("rstd")], writes=[tb])
                    if ydst is None:
                        k.op(AC, lambda tm=tm, t=t: nc.scalar.activation(out=hdst[:, t, 0:W], in_=tm[:, 0:W], func=ACT.Identity,
                                                                       bias=mod(l, v, mS, t), scale=mod(l, v, mA, t)),
                             reads=[tb, MS], writes=[B("hsb")])
                    else:
                        ydst(t, tm, tb)

        def mk_norm_bufs(ss):
            xt = [ss(f"xt{i}", [P, 2, 514]) for i in range(2)]
            sq = [ss(f"sq{i}", [P, 514], BF16) for i in range(2)]
            rstd = ss("rstd", [P, 514])
            tmpf = [ss(f"tmpf{i}", [P, 514]) for i in range(3)]
            tc_ = [0]

            def tmp():
                i = tc_[0] % 3
                tc_[0] += 1
                return tmpf[i], B("tmpf", i)
            return (xt, sq, rstd, tmp)

        def mk_w(ss, KT, nb=4):
            wb_ = [ss(f"wb{i}", [P, KT * P], BF16) for i in range(nb)]
            wr = [0]

            def wload(n, l, ct, KT_):
                i = wr[0] % nb
                wr[0] += 1
                k.dma(SP, wb_[i][:, 0:KT_ * P], wtile(n, l, ct, KT_), reads=[B("wf", n, l)], writes=[B("wb", i)])
                return wb_[i], B("wb", i)
            return wload

        def mk_ob(ss):
            ob = [ss(f"ob{i}", [P, 512], BF16) for i in range(3)]
            oc = [0]

            def obuf():
                i = oc[0] % 3
                oc[0] += 1
                return ob[i], B("ob", i)
            return obuf

        cur, nxt = (xA, "xA"), (xB_, "xB")
        for l in range(L):
            last = (l == L - 1)
            X, XN = cur
            pvo = lambda name, j=0: pv[:, l, c.pv[name] + j:c.pv[name] + j + 1]
            with Scope() as ss:
                wk = mk_norm_bufs(ss)
                tmp = wk[3]
                hsb = ss("hsb", [P, NT, 514], BF16)
                wload = mk_w(ss, NT)
                obuf = mk_ob(ss)
                vt = [ss(f"vt{i}", [P, 4 * P], BF16) for i in range(2)]
                sg = ss("sg", [P, c.BT, 512])
                cos_sb = ss("cos_sb", [P, T]); sin_sb = ss("sin_sb", [P, T])
                k.dma(SP, cos_sb[:], cosT[:, :], writes=[B("rope")])
                k.dma(SP, sin_sb[:], sinT[:, :], writes=[B("rope")])
                RB = B("rope")
                xcnt = 0
                for ci, (t0, n) in enumerate(c.chunks):
                    v = 1 if ci == 0 else 0
                    norm_mod(ss, [xpiece(X, XN, ci, t0, t0 + n)], n, l, v, 1, 0, hsb, wk)
                    HB = B("hsb")
                    if l == 0 and ci == 0:
                        ck('n1')
                    order = list(range(c.o_qa, c.o_ba)) + [x_ for j in range(c.BT) for x_ in (c.o_bg + j, c.o_ba + j)] + list(range(c.o_qn, c.CTI))
                    for ct in order:
                        if ci == 0 and last and (ct < c.o_ka or c.o_ba <= ct < c.o_kn):
                            continue
                        w_, wbb = wload("win", l, ct, NT)
                        ck('w0')
                        ps_, pb = ps_next()
                        for kt in range(NT):
                            k.op(PE, lambda ps_=ps_, w_=w_, kt=kt: nc.tensor.matmul(ps_[:, 0:n], lhsT=w_[:, kt * P:(kt + 1) * P], rhs=hsb[:, kt, 0:n],
                                                                                  start=(kt == 0), stop=(kt == NT - 1)),
                                 reads=[wbb, HB], writes=[pb], pe_acc=(kt > 0), inc=(kt == NT - 1))
                        ck('mm0')
                        if ct < c.o_va:
                            isq = ct < c.o_ka
                            qf, qfb = tmp()
                            k.op(DV, lambda qf=qf, ps_=ps_: nc.vector.tensor_copy(out=qf[:, 0:n], in_=ps_[:, 0:n]), reads=[pb], writes=[qfb])
                            ck('h0')
                            s2, s2b = tmp()
                            k.op(DV, lambda s2=s2, qf=qf: nc.vector.tensor_tensor(out=s2[:, 0:n], in0=qf[:, 0:n], in1=qf[:, 0:n], op=ALU.mult), reads=[qfb], writes=[s2b])
                            ck('h1')
                            k.op(PE, lambda s2=s2: nc.tensor.matmul(PS_AUX[0][:, 0:n], lhsT=ones_f[:], rhs=s2[:, 0:n], start=True, stop=True),
                                 reads=[s2b, CB], writes=[PS_AUX[1]])
                            ck('h2')
                            k.op(DV, lambda s2=s2: nc.vector.tensor_scalar(out=s2[:, 0:n], in0=PS_AUX[0][:, 0:n], scalar1=1.0 / P, scalar2=EPS, op0=ALU.mult, op1=ALU.add),
                                 reads=[PS_AUX[1], s2b], writes=[s2b])
                            ck('h3')
                            k.op(AC, lambda s2=s2: nc.scalar.sqrt(out=s2[:, 0:n], in_=s2[:, 0:n]), reads=[s2b], writes=[s2b])
                            k.op(DV, lambda s2=s2: nc.vector.reciprocal(out=s2[:, 0:n], in_=s2[:, 0:n]), reads=[s2b], writes=[s2b])
                            ck('hn')
                            gain = qgs[:, l:l + 1] if isq else pvo("kg")
                            k.op(DV, lambda qf=qf, s2=s2, gain=gain: nc.vector.scalar_tensor_tensor(out=qf[:, 0:n], in0=qf[:, 0:n], scalar=gain, in1=s2[:, 0:n],
                                                                                                 op0=ALU.mult, op1=ALU.mult),
                                 reads=[qfb, s2b, B("qgs"), B("pv")], writes=[qfb])
                            k.op(PE, lambda qf=qf: nc.tensor.matmul(PS_AUX[0][:, 0:n], lhsT=rot_f[:], rhs=qf[:, 0:n], start=True, stop=True),
                                 reads=[qfb, CB], writes=[PS_AUX[1]])
                            k.op(DV, lambda s2=s2: nc.vector.tensor_tensor(out=s2[:, 0:n], in0=PS_AUX[0][:, 0:n], in1=sin_sb[:, t0:t0 + n], op=ALU.mult),
                                 reads=[PS_AUX[1], RB, s2b], writes=[s2b])
                            k.op(DV, lambda qf=qf: nc.vector.tensor_tensor(out=qf[:, 0:n], in0=qf[:, 0:n], in1=cos_sb[:, t0:t0 + n], op=ALU.mult),
                                 reads=[qfb, RB], writes=[qfb])
                            o_, obb = obuf()
                            k.op(DV, lambda o_=o_, qf=qf, s2=s2: nc.vector.tensor_tensor(out=o_[:, 0:n], in0=qf[:, 0:n], in1=s2[:, 0:n], op=ALU.add),
                                 reads=[qfb, s2b], writes=[obb])
                            ck('rope')
                            if isq:
                                k.dma(PO, qaT[ct * P:(ct + 1) * P, t0:t0 + n], o_[:, 0:n], reads=[obb], writes=[B("qa", ct, ci)])
                            else:
                                h_ = ct - c.o_ka
                                k.dma(PO, kaT[h_ * P:(h_ + 1) * P, t0:t0 + n], o_[:, 0:n], reads=[obb], writes=[B("ka", h_, ci)])
                                if ci > 0:
                                    k.dma(PO, kaS[h_ * P:(h_ + 1) * P, t0 - CTX:t0 - CTX + n], o_[:, 0:n], reads=[obb], writes=[B("kaS")])
                        elif ct < c.o_ba or ct >= c.o_vn:
                            isa = ct < c.o_ba
                            h_ = ct - (c.o_va if isa else c.o_vn)
                            o_, obb = obuf()
                            k.op(AC, lambda o_=o_, ps_=ps_: nc.scalar.copy(out=o_[:, 0:n], in_=ps_[:, 0:n]), reads=[pb], writes=[obb])
                            nb = n // P
                            for bi in range(nb):
                                k.op(PE, lambda o_=o_, bi=bi: nc.tensor.transpose(PST[0][:, bi * P:(bi + 1) * P], o_[:, bi * P:(bi + 1) * P], ident_b[:]),
                                     reads=[obb, CB], writes=[PST[1]], pe_acc=(bi > 0), inc=(bi == nb - 1))
                            vt_, vtb = vt[xcnt % 2], B("vt", xcnt % 2)
                            xcnt += 1
                            k.op(DV, lambda vt_=vt_: nc.vector.tensor_copy(out=vt_[:, 0:n], in_=PST[0][:, 0:n]), reads=[PST[1]], writes=[vtb])
                            dstT = va if isa else vn
                            v3 = lambda ap: ap.rearrange("(b p) d -> p b d", p=P)
                            s3 = lambda lo, hi: vt_[:, lo:hi].rearrange("p (b d) -> p b d", d=P)
                            k.dma(PO, v3(dstT[h_ * T + t0:h_ * T + t0 + n, :]), s3(0, n), reads=[vtb], writes=[B("va" if isa else "vn", h_, ci)])
                            if isa and ci > 0:
                                k.dma(PO, v3(vaS[h_ * SL + t0 - CTX:h_ * SL + t0 - CTX + n, :]), s3(0, n), reads=[vtb], writes=[B("vaS")])
                            if (not isa) and ci == 1:
                                k.dma(PO, v3(vnS[h_ * 512:h_ * 512 + 256, :]), s3(0, 256), reads=[vtb], writes=[B("vnS")])
                            if (not isa) and ci == c.NBLK:
                                k.dma(PO, v3(vnS[h_ * 512 + 256:h_ * 512 + 512, :]), s3(n - 256, n), reads=[vtb], writes=[B("vnS")])
                        elif ct < c.o_qn:
                            if ct >= c.o_bg:
                                j = ct - c.o_bg
                                k.op(AC, lambda ps_=ps_, j=j: nc.scalar.activation(out=sg[:, j, 0:n], in_=ps_[:, 0:n], func=ACT.Sigmoid), reads=[pb], writes=[B("sg", j)])
                            else:
                                j = ct - c.o_ba
                                z_, zb = tmp()
                                k.op(DV, lambda z_=z_, ps_=ps_, j=j: nc.vector.tensor_tensor(out=z_[:, 0:n], in0=ps_[:, 0:n], in1=sg[:, j, 0:n], op=ALU.mult),
                                     reads=[pb, B("sg", j)], writes=[zb])
                                zo = 15 if ci == 0 else (15 + CTX + 15 + 15 + t0 - CTX)
                                k.dma(PO, zp[j * P:(j + 1) * P, zo:zo + n], z_[:, 0:n], reads=[zb], writes=[B("zp", j, ci)])
                                if ci == 1:
                                    k.dma(PO, zS[j * P:(j + 1) * P, 0:16], z_[:, 0:16], reads=[zb], writes=[B("zS")])
                                if ci == c.NBLK:
                                    k.dma(PO, zS[j * P:(j + 1) * P, 16:32], z_[:, n - 16:n], reads=[zb], writes=[B("zS")])
                        else:
                            isq = ct < c.o_kn
                            h_ = ct - (c.o_qn if isq else c.o_kn)
                            o_, obb = obuf()
                            if isq:
                                k.op(AC, lambda o_=o_, ps_=ps_: nc.scalar.mul(out=o_[:, 0:n], in_=ps_[:, 0:n], mul=float(P) ** -0.5), reads=[pb], writes=[obb])
                                k.dma(PO, qnT[h_ * P:(h_ + 1) * P, t0:t0 + n], o_[:, 0:n], reads=[obb], writes=[B("qn", h_, ci)])
                            else:
                                k.op(AC, lambda o_=o_, ps_=ps_: nc.scalar.copy(out=o_[:, 0:n], in_=ps_[:, 0:n]), reads=[pb], writes=[obb])
                                k.dma(PO, knT[h_ * P:(h_ + 1) * P, t0:t0 + n], o_[:, 0:n], reads=[obb], writes=[B("kn", h_, ci)])
                                if ci == 1:
                                    k.dma(PO, knS[h_ * P:(h_ + 1) * P, 0:256], o_[:, 0:256], reads=[obb], writes=[B("knS")])
                                if ci == c.NBLK:
                                    k.dma(PO, knS[h_ * P:(h_ + 1) * P, 256:512], o_[:, n - 256:n], reads=[obb], writes=[B("knS")])
                        if l == 0 and ci == 0:
                            ck(f'ct{ct}')
            ck('AB')
            for h_ in range(c.AKV):
                k.allgather(PAIRS, kaS[h_ * P:(h_ + 1) * P, :], kaG[2 * h_ * P:2 * (h_ + 1) * P, :], reads=[B("kaS")], writes=[B("kaG")])
                k.allgather(PAIRS, vaS[h_ * SL:(h_ + 1) * SL, :], vaG[2 * h_ * SL:2 * (h_ + 1) * SL, :], reads=[B("vaS")], writes=[B("vaG")])
            for h_ in range(c.CH):
                k.allgather(PAIRS, knS[h_ * P:(h_ + 1) * P, :], knG[2 * h_ * P:2 * (h_ + 1) * P, :], reads=[B("knS")], writes=[B("knG")])
                k.allgather(PAIRS, vnS[h_ * 512:(h_ + 1) * 512, :], vnG[2 * h_ * 512:2 * (h_ + 1) * 512, :], reads=[B("vnS")], writes=[B("vnG")])
            k.allgather(PAIRS, zS[:, :], zG[:, :], reads=[B("zS")], writes=[B("zG")])
            zl = 15 + CTX + 15
            with Scope() as ss:
                halo_sb = ss("halo_sb", [P, 2, 16])
                for j in range(c.BT):
                    k.dma(SP, halo_sb[:, 0, :], zG[j * P:(j + 1) * P, 16:32], reads=[B("zG")], writes=[B("halo")])
                    k.dma(SP, halo_sb[:, 1, :], zG[c.BCH + j * P:c.BCH + (j + 1) * P, 0:16], reads=[B("zG")], writes=[B("halo")])
                    k.op(DV, lambda: nc.vector.tensor_scalar(out=halo_sb[:, 0, :], in0=halo_sb[:, 0, :], scalar1=flg[:, 4:5], scalar2=None, op0=ALU.mult),
                         reads=[B("halo"), CB], writes=[B("halo")])
                    k.op(DV, lambda: nc.vector.tensor_scalar(out=halo_sb[:, 1, :], in0=halo_sb[:, 1, :], scalar1=flg[:, 5:6], scalar2=None, op0=ALU.mult),
                         reads=[B("halo"), CB], writes=[B("halo")])
                    k.dma(PO, zp[j * P:(j + 1) * P, zl:zl + 15], halo_sb[:, 0, 1:16], reads=[B("halo")], writes=[B("zp", j, "h0")])
                    k.dma(PO, zp[j * P:(j + 1) * P, zl + 15 + SL:zl + 30 + SL], halo_sb[:, 1, 0:15], reads=[B("halo")], writes=[B("zp", j, "h1")])
            ck('C')
            KMAX = CTX + c.SEQ
            with Scope() as ss:
                kT_sb = ss("kT_sb", [P, KMAX], BF16)
                v_sb = ss("v_sb", [P, KMAX // P, P], BF16)
                q_sb = [ss(f"q_sb{i}", [P, 512], BF16) for i in range(2)]
                p_sb = [ss(f"p_sb{i}", [P, 512], BF16) for i in range(3)]
                bias_sb = ss("bias_sb", [P, 8, 512], BF16)
                mask_sb = ss("mask_sb", [P, c.NBLK, 8 * 512], BF16)
                rden = ss("rden", [P, 512])
                obuf = mk_ob(ss)
                for bi in range(c.NBLK):
                    k.dma(SP, mask_sb[:, bi, :], nmask[bi * P:(bi + 1) * P, :], writes=[B("mask_sb")])
                cn = [0, 0]

                def attend(qsrc, qbufs, nq, ktiles, dst_ap, dst_bufs, kb):
                    qs, qb = q_sb[cn[0] % 2], B("q_sb", cn[0] % 2)
                    cn[0] += 1
                    k.dma(SP, qs[:, 0:nq], qsrc, reads=qbufs, writes=[qb])
                    nk = len(ktiles)
                    for i, (ko, vi, extra) in enumerate(ktiles):
                        ps_, pb = ps_next()
                        k.op(PE, lambda ps_=ps_, ko=ko: nc.tensor.matmul(ps_[:, 0:nq], lhsT=kT_sb[:, ko:ko + P], rhs=qs[:, 0:nq], start=True, stop=(extra is None)),
                             reads=[kb, qb], writes=[pb], inc=(extra is None))
                        if extra is not None:
                            bj, mblk, mj = extra
                            k.op(PE, lambda ps_=ps_, bj=bj: nc.tensor.matmul(ps_[:, 0:nq], lhsT=ident_b[:], rhs=bias_sb[:, bj, 0:nq], start=False, stop=False),
                                 reads=[B("bias_sb"), CB], writes=[pb], pe_acc=True, inc=False)
                            k.op(PE, lambda ps_=ps_, mblk=mblk, mj=mj: nc.tensor.matmul(ps_[:, 0:nq], lhsT=ident_b[:], rhs=mask_sb[:, mblk, mj * 512:mj * 512 + nq],
                                                                                      start=False, stop=True),
                                 reads=[B("mask_sb"), CB], writes=[pb], pe_acc=True)
                        pt, ptb = p_sb[cn[1] % 3], B("p_sb", cn[1] % 3)
                        cn[1] += 1
                        k.op(AC, lambda ps_=ps_, pt=pt: nc.scalar.activation(out=pt[:, 0:nq], in_=ps_[:, 0:nq], func=ACT.Exp), reads=[pb], writes=[ptb])
                        k.op(PE, lambda pt=pt, vi=vi, i=i: nc.tensor.matmul(PS_O[0][:, 0:nq], lhsT=v_sb[:, vi, :], rhs=pt[:, 0:nq], start=(i == 0), stop=(i == nk - 1)),
                             reads=[ptb, kb], writes=[PS_O[1]], pe_acc=(i > 0), inc=False)
                        k.op(PE, lambda pt=pt, i=i: nc.tensor.matmul(PS_DEN[0][:, 0:nq], lhsT=ones_b[:], rhs=pt[:, 0:nq], start=(i == 0), stop=(i == nk - 1)),
                             reads=[ptb, CB], writes=[PS_DEN[1]], pe_acc=(i > 0))
                    k.op(DV, lambda: nc.vector.reciprocal(out=rden[:, 0:nq], in_=PS_DEN[0][:, 0:nq]), reads=[PS_DEN[1]], writes=[B("rden")])
                    o_, obb = obuf()
                    k.op(DV, lambda o_=o_: nc.vector.tensor_tensor(out=o_[:, 0:nq], in0=PS_O[0][:, 0:nq], in1=rden[:, 0:nq], op=ALU.mult),
                         reads=[PS_O[1], B("rden")], writes=[obb])
                    k.dma(PO, dst_ap, o_[:, 0:nq], reads=[obb], writes=dst_bufs)

                KB = B("kv_sb")
                v3 = lambda ap: ap.rearrange("(b p) d -> p b d", p=P)
                for kv in range(c.AKV):
                    k.dma(SP, kT_sb[:, 0:CTX], kaT[kv * P:(kv + 1) * P, 0:CTX], reads=[B("ka", kv, 0)], writes=[KB])
                    k.dma(SP, kT_sb[:, CTX:CTX + SL], kaG[(2 * kv) * P:(2 * kv + 1) * P, :], reads=[B("kaG")], writes=[KB])
                    k.dma(SP, kT_sb[:, CTX + SL:CTX + 2 * SL], kaG[(2 * kv + 1) * P:(2 * kv + 2) * P, :], reads=[B("kaG")], writes=[KB])
                    k.dma(SP, v_sb[:, 0:CTX // P, :], v3(va[kv * T:kv * T + CTX, :]), reads=[B("va", kv, 0)], writes=[KB])
                    for hf in range(2):
                        base = (2 * kv + hf) * SL
                        k.dma(SP, v_sb[:, (CTX + hf * SL) // P:(CTX + (hf + 1) * SL) // P, :], v3(vaG[base:base + SL, :]), reads=[B("vaG")], writes=[KB])
                    for g in range(3):
                        h_ = kv * 3 + g
                        for ci in lat:
                            t0, n = c.chunks[ci]
                            kts = [(i * P, i, None) for i in range(KMAX // P)]
                            attend(qaT[h_ * P:(h_ + 1) * P, t0:t0 + n], [B("qa", h_, ci)], n, kts,
                                   catT[h_ * P:(h_ + 1) * P, t0:t0 + n], [B("cat", h_, ci)], KB)
                        if not last:
                            kts = [(i * P, i, None) for i in range(CTX // P)]
                            attend(qaT[h_ * P:(h_ + 1) * P, 0:CTX], [B("qa", h_, 0)], CTX, kts,
                                   catT[h_ * P:(h_ + 1) * P, 0:CTX], [B("cat", h_, 0)], KB)
                for h_ in range(c.CH):
                    k.dma(SP, kT_sb[:, 0:CTX], knT[h_ * P:(h_ + 1) * P, 0:CTX], reads=[B("kn", h_, 0)], writes=[KB])
                    k.dma(SP, kT_sb[:, CTX:CTX + 256], knG[(2 * h_) * P:(2 * h_ + 1) * P, 256:512], reads=[B("knG")], writes=[KB])
                    k.dma(SP, kT_sb[:, CTX + 256:CTX + 256 + SL], knT[h_ * P:(h_ + 1) * P, CTX:T], reads=[B("kn", h_, ci) for ci in lat], writes=[KB])
                    k.dma(SP, kT_sb[:, CTX + 256 + SL:CTX + 512 + SL], knG[(2 * h_ + 1) * P:(2 * h_ + 2) * P, 0:256], reads=[B("knG")], writes=[KB])
                    k.dma(SP, v_sb[:, 0:2, :], v3(vn[h_ * T:h_ * T + CTX, :]), reads=[B("vn", h_, 0)], writes=[KB])
                    k.dma(SP, v_sb[:, 2:4, :], v3(vnG[(2 * h_) * 512 + 256:(2 * h_) * 512 + 512, :]), reads=[B("vnG")], writes=[KB])
                    k.dma(SP, v_sb[:, 4:4 + SL // P, :], v3(vn[h_ * T + CTX:h_ * T + T, :]), reads=[B("vn", h_, ci) for ci in lat], writes=[KB])
                    k.dma(SP, v_sb[:, 4 + SL // P:6 + SL // P, :], v3(vnG[(2 * h_ + 1) * 512:(2 * h_ + 1) * 512 + 256, :]), reads=[B("vnG")], writes=[KB])
                    tab = rpbT[(l * c.CH + h_) * 64:(l * c.CH + h_ + 1) * 64, :]
                    for j in range(8):
                        for b in range(2):
                            d0 = 2 * j + b
                            k.dma(PO, bias_sb[b * 64:(b + 1) * 64, j, :], tab[:, (15 - d0) * 64:(15 - d0 + 8) * 64], writes=[B("bias_sb")])
                    for bi, ci in enumerate(lat):
                        t0, n = c.chunks[ci]
                        kts = [(i * P, i, None) for i in range(2)]
                        for j in range(8):
                            eo = (8 * bi + 2 * j) * 64
                            kts.append((CTX + eo, 2 + eo // P, (j, bi, j)))
                        cr = c.AH + c.BT + h_
                        attend(qnT[h_ * P:(h_ + 1) * P, t0:t0 + n], [B("qn", h_, ci)], n, kts,
                               catT[cr * P:(cr + 1) * P, t0:t0 + n], [B("cat", cr, ci)], KB)
                    if not last:
                        kts = [(i * P, i, None) for i in range(2)]
                        cr = c.AH + c.BT + h_
                        attend(qnT[h_ * P:(h_ + 1) * P, 0:CTX], [B("qn", h_, 0)], CTX, kts,
                               catT[cr * P:(cr + 1) * P, 0:CTX], [B("cat", cr, 0)], KB)
            ck('D12')
            with Scope() as ss:
                zwin = [ss(f"zwin{i}", [P, 512 + 30]) for i in range(2)]
                yconv = ss("yconv", [P, c.BT, 512])
                ysq = ss("ysq", [P, 512]); mean = ss("mean", [P, 512]); var = ss("var", [P, 512])
                zz = ss("zz", [P, c.BT, 512], BF16)
                wload = mk_w(ss, c.BT)
                obuf = mk_ob(ss)
                for ci, (t0, n) in enumerate(c.chunks):
                    if ci == 0 and last:
                        continue
                    zo = 0 if ci == 0 else (15 + CTX + 15 + t0 - CTX)
                    for j in range(c.BT):
                        zi = (ci * c.BT + j) % 2
                        zw, zwb = zwin[zi], B("zwin", zi)
                        deps = [B("zp", j, "pad"), B("zp", j, "h0"), B("zp", j, "h1")] + [B("zp", j, cj) for cj in range(nch)]
                        k.dma(SP, zw[:, 0:n + 30], zp[j * P:(j + 1) * P, zo:zo + n + 30], reads=deps, writes=[zwb])
                        YB = B("yconv", j)
                        k.op(DV, lambda zw=zw, j=j: nc.vector.tensor_scalar(out=yconv[:, j, 0:n], in0=zw[:, 0:n], scalar1=pvo("bdw", j * 31), scalar2=pvo("bdb", j),
                                                                          op0=ALU.mult, op1=ALU.add), reads=[zwb, B("pv")], writes=[YB])
                        for tap in range(1, 31):
                            k.op(DV, lambda zw=zw, j=j, tap=tap: nc.vector.scalar_tensor_tensor(out=yconv[:, j, 0:n], in0=zw[:, tap:tap + n], scalar=pvo("bdw", j * 31 + tap),
                                                                                             in1=yconv[:, j, 0:n], op0=ALU.mult, op1=ALU.add),
                                 reads=[zwb, B("pv"), YB], writes=[YB])
                        k.op(PE, lambda j=j: nc.tensor.matmul(PS_AUX[0][:, 0:n], lhsT=ones_f[:], rhs=yconv[:, j, 0:n], start=(j == 0), stop=(j == c.BT - 1)),
                             reads=[YB, CB], writes=[PS_AUX[1]], pe_acc=(j > 0))
                        k.op(DV, lambda j=j: nc.vector.tensor_tensor(out=ysq[:, 0:n], in0=yconv[:, j, 0:n], in1=yconv[:, j, 0:n], op=ALU.mult), reads=[YB, B("ysq")], writes=[B("ysq")])
                        k.op(PE, lambda j=j: nc.tensor.matmul(PS_DEN[0][:, 0:n], lhsT=ones_f[:], rhs=ysq[:, 0:n], start=(j == 0), stop=(j == c.BT - 1)),
                             reads=[B("ysq"), CB], writes=[PS_DEN[1]], pe_acc=(j > 0))
                    k.op(DV, lambda: nc.vector.tensor_scalar(out=mean[:, 0:n], in0=PS_AUX[0][:, 0:n], scalar1=1.0 / c.BCH, scalar2=None, op0=ALU.mult),
                         reads=[PS_AUX[1]], writes=[B("mean")])
                    k.op(DV, lambda: nc.vector.tensor_scalar(out=var[:, 0:n], in0=PS_DEN[0][:, 0:n], scalar1=1.0 / c.BCH, scalar2=EPS, op0=ALU.mult, op1=ALU.add),
                         reads=[PS_DEN[1]], writes=[B("var")])
                    k.op(DV, lambda: nc.vector.tensor_tensor(out=ysq[:, 0:n], in0=mean[:, 0:n], in1=mean[:, 0:n], op=ALU.mult), reads=[B("mean"), B("ysq")], writes=[B("ysq")])
                    k.op(DV, lambda: nc.vector.tensor_tensor(out=var[:, 0:n], in0=var[:, 0:n], in1=ysq[:, 0:n], op=ALU.subtract), reads=[B("var"), B("ysq")], writes=[B("var")])
                    k.op(AC, lambda: nc.scalar.sqrt(out=var[:, 0:n], in_=var[:, 0:n]), reads=[B("var")], writes=[B("var")])
                    k.op(DV, lambda: nc.vector.reciprocal(out=var[:, 0:n], in_=var[:, 0:n]), reads=[B("var")], writes=[B("var")])
                    for j in range(c.BT):
                        YB = B("yconv", j)
                        k.op(DV, lambda j=j: nc.vector.tensor_tensor(out=yconv[:, j, 0:n], in0=yconv[:, j, 0:n], in1=mean[:, 0:n], op=ALU.subtract),
                             reads=[YB, B("mean")], writes=[YB])
                        k.op(DV, lambda j=j: nc.vector.tensor_tensor(out=yconv[:, j, 0:n], in0=yconv[:, j, 0:n], in1=var[:, 0:n], op=ALU.mult),
                             reads=[YB, B("var")], writes=[YB])
                        k.op(DV, lambda j=j: nc.vector.tensor_scalar(out=yconv[:, j, 0:n], in0=yconv[:, j, 0:n], scalar1=pvo("blg", j), scalar2=pvo("blb", j),
                                                                   op0=ALU.mult, op1=ALU.add), reads=[YB, B("pv")], writes=[YB])
                        k.op(AC, lambda j=j: nc.scalar.activation(out=zz[:, j, 0:n], in_=yconv[:, j, 0:n], func=ACT.Silu), reads=[YB, B("zz")], writes=[B("zz")])
                    for j in range(c.BT):
                        w_, wbb = wload("wpw", l, j, c.BT)
                        ps_, pb = ps_next()
                        for kt in range(c.BT):
                            k.op(PE, lambda ps_=ps_, w_=w_, kt=kt: nc.tensor.matmul(ps_[:, 0:n], lhsT=w_[:, kt * P:(kt + 1) * P], rhs=zz[:, kt, 0:n],
                                                                                  start=(kt == 0), stop=(kt == c.BT - 1)),
                                 reads=[wbb, B("zz")], writes=[pb], pe_acc=(kt > 0), inc=(kt == c.BT - 1))
                        o_, obb = obuf()
                        k.op(AC, lambda o_=o_, ps_=ps_, j=j: nc.scalar.activation(out=o_[:, 0:n], in_=ps_[:, 0:n], func=ACT.Identity, bias=pvo("bpb", j), scale=1.0),
                             reads=[pb, B("pv")], writes=[obb])
                        k.dma(PO, catT[(c.AH + j) * P:(c.AH + j + 1) * P, t0:t0 + n], o_[:, 0:n], reads=[obb], writes=[B("cat", c.AH + j, ci)])
            ck('D3')
            with Scope() as ss:
                cat_sb = ss("cat_sb", [P, NT, 512], BF16)
                wload = mk_w(ss, NT)
                xres = [ss(f"xres{i}", [P, 512]) for i in range(2)]
                xnew = [ss(f"xnew{i}", [P, 512]) for i in range(2)]
                xe_sb = ss("xe_sb", [P, 2 * NT])
                for ci, (t0, n) in enumerate(c.chunks):
                    if ci == 0 and last:
                        continue
                    v = 1 if ci == 0 else 0
                    k.dma(SP, cat_sb[:, :, 0:n], catT[:, t0:t0 + n].rearrange("(t p) n -> p t n", p=P), reads=[B("cat", t, ci) for t in range(NT)], writes=[B("cat_sb")])
                    for ct in range(NT):
                        w_, wbb = wload("wout", l, ct, NT)
                        ps_, pb = ps_next()
                        for kt in range(NT):
                            k.op(PE, lambda ps_=ps_, w_=w_, kt=kt: nc.tensor.matmul(ps_[:, 0:n], lhsT=w_[:, kt * P:(kt + 1) * P], rhs=cat_sb[:, kt, 0:n],
                                                                                  start=(kt == 0), stop=(kt == NT - 1)),
                                 reads=[wbb, B("cat_sb")], writes=[pb], pe_acc=(kt > 0), inc=(kt == NT - 1))
                        xr, xrb = xres[ct % 2], B("xres", ct % 2)
                        k.dma(SP, xr[:, 0:n], X[ct * P:(ct + 1) * P, t0:t0 + n], reads=[B(XN, ci, ct)], writes=[xrb])
                        xn, xnb = xnew[ct % 2], B("xnew", ct % 2)
                        k.op(DV, lambda xn=xn, ps_=ps_, xr=xr, ct=ct: nc.vector.scalar_tensor_tensor(out=xn[:, 0:n], in0=ps_[:, 0:n], scalar=mod(l, v, 2, ct), in1=xr[:, 0:n],
                                                                                                  op0=ALU.mult, op1=ALU.add), reads=[pb, xrb, MS], writes=[xnb])
                        k.dma(PO, X[ct * P:(ct + 1) * P, t0:t0 + n], xn[:, 0:n], reads=[xnb], writes=[B(XN, ci, ct)])
                        if ci == 1:
                            k.op(DV, lambda xn=xn, ct=ct: nc.vector.tensor_copy(out=xe_sb[:, 2 * ct:2 * ct + 1], in_=xn[:, 0:1]), reads=[xnb, B("xe_sb")], writes=[B("xe_sb")])
                        if ci == c.NBLK:
                            k.op(DV, lambda xn=xn, ct=ct: nc.vector.tensor_copy(out=xe_sb[:, 2 * ct + 1:2 * ct + 2], in_=xn[:, n - 1:n]), reads=[xnb, B("xe_sb")], writes=[B("xe_sb")])
                k.dma(PO, xeS[:, :], xe_sb[:], reads=[B("xe_sb")], writes=[B("xeS")])
            k.allgather(PAIRS, xeS[:, :], xeG[:, :], reads=[B("xeS")], writes=[B("xeG")])
            ck('E')
            Y, YN = nxt
            with Scope() as ss:
                wk = mk_norm_bufs(ss)
                tmp = wk[3]
                hsb = ss("hsb", [P, NT, 514], BF16)
                wload = mk_w(ss, max(NT, c.FT), nb=3)
                U = [ss(f"U{i}", [P, 514]) for i in range(2)]
                act_sb = ss("act_sb", [P, c.FT, 512], BF16)
                gsil = ss("gsil", [P, 512])
                xres = [ss(f"xres{i}", [P, 512]) for i in range(2)]
                xnew = [ss(f"xnew{i}", [P, 512]) for i in range(2)]

                def xepiece(r, e):
                    return (lambda gi, G: xeG[r * P:(r + 1) * P, :].rearrange("p (t e) -> p t e", e=2)[:, gi:gi + G, e:e + 1],
                            lambda gi, G: [B("xeG")], 1)
                for ci, (t0, n) in enumerate(c.chunks):
                    if ci == 0 and last:
                        continue
                    v = 1 if ci == 0 else 0
                    W = n + 2
                    first_lat, last_lat = ci == 1, ci == c.NBLK
                    if ci == 0:
                        left, right = xpiece(X, XN, ci, t0, t0 + 1), xpiece(X, XN, ci, t0 + n - 1, t0 + n)
                    else:
                        left = xepiece(0, 1) if first_lat else xpiece(X, XN, ci - 1, t0 - 1, t0)
                        right = xepiece(1, 0) if last_lat else xpiece(X, XN, ci + 1, t0 + n, t0 + n + 1)
                    norm_mod(ss, [left, xpiece(X, XN, ci, t0, t0 + n), right], W, l, v, 4, 3, hsb, wk)
                    HB = B("hsb")
                    hw = W // 2
                    for ft in range(c.FT):
                        for part in range(2):
                            ct = ft + part * c.FT
                            w_, wbb = wload("wup", l, ct, NT)
                            pA, pAb = ps_next()
                            pB, pBb = ps_next()
                            for (pp, ppb, a) in [(pA, pAb, 0), (pB, pBb, hw)]:
                                for kt in range(NT):
                                    k.op(PE, lambda pp=pp, w_=w_, kt=kt, a=a: nc.tensor.matmul(pp[:, 0:hw], lhsT=w_[:, kt * P:(kt + 1) * P], rhs=hsb[:, kt, a:a + hw],
                                                                                             start=(kt == 0), stop=(kt == NT - 1)),
                                         reads=[wbb, HB], writes=[ppb], pe_acc=(kt > 0), inc=(kt == NT - 1))
                            u_, ub = U[part], B("U", part)
                            k.op(AC, lambda u_=u_, pA=pA: nc.scalar.copy(out=u_[:, 0:hw], in_=pA[:, 0:hw]), reads=[pAb], writes=[ub])
                            k.op(AC, lambda u_=u_, pB=pB: nc.scalar.copy(out=u_[:, hw:W], in_=pB[:, 0:hw]), reads=[pBb, ub], writes=[ub])
                            for (colx, isl) in [(0, True), (W - 1, False)]:
                                if ci == 0:
                                    k.op(DV, lambda u_=u_, colx=colx: nc.vector.memset(u_[:, colx:colx + 1], 0.0), reads=[ub], writes=[ub])
                                elif (isl and first_lat) or ((not isl) and last_lat):
                                    fcol = 4 if isl else 5
                                    k.op(DV, lambda u_=u_, colx=colx, fcol=fcol: nc.vector.tensor_scalar(out=u_[:, colx:colx + 1], in0=u_[:, colx:colx + 1],
                                                                                                       scalar1=flg[:, fcol:fcol + 1], scalar2=None, op0=ALU.mult),
                                         reads=[ub, CB], writes=[ub])
                            cv, cvb = tmp()
                            k.op(DV, lambda cv=cv, u_=u_, ct=ct: nc.vector.tensor_scalar(out=cv[:, 0:n], in0=u_[:, 1:n + 1], scalar1=pvo("fdw", ct * 3 + 1), scalar2=pvo("fdb", ct),
                                                                                      op0=ALU.mult, op1=ALU.add), reads=[ub, B("pv")], writes=[cvb])
                            k.op(DV, lambda cv=cv, u_=u_, ct=ct: nc.vector.scalar_tensor_tensor(out=cv[:, 0:n], in0=u_[:, 0:n], scalar=pvo("fdw", ct * 3), in1=cv[:, 0:n],
                                                                                             op0=ALU.mult, op1=ALU.add), reads=[ub, B("pv"), cvb], writes=[cvb])
                            k.op(DV, lambda cv=cv, u_=u_, ct=ct: nc.vector.scalar_tensor_tensor(out=cv[:, 0:n], in0=u_[:, 2:n + 2], scalar=pvo("fdw", ct * 3 + 2), in1=cv[:, 0:n],
                                                                                             op0=ALU.mult, op1=ALU.add), reads=[ub, B("pv"), cvb], writes=[cvb])
                            if part == 0:
                                k.op(AC, lambda cv=cv: nc.scalar.activation(out=gsil[:, 0:n], in_=cv[:, 0:n], func=ACT.Silu), reads=[cvb], writes=[B("gsil")])
                            else:
                                k.op(DV, lambda cv=cv, ft=ft: nc.vector.tensor_tensor(out=act_sb[:, ft, 0:n], in0=cv[:, 0:n], in1=gsil[:, 0:n], op=ALU.mult),
                                     reads=[cvb, B("gsil"), B("act_sb")], writes=[B("act_sb")])
                    for ct in range(NT):
                        w_, wbb = wload("wdn", l, ct, c.FT)
                        ps_, pb = ps_next()
                        for kt in range(c.FT):
                            k.op(PE, lambda ps_=ps_, w_=w_, kt=kt: nc.tensor.matmul(ps_[:, 0:n], lhsT=w_[:, kt * P:(kt + 1) * P], rhs=act_sb[:, kt, 0:n],
                                                                                  start=(kt == 0), stop=(kt == c.FT - 1)),
                                 reads=[wbb, B("act_sb")], writes=[pb], pe_acc=(kt > 0), inc=(kt == c.FT - 1))
                        xr, xrb = xres[ct % 2], B("xres", ct % 2)
                        k.dma(SP, xr[:, 0:n], X[ct * P:(ct + 1) * P, t0:t0 + n], reads=[B(XN, ci, ct)], writes=[xrb])
                        xn, xnb = xnew[ct % 2], B("xnew", ct % 2)
                        k.op(DV, lambda xn=xn, ps_=ps_, xr=xr, ct=ct: nc.vector.scalar_tensor_tensor(out=xn[:, 0:n], in0=ps_[:, 0:n], scalar=mod(l, v, 5, ct), in1=xr[:, 0:n],
                                                                                                  op0=ALU.mult, op1=ALU.add), reads=[pb, xrb, MS], writes=[xnb])
                        k.dma(PO, Y[ct * P:(ct + 1) * P, t0:t0 + n], xn[:, 0:n], reads=[xnb], writes=[B(YN, ci, ct)])
            cur, nxt = nxt, cur

        X, XN = cur
        with Scope() as ss:
            wk = mk_norm_bufs(ss)
            tmp = wk[3]
            for ci in lat:
                t0, n = c.chunks[ci]

                def ydst(t, tm, tb, t0=t0, n=n):
                    o_, ob_ = tmp()
                    k.op(DV, lambda: nc.vector.tensor_scalar(out=o_[:, 0:n], in0=tm[:, 0:n], scalar1=fin_g[:, t:t + 1], scalar2=None, op0=ALU.mult),
                         reads=[tb, CB], writes=[ob_])
                    k.dma(PO, yout[t * P:(t + 1) * P, t0 - CTX:t0 - CTX + n], o_[:, 0:n], reads=[ob_], writes=[B("yout")])
                norm_mod(ss, [xpiece(X, XN, ci, t0, t0 + n)], n, 0, 0, 0, 0, None, wk, ydst=ydst)
        k.barrier()
    return nc


def _fm(vec, nt):
    return np.ascontiguousarray(vec.reshape(nt, P).T)


def _tile_major(w):
    K_, N_ = w.shape
    return np.ascontiguousarray(w.reshape(K_ // P, P, N_ // P, P).transpose(2, 1, 0, 3)).reshape(-1)


def rope_tables(cfg, half):
    c = cfg
    t = np.arange(c.SL, dtype=np.int64) + half * c.SL
    row = (t // c.GW).astype(np.float32)
    col = (t % c.GW).astype(np.float32)
    hh = P // 2
    inv = (np.float32(10000.0) ** (-np.arange(0, hh, 2, dtype=np.float32) / np.float32(hh))).astype(np.float32)
    ang = np.concatenate([row[:, None] * inv, col[:, None] * inv], axis=-1).astype(np.float32)
    cs, sn = np.cos(ang).astype(np.float32), np.sin(ang).astype(np.float32)
    C = np.ones((P, c.T), np.float32)
    S = np.zeros((P, c.T), np.float32)
    C[:, c.CTX:] = np.repeat(cs.T, 2, axis=0)
    S[:, c.CTX:] = np.repeat(sn.T, 2, axis=0)
    return C, S


def nbr_mask(cfg, half):
    c = cfg
    base = half * c.RL
    m = np.full((c.NBLK, 2, 64, 8, 8, 64), NEG, np.float32)
    wq = np.arange(64)
    cstart = np.clip(wq - 8, 0, 64 - 16)
    wk = np.arange(64)
    colok = (wk[:, None] >= cstart[None, :]) & (wk[:, None] < cstart[None, :] + 16)
    for bi in range(c.NBLK):
        for a in range(8):
            r = base + 8 * bi + a
            r0 = int(np.clip(r - 4, 0, c.ROWS - 8))
            for j in range(8):
                for b in range(2):
                    rk = base + 8 * bi + 2 * j + b - 4
                    if r0 <= rk < r0 + 8:
                        m[bi, b, :, j, a, :] = np.where(colok, 0.0, NEG)
    return m.reshape(c.NBLK * P, 8 * 512).astype(ml_dtypes.bfloat16)


def prep_inputs(cfg, inp):
    c = cfg
    L, D, NT = c.L, c.D, c.NT
    f = lambda a: np.asarray(a, dtype=np.float32)
    x, cc, ctx, c_ctx = f(inp["x"]), f(inp["c"]), f(inp["ctx"]), f(inp["c_ctx"])
    consts = np.zeros((P, 3 * P), np.float32)
    consts[:, 0:P] = np.eye(P, dtype=np.float32)
    for i in range(P // 2):
        consts[2 * i + 1, P + 2 * i] = -1.0
        consts[2 * i, P + 2 * i + 1] = 1.0
    consts[:, 2 * P:] = 1.0
    c5 = np.concatenate([cc, c_ctx[None]], 0)
    c5T = np.ascontiguousarray(c5.reshape(5, NT, P).transpose(2, 1, 0)).reshape(P, NT * 5)
    pvec = np.zeros((L, P, c.NP), np.float32)
    for l in range(L):
        pvec[l, :, c.pv["n1g"]:c.pv["n1g"] + NT] = _fm(f(inp["norm1_g"])[l], NT)
        pvec[l, :, c.pv["n2g"]:c.pv["n2g"] + NT] = _fm(f(inp["norm2_g"])[l], NT)
        pvec[l, :, c.pv["qg"]] = f(inp["a_qn_g"])[l]
        pvec[l, :, c.pv["kg"]] = f(inp["a_kn_g"])[l]
        bd = f(inp["b_dw_w"])[l]
        pvec[l, :, c.pv["bdw"]:c.pv["bdw"] + c.BT * 31] = bd.reshape(31, c.BT, P).transpose(2, 1, 0).reshape(P, c.BT * 31)
        for nm, key in [("bdb", "b_dw_b"), ("blg", "b_ln_g"), ("blb", "b_ln_b"), ("bpb", "b_pw_b")]:
            pvec[l, :, c.pv[nm]:c.pv[nm] + c.BT] = _fm(f(inp[key])[l], c.BT)
        fd = f(inp["ffn_dw_w"])[l]
        pvec[l, :, c.pv["fdw"]:c.pv["fdw"] + 2 * c.FT * 3] = fd.reshape(3, 2 * c.FT, P).transpose(2, 1, 0).reshape(P, 2 * c.FT * 3)
        pvec[l, :, c.pv["fdb"]:c.pv["fdb"] + 2 * c.FT] = _fm(f(inp["ffn_dw_b"])[l], 2 * c.FT)
    pvec = pvec.reshape(L * P, c.NP)
    fing = _fm(f(inp["final_g"]), NT)
    rpb = f(inp["c_rpb"])
    wk, wq = np.arange(64)[:, None], np.arange(64)[None, :]
    cidx = np.clip(wk - wq + 15, 0, 30)
    tab = np.zeros((L, c.CH, 64, 23, 64), np.float32)
    for dp in range(23):
        dl = 11 - dp
        if -7 <= dl <= 7:
            tab[:, :, :, dp, :] = rpb[:, :, dl + 7, :][:, :, cidx]
    rpbT = tab.reshape(L * c.CH * 64, 23 * 64)
    wflat = {}
    for n, key in [("win", "w_in"), ("wout", "w_out"), ("wup", "ffn_w_up"), ("wdn", "ffn_w_down"), ("wpw", "b_pw_w")]:
        w = f(inp[key])
        rows = w.shape[1] * w.shape[2] // 8 // 1024
        blk = wblk(rows)
        wflat[n] = np.stack([_tile_major(w[l]).reshape(rows // blk, 8, blk * 1024).transpose(1, 0, 2).reshape(8, -1) for l in range(L)], 0)
    ada_w, ada_b = f(inp["ada_w"]), f(inp["ada_b"])
    maps = []
    for r in range(8):
        b, half = r // 2, r % 2
        m = {}
        xT = np.empty((D, c.T), np.float32)
        xT[:, :c.CTX] = ctx[b].T
        xT[:, c.CTX:] = x[b, half * c.SL:(half + 1) * c.SL].T
        m["xin"] = xT
        m["c5T"] = c5T
        m["ada_s"] = np.ascontiguousarray(ada_w[:, :, r * c.MC:(r + 1) * c.MC]).reshape(L * D, c.MC)
        m["adab5"] = np.ascontiguousarray(np.broadcast_to(ada_b[None, :, r * c.MC:(r + 1) * c.MC], (5, L, c.MC))).reshape(5, L * c.MC)
        fl = np.zeros((P, 8), np.float32)
        fl[:, b] = 1.0
        fl[:, 4] = 1.0 if half == 1 else 0.0
        fl[:, 5] = 1.0 if half == 0 else 0.0
        m["flags"] = fl
        m["pvec"] = pvec
        m["fing"] = fing
        C_, S_ = rope_tables(c, half)
        m["cosT"], m["sinT"] = C_, S_
        m["consts"] = consts
        m["nmask"] = nbr_mask(c, half)
        m["rpbT"] = rpbT
        for n in wflat:
            m[n + "_s"] = np.ascontiguousarray(wflat[n][:, r, :]).reshape(-1, 1024)
        maps.append(m)
    return maps


_NC_CACHE = {}


def run(cfg, inp):
    key = (cfg.D, cfg.SEQ, cfg.L)
    if key not in _NC_CACHE:
        _NC_CACHE[key] = build(cfg)
    nc = _NC_CACHE[key]
    maps = prep_inputs(cfg, inp)
    res = run_bass_kernel_spmd(nc, maps, core_ids=list(range(8)))
    out = np.empty((cfg.B, cfg.SEQ, cfg.D), np.float32)
    for r in range(8):
        b, half = r // 2, r % 2
        out[b, half * cfg.SL:(half + 1) * cfg.SL, :] = res.results[r]["yout"].T
    return out


def kernel(**inputs):
    return run(Cfg(), inputs)
```

```python
import numpy as np
import ml_dtypes
from contextlib import ExitStack
import concourse.bass as bass
import concourse.mybir as mybir
from concourse.bass_utils import run_bass_kernel_spmd

F32 = mybir.dt.float32
BF16 = mybir.dt.bfloat16
ACT = mybir.ActivationFunctionType
ALU = mybir.AluOpType
NEG = -1e30
EPS = 1e-6
P = 128


class _Stop(Exception):
    pass


class Cfg:
    stop = None

    def __init__(s, D=4096, SEQ=4096, L=4):
        s.D, s.SEQ, s.L = D, SEQ, L
        s.B, s.CTX, s.GW = 4, 256, 64
        s.NT = D // P
        NH = D // P
        s.AH = 3 * NH // 8
        s.AKV = s.AH // 3
        s.CH = 3 * NH // 8
        s.BCH = D - (s.AH + s.CH) * P
        s.BT = s.BCH // P
        s.DFF = 11 * D // 8
        s.FT = s.DFF // P
        s.INW = s.AH * P + 2 * s.AKV * P + 2 * s.BCH + 3 * s.CH * P
        s.CTI = s.INW // P
        s.SL = SEQ // 2
        s.T = s.CTX + s.SL
        s.ROWS = SEQ // s.GW
        s.RL = s.SL // s.GW
        s.NBLK = s.SL // 512
        s.MC = 6 * D // 8
        s.o_qa = 0
        s.o_ka = s.o_qa + s.AH
        s.o_va = s.o_ka + s.AKV
        s.o_ba = s.o_va + s.AKV
        s.o_bg = s.o_ba + s.BT
        s.o_qn = s.o_bg + s.BT
        s.o_kn = s.o_qn + s.CH
        s.o_vn = s.o_kn + s.CH
        o = 0
        s.pv = {}
        for name, n in [("n1g", s.NT), ("n2g", s.NT), ("qg", 1), ("kg", 1), ("bdw", s.BT * 31), ("bdb", s.BT),
                        ("blg", s.BT), ("blb", s.BT), ("bpb", s.BT), ("fdw", 2 * s.FT * 3), ("fdb", 2 * s.FT)]:
            s.pv[name] = o
            o += n
        s.NP = o
        s.chunks = [(0, s.CTX)] + [(s.CTX + 512 * i, 512) for i in range(s.NBLK)]


class Sem:
    def __init__(s, h):
        s.h, s.count = h, 0


class Buf:
    def __init__(s, name):
        s.name, s.w, s.r = name, None, []


class Eng:
    def __init__(s, e, sems):
        s.e, s.sems, s.si, s.seen = e, sems, 0, {}
        s.sem = sems[0]

    def rotate(s):
        if s.sem.count >= 30000:
            s.si += 1
            s.sem = s.sems[s.si]


class K:
    def __init__(s, nc, stack, n_eng_sems=10, n_dma_sems=20):
        s.nc = nc
        mk = lambda nm: Sem(stack.enter_context(nc.semaphore(nm)))
        s.pe = Eng(nc.tensor, [mk(f"pe{i}") for i in range(5)])
        s.act = Eng(nc.scalar, [mk(f"ac{i}") for i in range(4)])
        s.dve = Eng(nc.vector, [mk(f"dv{i}") for i in range(6)])
        s.pool = Eng(nc.gpsimd, [mk(f"po{i}") for i in range(2)])
        s.sp = Eng(nc.sync, [mk("spx")])
        s.dsem = {id(s.sp): [mk(f"ds{i}") for i in range(16)],
                  id(s.pool): [mk(f"dp{i}") for i in range(28)]}
        s.dsi = {k: 0 for k in s.dsem}
        s.ccsem = [mk(f"cc{i}") for i in range(4)]
        s.cci = 0
        s.bufs = {}

    def B(s, *key):
        if key not in s.bufs:
            s.bufs[key] = Buf(str(key))
        return s.bufs[key]

    def _wait(s, eng, reads, writes, pe_acc=False):
        deps = {}

        def add(tok):
            if tok is None:
                return
            sem, val = tok
            if deps.get(id(sem), (None, 0))[1] < val:
                deps[id(sem)] = (sem, val)
        for b in reads:
            add(b.w)
        for b in writes:
            if not (pe_acc and b.w is not None and b.w[0] in eng.sems):
                add(b.w)
            for t in b.r:
                add(t)
        for sem, val in deps.values():
            if pe_acc and sem in eng.sems:
                continue
            if eng.seen.get(id(sem), 0) < val:
                eng.e.wait_ge(sem.h, val)
                eng.seen[id(sem)] = val

    def _done(s, tok, reads, writes):
        for b in reads:
            b.r.append(tok)
            if len(b.r) > 64:
                b.r = b.r[-64:] if False else b.r
        for b in writes:
            b.w, b.r = tok, []

    def op(s, eng, fn, reads=(), writes=(), inc=True, pe_acc=False):
        s._wait(eng, reads, writes, pe_acc)
        ins = fn()
        if inc:
            eng.sem.count += 1
            ins.then_inc(eng.sem.h, 1)
            tok = (eng.sem, eng.sem.count)
            s._done(tok, reads, writes)
            eng.rotate()
        else:
            tok = (eng.sem, eng.sem.count + 1)
            s._done(tok, reads, writes)
        return ins

    def dma(s, eng, out, in_, reads=(), writes=(), slow=False):
        s._wait(eng, reads, writes)
        pool = s.dsem[id(eng)]
        sem = pool[s.dsi[id(eng)] % len(pool)]
        s.dsi[id(eng)] += 1
        if sem.count and eng.seen.get(id(sem), 0) < sem.count:
            eng.e.wait_ge(sem.h, sem.count)
            eng.seen[id(sem)] = sem.count
        ins = eng.e.dma_start(out=out, in_=in_, allow_slow_non_contiguous=True) if slow else eng.e.dma_start(out=out, in_=in_)
        sem.count += 16
        ins.then_inc(sem.h, 16)
        s._done((sem, sem.count), reads, writes)

    def allgather(s, groups, in_ap, out_ap, reads=(), writes=()):
        eng = s.pool
        s._wait(eng, reads, writes)
        sem = s.ccsem[0]
        ins = eng.e.collective_compute("AllGather", ALU.bypass, replica_groups=groups,
                                       ins=[in_ap.opt()], outs=[out_ap.opt()])
        sem.count += 1
        ins.then_inc(sem.h)
        s._done((sem, sem.count), reads, writes)

    def wait_all(s, eng, bufs):
        s._wait(eng, bufs, ())

    def barrier(s):
        sems = []
        for e in (s.pe, s.act, s.dve, s.pool, s.sp):
            sems += e.sems
        for v in s.dsem.values():
            sems += v
        sems += s.ccsem
        for e in (s.pe, s.act, s.dve, s.pool, s.sp):
            for sem in sems:
                if sem.count and e.seen.get(id(sem), 0) < sem.count:
                    e.e.wait_ge(sem.h, sem.count)
                    e.seen[id(sem)] = sem.count


PAIRS = [[0, 1], [2, 3], [4, 5], [6, 7]]
ALL8 = [list(range(8))]
QUADS = [[0, 1, 2, 3], [4, 5, 6, 7]]
P4 = [[0, 4], [1, 5], [2, 6], [3, 7]]


def wblk(rows):
    b = min(128, rows)
    while rows % b:
        b -= 1
    return b


def build(cfg):
    try:
        return _build(cfg)
    except _Stop as e:
        return e.args[0]


def _build(cfg):
    c = cfg
    D, NT, T, SL, CTX, L = c.D, c.NT, c.T, c.SL, c.CTX, c.L
    nc = bass.Bass("TRN2", target_bir_lowering=False)
    din = lambda n, sh, dt=F32: nc.dram_tensor(n, sh, dt, kind="ExternalInput")
    dsc = lambda n, sh, dt=F32: nc.dram_tensor(n, sh, dt)
    xin = din("xin", [D, T])
    c5T = din("c5T", [P, NT * 5])
    ada_s = din("ada_s", [L * D, c.MC])
    adab5 = din("adab5", [5, L * c.MC])
    flags = din("flags", [P, 8])
    pvec = din("pvec", [L * P, c.NP])
    fing = din("fing", [P, NT])
    cosT = din("cosT", [P, T])
    sinT = din("sinT", [P, T])
    consts = din("consts", [P, 3 * P])
    nmask = din("nmask", [c.NBLK * P, 8 * 512], BF16)
    rpbT = din("rpbT", [L * c.CH * 64, 23 * 64])
    wspec = {"win": (D, c.INW), "wout": (D, D), "wup": (D, 2 * c.DFF), "wdn": (c.DFF, D), "wpw": (c.BCH, c.BCH)}
    wsh, wbf, wfull, wquad = {}, {}, {}, {}
    for n, (kk, nn) in wspec.items():
        rows = kk * nn // 8 // 1024
        wsh[n] = din(n + "_s", [L * rows, 1024])
        wbf[n] = [dsc(f"{n}_b{l}", [rows, 1024], BF16) for l in range(L)]
        wfull[n] = [dsc(f"{n}_f{l}", [8 * rows, 1024], BF16) for l in range(L)]
        wquad[n] = [dsc(f"{n}_q{l}", [4 * rows, 1024], BF16) for l in range(L)]
    yout = nc.dram_tensor("yout", [D, SL], F32, kind="ExternalOutput")
    xA = dsc("xA", [D, T]); xB_ = dsc("xB", [D, T])
    qaT = dsc("qaT", [c.AH * P, T], BF16)
    kaT = dsc("kaT", [c.AKV * P, T], BF16)
    va = dsc("va", [c.AKV * T, P], BF16)
    qnT = dsc("qnT", [c.CH * P, T], BF16)
    knT = dsc("knT", [c.CH * P, T], BF16)
    vn = dsc("vn", [c.CH * T, P], BF16)
    ZW = 15 + CTX + 15 + 15 + SL + 15
    zp = dsc("zp", [c.BCH, ZW])
    catT = dsc("catT", [D, T], BF16)
    kaS = dsc("kaS", [c.AKV * P, SL], BF16); kaG = dsc("kaG", [2 * c.AKV * P, SL], BF16)
    vaS = dsc("vaS", [c.AKV * SL, P], BF16); vaG = dsc("vaG", [2 * c.AKV * SL, P], BF16)
    knS = dsc("knS", [c.CH * P, 512], BF16); knG = dsc("knG", [2 * c.CH * P, 512], BF16)
    vnS = dsc("vnS", [c.CH * 512, P], BF16); vnG = dsc("vnG", [2 * c.CH * 512, P], BF16)
    zS = dsc("zS", [c.BCH, 32]); zG = dsc("zG", [2 * c.BCH, 32])
    xeS = dsc("xeS", [P, 2 * NT]); xeG = dsc("xeG", [2 * P, 2 * NT])
    modS = dsc("modS", [5, L * c.MC]); modG = dsc("modG", [8 * 5, L * c.MC]); modQ = dsc("modQ", [4 * 5, L * c.MC])

    with ExitStack() as st:
        k = K(nc, st)
        PE, AC, DV, PO, SP = k.pe, k.act, k.dve, k.pool, k.sp
        B = k.B

        uid = [0]

        class Scope:
            def __enter__(s):
                s.st = ExitStack()
                s.st.__enter__()
                uid[0] += 1
                u = uid[0]
                return lambda n, sh, dt=F32: s.st.enter_context(nc.sbuf_tensor(f"{n}_u{u}", sh, dt))

            def __exit__(s, *a):
                k.barrier()
                return s.st.__exit__(*a)

        sb = lambda n, sh, dt=F32: st.enter_context(nc.sbuf_tensor(n, sh, dt))
        psb = [st.enter_context(nc.psum_tensor(f"ps{i}", [P, 512], F32)) for i in range(7)]
        pst = st.enter_context(nc.psum_tensor("pst", [P, 4 * P], BF16))
        rot_i = [0]

        def ps_next():
            i = rot_i[0] % 4
            rot_i[0] += 1
            return psb[i], B("ps", i)
        PS_O, PS_DEN, PS_AUX = (psb[4], B("ps", 4)), (psb[5], B("ps", 5)), (psb[6], B("ps", 6))
        PST = (pst, B("pst"))
        ident_f = sb("ident_f", [P, P]); rot_f = sb("rot_f", [P, P]); ones_f = sb("ones_f", [P, P])
        ident_b = sb("ident_b", [P, P], BF16); ones_b = sb("ones_b", [P, P], BF16)
        flg = sb("flg", [P, 8]); fin_g = sb("fin_g", [P, NT])
        zero_f = sb("zero_f", [P, 16])
        msb = sb("msb", [P, L, 2, 6 * NT])
        pv = sb("pv", [P, L, c.NP])
        qgs = sb("qgs", [P, L])
        CB = B("consts")
        k.dma(SP, ident_f[:], consts[:, 0:P], writes=[CB])
        k.dma(SP, rot_f[:], consts[:, P:2 * P], writes=[CB])
        k.dma(SP, ones_f[:], consts[:, 2 * P:3 * P], writes=[CB])
        k.dma(PO, ident_b[:], consts[:, 0:P], writes=[CB])
        k.dma(PO, ones_b[:], consts[:, 2 * P:3 * P], writes=[CB])
        k.dma(SP, flg[:], flags[:, :], writes=[CB])
        k.dma(SP, fin_g[:], fing[:, :], writes=[CB])
        k.op(DV, lambda: nc.vector.memset(zero_f[:], 0.0), writes=[CB])

        def wpieces(l):
            for n, (kk, nn) in wspec.items():
                rows = kk * nn // 8 // 1024
                blk = wblk(rows)
                for pi in range(rows // blk):
                    yield n, rows, blk, pi
        for l in range(L):
            for n, rows, blk, pi in wpieces(l):
                r0 = pi * blk
                k.dma(PO, wbf[n][l][r0:r0 + blk, :], wsh[n][l * rows + r0:l * rows + r0 + blk, :], writes=[B("wbfp", n, l, pi)])
        for l in range(L):
            for n, rows, blk, pi in wpieces(l):
                r0, q0 = pi * blk, pi * 4 * blk
                k.allgather(QUADS, wbf[n][l][r0:r0 + blk, :], wquad[n][l][q0:q0 + 4 * blk, :],
                            reads=[B("wbfp", n, l, pi)], writes=[B("wq", n, l, pi)])
            for n, rows, blk, pi in wpieces(l):
                q0 = pi * 4 * blk
                k.allgather(P4, wquad[n][l][q0:q0 + 4 * blk, :], wfull[n][l][2 * q0:2 * q0 + 8 * blk, :],
                            reads=[B("wq", n, l, pi)], writes=[B("wf", n, l)])

        def ck(name):
            if c.stop == name:
                k.barrier()
                raise _Stop(nc)
        ck('weights')

        def wtile(n, l, ct, KT):
            v = wfull[n][l].ap().rearrange("r j -> (r j)").rearrange("(c p q) -> c p q", p=P, q=KT * P)
            return v[ct]

        ng = 1
        while c.MC % ng or c.MC // ng > 512:
            ng += 1
        gw = c.MC // ng
        assert ng <= 6
        NJ = c.MC // P
        NR = 8 * NJ
        with Scope() as ss:
            c5 = ss("c5", [P, NT * 5])
            adat = [ss(f"adat{i}", [P, c.MC]) for i in range(2)]
            modrow = ss("modrow", [5, c.MC]); adabs = ss("adabs", [5, c.MC])
            modT = ss("modT", [P, L, 5, NR])
            mrow = [ss(f"mrow{i}", [P, P]) for i in range(2)]
            k.dma(SP, c5[:], c5T[:, :], writes=[B("c5")])
            k.op(AC, lambda: nc.scalar.activation(out=c5[:], in_=c5[:], func=ACT.Silu), reads=[B("c5")], writes=[B("c5")])
            for l in range(L):
                k.dma(SP, adabs[:], adab5[:, l * c.MC:(l + 1) * c.MC], writes=[B("adabs")])
                for kt in range(NT):
                    at, ab = adat[kt % 2], B("adat", kt % 2)
                    k.dma(SP, at[:], ada_s[l * D + kt * P:l * D + (kt + 1) * P, :], writes=[ab])
                    for g in range(ng):
                        psg, pb = psb[g], B("ps", g)
                        k.op(PE, lambda psg=psg, at=at, g=g, kt=kt: nc.tensor.matmul(
                            psg[0:5, 0:gw], lhsT=c5[:, kt * 5:(kt + 1) * 5], rhs=at[:, g * gw:(g + 1) * gw],
                            start=(kt == 0), stop=(kt == NT - 1)),
                            reads=[ab, B("c5")], writes=[pb], pe_acc=(kt > 0))
                for g in range(ng):
                    psg, pb = psb[g], B("ps", g)
                    k.op(DV, lambda psg=psg, g=g: nc.vector.tensor_tensor(
                        out=modrow[:, g * gw:(g + 1) * gw], in0=psg[0:5, 0:gw], in1=adabs[:, g * gw:(g + 1) * gw], op=ALU.add),
                        reads=[pb, B("adabs")], writes=[B("modrow")])
                k.dma(PO, modS[:, l * c.MC:(l + 1) * c.MC], modrow[:], reads=[B("modrow")], writes=[B("modS")])
            k.allgather(QUADS, modS[:, :], modQ[:, :], reads=[B("modS")], writes=[B("modQ")])
            k.allgather(P4, modQ[:, :], modG[:, :], reads=[B("modQ")], writes=[B("modG")])
            cnt = 0
            mg = modG.ap().rearrange("(k r) (l j q) -> r l k j q", r=5, l=L, q=P)
            for l in range(L):
                for r in range(5):
                    for h0 in range(0, NR, P):
                        nrow = min(P, NR - h0)
                        mr, mb = mrow[cnt % 2], B("mrow", cnt % 2)
                        cnt += 1
                        for kk_ in range(8):
                            lo, hi = max(h0, kk_ * NJ), min(h0 + nrow, (kk_ + 1) * NJ)
                            if lo < hi:
                                k.dma(SP, mr[lo - h0:hi - h0, :], mg[r, l, kk_, lo - kk_ * NJ:hi - kk_ * NJ, :],
                                      reads=[B("modG")], writes=[mb])
                        k.op(PE, lambda mr=mr, nrow=nrow: nc.tensor.transpose(PS_AUX[0][:, 0:nrow], mr[0:nrow, :], ident_f[0:nrow, 0:nrow]),
                             reads=[mb, CB], writes=[PS_AUX[1]])
                        k.op(DV, lambda l=l, r=r, h0=h0, nrow=nrow: nc.vector.tensor_copy(out=modT[:, l, r, h0:h0 + nrow], in_=PS_AUX[0][:, 0:nrow]),
                             reads=[PS_AUX[1]], writes=[B("modT")])
            for l in range(L):
                k.dma(SP, pv[:, l, :], pvec[l * P:(l + 1) * P, :], writes=[B("pv")])
            for l in range(L):
                k.op(DV, lambda l=l: nc.vector.tensor_scalar(out=msb[:, l, 0, :], in0=modT[:, l, 0, :], scalar1=flg[:, 0:1], scalar2=None, op0=ALU.mult),
                     reads=[B("modT"), CB], writes=[B("msb")])
                for b in range(1, 4):
                    k.op(DV, lambda l=l, b=b: nc.vector.scalar_tensor_tensor(out=msb[:, l, 0, :], in0=modT[:, l, b, :], scalar=flg[:, b:b + 1],
                                                                            in1=msb[:, l, 0, :], op0=ALU.mult, op1=ALU.add),
                         reads=[B("modT"), B("msb"), CB], writes=[B("msb")])
                k.op(DV, lambda l=l: nc.vector.tensor_copy(out=msb[:, l, 1, :], in_=modT[:, l, 4, :]), reads=[B("modT"), B("msb")], writes=[B("msb")])
                for v in range(2):
                    for (mi, gname) in [(1, "n1g"), (4, "n2g")]:
                        k.op(DV, lambda l=l, v=v, mi=mi, gname=gname: nc.vector.scalar_tensor_tensor(
                            out=msb[:, l, v, mi * NT:(mi + 1) * NT], in0=msb[:, l, v, mi * NT:(mi + 1) * NT], scalar=1.0,
                            in1=pv[:, l, c.pv[gname]:c.pv[gname] + NT], op0=ALU.add, op1=ALU.mult),
                            reads=[B("msb"), B("pv")], writes=[B("msb")])
                k.op(DV, lambda l=l: nc.vector.tensor_scalar(out=qgs[:, l:l + 1], in0=pv[:, l, c.pv["qg"]:c.pv["qg"] + 1], scalar1=float(P) ** -0.5,
                                                            scalar2=None, op0=ALU.mult), reads=[B("pv")], writes=[B("qgs")])
        ck('mods')
        MS = B("msb")

        def mod(l, v, mi, t):
            return msb[:, l, v, mi * NT + t:mi * NT + t + 1]

        nch = len(c.chunks)
        lat = list(range(1, nch))
        XT = lambda nm, ci, t: B(nm, ci, t)
        XC = lambda nm, ci: [B(nm, ci, t) for t in range(NT)]
        for ci, (t0, n) in enumerate(c.chunks):
            k.dma(PO, xA[:, t0:t0 + n], xin[:, t0:t0 + n], writes=XC("xA", ci))
        for j in range(c.BT):
            k.dma(PO, zp[j * P:(j + 1) * P, 0:15], zero_f[:, 0:15], reads=[CB], writes=[B("zp", j, "pad")])
            k.dma(PO, zp[j * P:(j + 1) * P, 15 + CTX:15 + CTX + 15], zero_f[:, 0:15], reads=[CB], writes=[B("zp", j, "pad")])

        def xpiece(xt_, nm, ci, a, b):
            return (lambda gi, G: xt_[:, a:b].rearrange("(t p) n -> p t n", p=P)[:, gi:gi + G, :],
                    lambda gi, G: [B(nm, ci, t) for t in range(gi, gi + G)], b - a)

        def norm_mod(ss, pieces, W, l, v, mA, mS, hdst, wk, ydst=None):
            xt, sq, rstd, tmp = wk
            G = 2
            halves = [(0, W)] if W <= 512 else [(0, W // 2), (W // 2, W)]

            def load(gi):
                xb_, xbb = xt[(gi // G) % 2], B("xt", (gi // G) % 2)
                col = 0
                for (apf, bf, w) in pieces:
                    k.dma(SP, xb_[:, :, col:col + w], apf(gi, G), reads=bf(gi, G), writes=[xbb], slow=(w == 1))
                    col += w
                return xb_, xbb
            for gi in range(0, NT, G):
                xb_, xbb = load(gi)
                for tt in range(G):
                    t = gi + tt
                    s_, sbb = sq[t % 2], B("sq", t % 2)
                    k.op(AC, lambda xb_=xb_, tt=tt, s_=s_: nc.scalar.activation(out=s_[:, 0:W], in_=xb_[:, tt, 0:W], func=ACT.Square),
                         reads=[xbb], writes=[sbb])
                    for hi, (a, b) in enumerate(halves):
                        pp = [PS_AUX, PS_DEN][hi]
                        k.op(PE, lambda pp=pp, s_=s_, a=a, b=b, t=t: nc.tensor.matmul(pp[0][:, 0:b - a], lhsT=ones_b[:], rhs=s_[:, a:b],
                                                                                  start=(t == 0), stop=(t == NT - 1)),
                             reads=[sbb, CB], writes=[pp[1]], pe_acc=(t > 0))
            for hi, (a, b) in enumerate(halves):
                pp = [PS_AUX, PS_DEN][hi]
                k.op(DV, lambda pp=pp, a=a, b=b: nc.vector.tensor_scalar(out=rstd[:, a:b], in0=pp[0][:, 0:b - a], scalar1=1.0 / D, scalar2=EPS,
                                                                       op0=ALU.mult, op1=ALU.add), reads=[pp[1], B("rstd")], writes=[B("rstd")])
            k.op(AC, lambda: nc.scalar.sqrt(out=rstd[:, 0:W], in_=rstd[:, 0:W]), reads=[B("rstd")], writes=[B("rstd")])
            k.op(DV, lambda: nc.vector.reciprocal(out=rstd[:, 0:W], in_=rstd[:, 0:W]), reads=[B("rstd")], writes=[B("rstd")])
            for gi in range(0, NT, G):
                xb_, xbb = load(gi)
                for tt in range(G):
                    t = gi + tt
                    tm, tb = tmp()
                    k.op(DV, lambda xb_=xb_, tt=tt, tm=tm: nc.vector.tensor_tensor(out=tm[:, 0:W], in0=xb_[:, tt, 0:W], in1=rstd[:, 0:W], op=ALU.mult),
                         reads=[xbb, B("rstd")], writes=[tb])
                    if ydst is None:
                        k.op(AC, lambda tm=tm, t=t: nc.scalar.activation(out=hdst[:, t, 0:W], in_=tm[:, 0:W], func=ACT.Identity,
                                                                       bias=mod(l, v, mS, t), scale=mod(l, v, mA, t)),
                             reads=[tb, MS], writes=[B("hsb")])
                    else:
                        ydst(t, tm, tb)

        def mk_norm_bufs(ss):
            xt = [ss(f"xt{i}", [P, 2, 514]) for i in range(2)]
            sq = [ss(f"sq{i}", [P, 514], BF16) for i in range(2)]
            rstd = ss("rstd", [P, 514])
            tmpf = [ss(f"tmpf{i}", [P, 514]) for i in range(3)]
            tc_ = [0]

            def tmp():
                i = tc_[0] % 3
                tc_[0] += 1
                return tmpf[i], B("tmpf", i)
            return (xt, sq, rstd, tmp)

        def mk_w(ss, KT, nb=4):
            wb_ = [ss(f"wb{i}", [P, KT * P], BF16) for i in range(nb)]
            wr = [0]

            def wload(n, l, ct, KT_):
                i = wr[0] % nb
                wr[0] += 1
                k.dma(SP, wb_[i][:, 0:KT_ * P], wtile(n, l, ct, KT_), reads=[B("wf", n, l)], writes=[B("wb", i)])
                return wb_[i], B("wb", i)
            return wload

        def mk_ob(ss):
            ob = [ss(f"ob{i}", [P, 512], BF16) for i in range(3)]
            oc = [0]

            def obuf():
                i = oc[0] % 3
                oc[0] += 1
                return ob[i], B("ob", i)
            return obuf

        cur, nxt = (xA, "xA"), (xB_, "xB")
        for l in range(L):
            last = (l == L - 1)
            X, XN = cur
            pvo = lambda name, j=0: pv[:, l, c.pv[name] + j:c.pv[name] + j + 1]
            with Scope() as ss:
                wk = mk_norm_bufs(ss)
                tmp = wk[3]
                hsb = ss("hsb", [P, NT, 514], BF16)
                wload = mk_w(ss, NT)
                obuf = mk_ob(ss)
                vt = [ss(f"vt{i}", [P, 4 * P], BF16) for i in range(2)]
                sg = ss("sg", [P, c.BT, 512])
                cos_sb = ss("cos_sb", [P, T]); sin_sb = ss("sin_sb", [P, T])
                k.dma(SP, cos_sb[:], cosT[:, :], writes=[B("rope")])
                k.dma(SP, sin_sb[:], sinT[:, :], writes=[B("rope")])
                RB = B("rope")
                xcnt = 0
                for ci, (t0, n) in enumerate(c.chunks):
                    v = 1 if ci == 0 else 0
                    norm_mod(ss, [xpiece(X, XN, ci, t0, t0 + n)], n, l, v, 1, 0, hsb, wk)
                    HB = B("hsb")
                    if l == 0 and ci == 0:
                        ck('n1')
                    order = list(range(c.o_qa, c.o_ba)) + [x_ for j in range(c.BT) for x_ in (c.o_bg + j, c.o_ba + j)] + list(range(c.o_qn, c.CTI))
                    for ct in order:
                        if ci == 0 and last and (ct < c.o_ka or c.o_ba <= ct < c.o_kn):
                            continue
                        w_, wbb = wload("win", l, ct, NT)
                        ck('w0')
                        ps_, pb = ps_next()
                        for kt in range(NT):
                            k.op(PE, lambda ps_=ps_, w_=w_, kt=kt: nc.tensor.matmul(ps_[:, 0:n], lhsT=w_[:, kt * P:(kt + 1) * P], rhs=hsb[:, kt, 0:n],
                                                                                  start=(kt == 0), stop=(kt == NT - 1)),
                                 reads=[wbb, HB], writes=[pb], pe_acc=(kt > 0), inc=(kt == NT - 1))
                        ck('mm0')
                        if ct < c.o_va:
                            isq = ct < c.o_ka
                            qf, qfb = tmp()
                            k.op(DV, lambda qf=qf, ps_=ps_: nc.vector.tensor_copy(out=qf[:, 0:n], in_=ps_[:, 0:n]), reads=[pb], writes=[qfb])
                            ck('h0')
                            s2, s2b = tmp()
                            k.op(DV, lambda s2=s2, qf=qf: nc.vector.tensor_tensor(out=s2[:, 0:n], in0=qf[:, 0:n], in1=qf[:, 0:n], op=ALU.mult), reads=[qfb], writes=[s2b])
                            ck('h1')
                            k.op(PE, lambda s2=s2: nc.tensor.matmul(PS_AUX[0][:, 0:n], lhsT=ones_f[:], rhs=s2[:, 0:n], start=True, stop=True),
                                 reads=[s2b, CB], writes=[PS_AUX[1]])
                            ck('h2')
                            k.op(DV, lambda s2=s2: nc.vector.tensor_scalar(out=s2[:, 0:n], in0=PS_AUX[0][:, 0:n], scalar1=1.0 / P, scalar2=EPS, op0=ALU.mult, op1=ALU.add),
                                 reads=[PS_AUX[1], s2b], writes=[s2b])
                            ck('h3')
                            k.op(AC, lambda s2=s2: nc.scalar.sqrt(out=s2[:, 0:n], in_=s2[:, 0:n]), reads=[s2b], writes=[s2b])
                            k.op(DV, lambda s2=s2: nc.vector.reciprocal(out=s2[:, 0:n], in_=s2[:, 0:n]), reads=[s2b], writes=[s2b])
                            ck('hn')
                            gain = qgs[:, l:l + 1] if isq else pvo("kg")
                            k.op(DV, lambda qf=qf, s2=s2, gain=gain: nc.vector.scalar_tensor_tensor(out=qf[:, 0:n], in0=qf[:, 0:n], scalar=gain, in1=s2[:, 0:n],
                                                                                                 op0=ALU.mult, op1=ALU.mult),
                                 reads=[qfb, s2b, B("qgs"), B("pv")], writes=[qfb])
                            k.op(PE, lambda qf=qf: nc.tensor.matmul(PS_AUX[0][:, 0:n], lhsT=rot_f[:], rhs=qf[:, 0:n], start=True, stop=True),
                                 reads=[qfb, CB], writes=[PS_AUX[1]])
                            k.op(DV, lambda s2=s2: nc.vector.tensor_tensor(out=s2[:, 0:n], in0=PS_AUX[0][:, 0:n], in1=sin_sb[:, t0:t0 + n], op=ALU.mult),
                                 reads=[PS_AUX[1], RB, s2b], writes=[s2b])
                            k.op(DV, lambda qf=qf: nc.vector.tensor_tensor(out=qf[:, 0:n], in0=qf[:, 0:n], in1=cos_sb[:, t0:t0 + n], op=ALU.mult),
                                 reads=[qfb, RB], writes=[qfb])
                            o_, obb = obuf()
                            k.op(DV, lambda o_=o_, qf=qf, s2=s2: nc.vector.tensor_tensor(out=o_[:, 0:n], in0=qf[:, 0:n], in1=s2[:, 0:n], op=ALU.add),
                                 reads=[qfb, s2b], writes=[obb])
                            ck('rope')
                            if isq:
                                k.dma(PO, qaT[ct * P:(ct + 1) * P, t0:t0 + n], o_[:, 0:n], reads=[obb], writes=[B("qa", ct, ci)])
                            else:
                                h_ = ct - c.o_ka
                                k.dma(PO, kaT[h_ * P:(h_ + 1) * P, t0:t0 + n], o_[:, 0:n], reads=[obb], writes=[B("ka", h_, ci)])
                                if ci > 0:
                                    k.dma(PO, kaS[h_ * P:(h_ + 1) * P, t0 - CTX:t0 - CTX + n], o_[:, 0:n], reads=[obb], writes=[B("kaS")])
                        elif ct < c.o_ba or ct >= c.o_vn:
                            isa = ct < c.o_ba
                            h_ = ct - (c.o_va if isa else c.o_vn)
                            o_, obb = obuf()
                            k.op(AC, lambda o_=o_, ps_=ps_: nc.scalar.copy(out=o_[:, 0:n], in_=ps_[:, 0:n]), reads=[pb], writes=[obb])
                            nb = n // P
                            for bi in range(nb):
                                k.op(PE, lambda o_=o_, bi=bi: nc.tensor.transpose(PST[0][:, bi * P:(bi + 1) * P], o_[:, bi * P:(bi + 1) * P], ident_b[:]),
                                     reads=[obb, CB], writes=[PST[1]], pe_acc=(bi > 0), inc=(bi == nb - 1))
                            vt_, vtb = vt[xcnt % 2], B("vt", xcnt % 2)
                            xcnt += 1
                            k.op(DV, lambda vt_=vt_: nc.vector.tensor_copy(out=vt_[:, 0:n], in_=PST[0][:, 0:n]), reads=[PST[1]], writes=[vtb])
                            dstT = va if isa else vn
                            v3 = lambda ap: ap.rearrange("(b p) d -> p b d", p=P)
                            s3 = lambda lo, hi: vt_[:, lo:hi].rearrange("p (b d) -> p b d", d=P)
                            k.dma(PO, v3(dstT[h_ * T + t0:h_ * T + t0 + n, :]), s3(0, n), reads=[vtb], writes=[B("va" if isa else "vn", h_, ci)])
                            if isa and ci > 0:
                                k.dma(PO, v3(vaS[h_ * SL + t0 - CTX:h_ * SL + t0 - CTX + n, :]), s3(0, n), reads=[vtb], writes=[B("vaS")])
                            if (not isa) and ci == 1:
                                k.dma(PO, v3(vnS[h_ * 512:h_ * 512 + 256, :]), s3(0, 256), reads=[vtb], writes=[B("vnS")])
                            if (not isa) and ci == c.NBLK:
                                k.dma(PO, v3(vnS[h_ * 512 + 256:h_ * 512 + 512, :]), s3(n - 256, n), reads=[vtb], writes=[B("vnS")])
                        elif ct < c.o_qn:
                            if ct >= c.o_bg:
                                j = ct - c.o_bg
                                k.op(AC, lambda ps_=ps_, j=j: nc.scalar.activation(out=sg[:, j, 0:n], in_=ps_[:, 0:n], func=ACT.Sigmoid), reads=[pb], writes=[B("sg", j)])
                            else:
                                j = ct - c.o_ba
                                z_, zb = tmp()
                                k.op(DV, lambda z_=z_, ps_=ps_, j=j: nc.vector.tensor_tensor(out=z_[:, 0:n], in0=ps_[:, 0:n], in1=sg[:, j, 0:n], op=ALU.mult),
                                     reads=[pb, B("sg", j)], writes=[zb])
                                zo = 15 if ci == 0 else (15 + CTX + 15 + 15 + t0 - CTX)
                                k.dma(PO, zp[j * P:(j + 1) * P, zo:zo + n], z_[:, 0:n], reads=[zb], writes=[B("zp", j, ci)])
                                if ci == 1:
                                    k.dma(PO, zS[j * P:(j + 1) * P, 0:16], z_[:, 0:16], reads=[zb], writes=[B("zS")])
                                if ci == c.NBLK:
                                    k.dma(PO, zS[j * P:(j + 1) * P, 16:32], z_[:, n - 16:n], reads=[zb], writes=[B("zS")])
                        else:
                            isq = ct < c.o_kn
                            h_ = ct - (c.o_qn if isq else c.o_kn)
                            o_, obb = obuf()
                            if isq:
                                k.op(AC, lambda o_=o_, ps_=ps_: nc.scalar.mul(out=o_[:, 0:n], in_=ps_[:, 0:n], mul=float(P) ** -0.5), reads=[pb], writes=[obb])
                                k.dma(PO, qnT[h_ * P:(h_ + 1) * P, t0:t0 + n], o_[:, 0:n], reads=[obb], writes=[B("qn", h_, ci)])
                            else:
                                k.op(AC, lambda o_=o_, ps_=ps_: nc.scalar.copy(out=o_[:, 0:n], in_=ps_[:, 0:n]), reads=[pb], writes=[obb])
                                k.dma(PO, knT[h_ * P:(h_ + 1) * P, t0:t0 + n], o_[:, 0:n], reads=[obb], writes=[B("kn", h_, ci)])
                                if ci == 1:
                                    k.dma(PO, knS[h_ * P:(h_ + 1) * P, 0:256], o_[:, 0:256], reads=[obb], writes=[B("knS")])
                                if ci == c.NBLK:
                                    k.dma(PO, knS[h_ * P:(h_ + 1) * P, 256:512], o_[:, n - 256:n], reads=[obb], writes=[B("knS")])
                        if l == 0 and ci == 0:
                            ck(f'ct{ct}')
            ck('AB')
            for h_ in range(c.AKV):
                k.allgather(PAIRS, kaS[h_ * P:(h_ + 1) * P, :], kaG[2 * h_ * P:2 * (h_ + 1) * P, :], reads=[B("kaS")], writes=[B("kaG")])
                k.allgather(PAIRS, vaS[h_ * SL:(h_ + 1) * SL, :], vaG[2 * h_ * SL:2 * (h_ + 1) * SL, :], reads=[B("vaS")], writes=[B("vaG")])
            for h_ in range(c.CH):
                k.allgather(PAIRS, knS[h_ * P:(h_ + 1) * P, :], knG[2 * h_ * P:2 * (h_ + 1) * P, :], reads=[B("knS")], writes=[B("knG")])
                k.allgather(PAIRS, vnS[h_ * 512:(h_ + 1) * 512, :], vnG[2 * h_ * 512:2 * (h_ + 1) * 512, :], reads=[B("vnS")], writes=[B("vnG")])
            k.allgather(PAIRS, zS[:, :], zG[:, :], reads=[B("zS")], writes=[B("zG")])
            zl = 15 + CTX + 15
            with Scope() as ss:
                halo_sb = ss("halo_sb", [P, 2, 16])
                for j in range(c.BT):
                    k.dma(SP, halo_sb[:, 0, :], zG[j * P:(j + 1) * P, 16:32], reads=[B("zG")], writes=[B("halo")])
                    k.dma(SP, halo_sb[:, 1, :], zG[c.BCH + j * P:c.BCH + (j + 1) * P, 0:16], reads=[B("zG")], writes=[B("halo")])
                    k.op(DV, lambda: nc.vector.tensor_scalar(out=halo_sb[:, 0, :], in0=halo_sb[:, 0, :], scalar1=flg[:, 4:5], scalar2=None, op0=ALU.mult),
                         reads=[B("halo"), CB], writes=[B("halo")])
                    k.op(DV, lambda: nc.vector.tensor_scalar(out=halo_sb[:, 1, :], in0=halo_sb[:, 1, :], scalar1=flg[:, 5:6], scalar2=None, op0=ALU.mult),
                         reads=[B("halo"), CB], writes=[B("halo")])
                    k.dma(PO, zp[j * P:(j + 1) * P, zl:zl + 15], halo_sb[:, 0, 1:16], reads=[B("halo")], writes=[B("zp", j, "h0")])
                    k.dma(PO, zp[j * P:(j + 1) * P, zl + 15 + SL:zl + 30 + SL], halo_sb[:, 1, 0:15], reads=[B("halo")], writes=[B("zp", j, "h1")])
            ck('C')
            KMAX = CTX + c.SEQ
            with Scope() as ss:
                kT_sb = ss("kT_sb", [P, KMAX], BF16)
                v_sb = ss("v_sb", [P, KMAX // P, P], BF16)
                q_sb = [ss(f"q_sb{i}", [P, 512], BF16) for i in range(2)]
                p_sb = [ss(f"p_sb{i}", [P, 512], BF16) for i in range(3)]
                bias_sb = ss("bias_sb", [P, 8, 512], BF16)
                mask_sb = ss("mask_sb", [P, c.NBLK, 8 * 512], BF16)
                rden = ss("rden", [P, 512])
                obuf = mk_ob(ss)
                for bi in range(c.NBLK):
                    k.dma(SP, mask_sb[:, bi, :], nmask[bi * P:(bi + 1) * P, :], writes=[B("mask_sb")])
                cn = [0, 0]

                def attend(qsrc, qbufs, nq, ktiles, dst_ap, dst_bufs, kb):
                    qs, qb = q_sb[cn[0] % 2], B("q_sb", cn[0] % 2)
                    cn[0] += 1
                    k.dma(SP, qs[:, 0:nq], qsrc, reads=qbufs, writes=[qb])
                    nk = len(ktiles)

                    def qk(i):
                        ko, vi, extra = ktiles[i]
                        ps_, pb = ps_next()
                        k.op(PE, lambda ps_=ps_, ko=ko: nc.tensor.matmul(ps_[:, 0:nq], lhsT=kT_sb[:, ko:ko + P], rhs=qs[:, 0:nq], start=True, stop=(extra is None)),
                             reads=[kb, qb], writes=[pb], inc=(extra is None))
                        if extra is not None:
                            bj, mblk, mj = extra
                            k.op(PE, lambda ps_=ps_, bj=bj: nc.tensor.matmul(ps_[:, 0:nq], lhsT=ident_b[:], rhs=bias_sb[:, bj, 0:nq], start=False, stop=False),
                                 reads=[B("bias_sb"), CB], writes=[pb], pe_acc=True, inc=False)
                            k.op(PE, lambda ps_=ps_, mblk=mblk, mj=mj: nc.tensor.matmul(ps_[:, 0:nq], lhsT=ident_b[:], rhs=mask_sb[:, mblk, mj * 512:mj * 512 + nq],
                                                                                      start=False, stop=True),
                                 reads=[B("mask_sb"), CB], writes=[pb], pe_acc=True)
                        return ps_, pb
                    cur_s = qk(0)
                    for i, (ko, vi, extra) in enumerate(ktiles):
                        nxt_s = qk(i + 1) if i + 1 < nk else None
                        ps_, pb = cur_s
                        pt, ptb = p_sb[cn[1] % 3], B("p_sb", cn[1] % 3)
                        cn[1] += 1
                        k.op(AC, lambda ps_=ps_, pt=pt: nc.scalar.activation(out=pt[:, 0:nq], in_=ps_[:, 0:nq], func=ACT.Exp), reads=[pb], writes=[ptb])
                        k.op(PE, lambda pt=pt, vi=vi, i=i: nc.tensor.matmul(PS_O[0][:, 0:nq], lhsT=v_sb[:, vi, :], rhs=pt[:, 0:nq], start=(i == 0), stop=(i == nk - 1)),
                             reads=[ptb, kb], writes=[PS_O[1]], pe_acc=(i > 0), inc=False)
                        k.op(PE, lambda pt=pt, i=i: nc.tensor.matmul(PS_DEN[0][:, 0:nq], lhsT=ones_b[:], rhs=pt[:, 0:nq], start=(i == 0), stop=(i == nk - 1)),
                             reads=[ptb, CB], writes=[PS_DEN[1]], pe_acc=(i > 0))
                        cur_s = nxt_s
                    k.op(DV, lambda: nc.vector.reciprocal(out=rden[:, 0:nq], in_=PS_DEN[0][:, 0:nq]), reads=[PS_DEN[1]], writes=[B("rden")])
                    o_, obb = obuf()
                    k.op(DV, lambda o_=o_: nc.vector.tensor_tensor(out=o_[:, 0:nq], in0=PS_O[0][:, 0:nq], in1=rden[:, 0:nq], op=ALU.mult),
                         reads=[PS_O[1], B("rden")], writes=[obb])
                    k.dma(PO, dst_ap, o_[:, 0:nq], reads=[obb], writes=dst_bufs)

                KB = B("kv_sb")
                v3 = lambda ap: ap.rearrange("(b p) d -> p b d", p=P)
                for kv in range(c.AKV):
                    k.dma(SP, kT_sb[:, 0:CTX], kaT[kv * P:(kv + 1) * P, 0:CTX], reads=[B("ka", kv, 0)], writes=[KB])
                    k.dma(SP, kT_sb[:, CTX:CTX + SL], kaG[(2 * kv) * P:(2 * kv + 1) * P, :], reads=[B("kaG")], writes=[KB])
                    k.dma(SP, kT_sb[:, CTX + SL:CTX + 2 * SL], kaG[(2 * kv + 1) * P:(2 * kv + 2) * P, :], reads=[B("kaG")], writes=[KB])
                    k.dma(SP, v_sb[:, 0:CTX // P, :], v3(va[kv * T:kv * T + CTX, :]), reads=[B("va", kv, 0)], writes=[KB])
                    for hf in range(2):
                        base = (2 * kv + hf) * SL
                        k.dma(SP, v_sb[:, (CTX + hf * SL) // P:(CTX + (hf + 1) * SL) // P, :], v3(vaG[base:base + SL, :]), reads=[B("vaG")], writes=[KB])
                    for g in range(3):
                        h_ = kv * 3 + g
                        for ci in lat:
                            t0, n = c.chunks[ci]
                            kts = [(i * P, i, None) for i in range(KMAX // P)]
                            attend(qaT[h_ * P:(h_ + 1) * P, t0:t0 + n], [B("qa", h_, ci)], n, kts,
                                   catT[h_ * P:(h_ + 1) * P, t0:t0 + n], [B("cat", h_, ci)], KB)
                        if not last:
                            kts = [(i * P, i, None) for i in range(CTX // P)]
                            attend(qaT[h_ * P:(h_ + 1) * P, 0:CTX], [B("qa", h_, 0)], CTX, kts,
                                   catT[h_ * P:(h_ + 1) * P, 0:CTX], [B("cat", h_, 0)], KB)
                for h_ in range(c.CH):
                    k.dma(SP, kT_sb[:, 0:CTX], knT[h_ * P:(h_ + 1) * P, 0:CTX], reads=[B("kn", h_, 0)], writes=[KB])
                    k.dma(SP, kT_sb[:, CTX:CTX + 256], knG[(2 * h_) * P:(2 * h_ + 1) * P, 256:512], reads=[B("knG")], writes=[KB])
                    k.dma(SP, kT_sb[:, CTX + 256:CTX + 256 + SL], knT[h_ * P:(h_ + 1) * P, CTX:T], reads=[B("kn", h_, ci) for ci in lat], writes=[KB])
                    k.dma(SP, kT_sb[:, CTX + 256 + SL:CTX + 512 + SL], knG[(2 * h_ + 1) * P:(2 * h_ + 2) * P, 0:256], reads=[B("knG")], writes=[KB])
                    k.dma(SP, v_sb[:, 0:2, :], v3(vn[h_ * T:h_ * T + CTX, :]), reads=[B("vn", h_, 0)], writes=[KB])
                    k.dma(SP, v_sb[:, 2:4, :], v3(vnG[(2 * h_) * 512 + 256:(2 * h_) * 512 + 512, :]), reads=[B("vnG")], writes=[KB])
                    k.dma(SP, v_sb[:, 4:4 + SL // P, :], v3(vn[h_ * T + CTX:h_ * T + T, :]), reads=[B("vn", h_, ci) for ci in lat], writes=[KB])
                    k.dma(SP, v_sb[:, 4 + SL // P:6 + SL // P, :], v3(vnG[(2 * h_ + 1) * 512:(2 * h_ + 1) * 512 + 256, :]), reads=[B("vnG")], writes=[KB])
                    tab = rpbT[(l * c.CH + h_) * 64:(l * c.CH + h_ + 1) * 64, :]
                    for j in range(8):
                        for b in range(2):
                            d0 = 2 * j + b
                            k.dma(PO, bias_sb[b * 64:(b + 1) * 64, j, :], tab[:, (15 - d0) * 64:(15 - d0 + 8) * 64], writes=[B("bias_sb")])
                    for bi, ci in enumerate(lat):
                        t0, n = c.chunks[ci]
                        kts = [(i * P, i, None) for i in range(2)]
                        for j in range(8):
                            eo = (8 * bi + 2 * j) * 64
                            kts.append((CTX + eo, 2 + eo // P, (j, bi, j)))
                        cr = c.AH + c.BT + h_
                        attend(qnT[h_ * P:(h_ + 1) * P, t0:t0 + n], [B("qn", h_, ci)], n, kts,
                               catT[cr * P:(cr + 1) * P, t0:t0 + n], [B("cat", cr, ci)], KB)
                    if not last:
                        kts = [(i * P, i, None) for i in range(2)]
                        cr = c.AH + c.BT + h_
                        attend(qnT[h_ * P:(h_ + 1) * P, 0:CTX], [B("qn", h_, 0)], CTX, kts,
                               catT[cr * P:(cr + 1) * P, 0:CTX], [B("cat", cr, 0)], KB)
            ck('D12')
            with Scope() as ss:
                zwin = [ss(f"zwin{i}", [P, 512 + 30]) for i in range(2)]
                yconv = ss("yconv", [P, c.BT, 512])
                ysq = ss("ysq", [P, 512]); mean = ss("mean", [P, 512]); var = ss("var", [P, 512])
                zz = ss("zz", [P, c.BT, 512], BF16)
                wload = mk_w(ss, c.BT)
                obuf = mk_ob(ss)
                for ci, (t0, n) in enumerate(c.chunks):
                    if ci == 0 and last:
                        continue
                    zo = 0 if ci == 0 else (15 + CTX + 15 + t0 - CTX)
                    for j in range(c.BT):
                        zi = (ci * c.BT + j) % 2
                        zw, zwb = zwin[zi], B("zwin", zi)
                        deps = [B("zp", j, "pad"), B("zp", j, "h0"), B("zp", j, "h1")] + [B("zp", j, cj) for cj in range(nch)]
                        k.dma(SP, zw[:, 0:n + 30], zp[j * P:(j + 1) * P, zo:zo + n + 30], reads=deps, writes=[zwb])
                        YB = B("yconv", j)
                        k.op(DV, lambda zw=zw, j=j: nc.vector.tensor_scalar(out=yconv[:, j, 0:n], in0=zw[:, 0:n], scalar1=pvo("bdw", j * 31), scalar2=pvo("bdb", j),
                                                                          op0=ALU.mult, op1=ALU.add), reads=[zwb, B("pv")], writes=[YB])
                        for tap in range(1, 31):
                            k.op(DV, lambda zw=zw, j=j, tap=tap: nc.vector.scalar_tensor_tensor(out=yconv[:, j, 0:n], in0=zw[:, tap:tap + n], scalar=pvo("bdw", j * 31 + tap),
                                                                                             in1=yconv[:, j, 0:n], op0=ALU.mult, op1=ALU.add),
                                 reads=[zwb, B("pv"), YB], writes=[YB])
                        k.op(PE, lambda j=j: nc.tensor.matmul(PS_AUX[0][:, 0:n], lhsT=ones_f[:], rhs=yconv[:, j, 0:n], start=(j == 0), stop=(j == c.BT - 1)),
                             reads=[YB, CB], writes=[PS_AUX[1]], pe_acc=(j > 0))
                        k.op(DV, lambda j=j: nc.vector.tensor_tensor(out=ysq[:, 0:n], in0=yconv[:, j, 0:n], in1=yconv[:, j, 0:n], op=ALU.mult), reads=[YB, B("ysq")], writes=[B("ysq")])
                        k.op(PE, lambda j=j: nc.tensor.matmul(PS_DEN[0][:, 0:n], lhsT=ones_f[:], rhs=ysq[:, 0:n], start=(j == 0), stop=(j == c.BT - 1)),
                             reads=[B("ysq"), CB], writes=[PS_DEN[1]], pe_acc=(j > 0))
                    k.op(DV, lambda: nc.vector.tensor_scalar(out=mean[:, 0:n], in0=PS_AUX[0][:, 0:n], scalar1=1.0 / c.BCH, scalar2=None, op0=ALU.mult),
                         reads=[PS_AUX[1]], writes=[B("mean")])
                    k.op(DV, lambda: nc.vector.tensor_scalar(out=var[:, 0:n], in0=PS_DEN[0][:, 0:n], scalar1=1.0 / c.BCH, scalar2=EPS, op0=ALU.mult, op1=ALU.add),
                         reads=[PS_DEN[1]], writes=[B("var")])
                    k.op(DV, lambda: nc.vector.tensor_tensor(out=ysq[:, 0:n], in0=mean[:, 0:n], in1=mean[:, 0:n], op=ALU.mult), reads=[B("mean"), B("ysq")], writes=[B("ysq")])
                    k.op(DV, lambda: nc.vector.tensor_tensor(out=var[:, 0:n], in0=var[:, 0:n], in1=ysq[:, 0:n], op=ALU.subtract), reads=[B("var"), B("ysq")], writes=[B("var")])
                    k.op(AC, lambda: nc.scalar.sqrt(out=var[:, 0:n], in_=var[:, 0:n]), reads=[B("var")], writes=[B("var")])
                    k.op(DV, lambda: nc.vector.reciprocal(out=var[:, 0:n], in_=var[:, 0:n]), reads=[B("var")], writes=[B("var")])
                    for j in range(c.BT):
                        YB = B("yconv", j)
                        k.op(DV, lambda j=j: nc.vector.tensor_tensor(out=yconv[:, j, 0:n], in0=yconv[:, j, 0:n], in1=mean[:, 0:n], op=ALU.subtract),
                             reads=[YB, B("mean")], writes=[YB])
                        k.op(DV, lambda j=j: nc.vector.tensor_tensor(out=yconv[:, j, 0:n], in0=yconv[:, j, 0:n], in1=var[:, 0:n], op=ALU.mult),
                             reads=[YB, B("var")], writes=[YB])
                        k.op(DV, lambda j=j: nc.vector.tensor_scalar(out=yconv[:, j, 0:n], in0=yconv[:, j, 0:n], scalar1=pvo("blg", j), scalar2=pvo("blb", j),
                                                                   op0=ALU.mult, op1=ALU.add), reads=[YB, B("pv")], writes=[YB])
                        k.op(AC, lambda j=j: nc.scalar.activation(out=zz[:, j, 0:n], in_=yconv[:, j, 0:n], func=ACT.Silu), reads=[YB, B("zz")], writes=[B("zz")])
                    for j in range(c.BT):
                        w_, wbb = wload("wpw", l, j, c.BT)
                        ps_, pb = ps_next()
                        for kt in range(c.BT):
                            k.op(PE, lambda ps_=ps_, w_=w_, kt=kt: nc.tensor.matmul(ps_[:, 0:n], lhsT=w_[:, kt * P:(kt + 1) * P], rhs=zz[:, kt, 0:n],
                                                                                  start=(kt == 0), stop=(kt == c.BT - 1)),
                                 reads=[wbb, B("zz")], writes=[pb], pe_acc=(kt > 0), inc=(kt == c.BT - 1))
                        o_, obb = obuf()
                        k.op(AC, lambda o_=o_, ps_=ps_, j=j: nc.scalar.activation(out=o_[:, 0:n], in_=ps_[:, 0:n], func=ACT.Identity, bias=pvo("bpb", j), scale=1.0),
                             reads=[pb, B("pv")], writes=[obb])
                        k.dma(PO, catT[(c.AH + j) * P:(c.AH + j + 1) * P, t0:t0 + n], o_[:, 0:n], reads=[obb], writes=[B("cat", c.AH + j, ci)])
            ck('D3')
            with Scope() as ss:
                cat_sb = ss("cat_sb", [P, NT, 512], BF16)
                wload = mk_w(ss, NT)
                xres = [ss(f"xres{i}", [P, 512]) for i in range(2)]
                xnew = [ss(f"xnew{i}", [P, 512]) for i in range(2)]
                xe_sb = ss("xe_sb", [P, 2 * NT])
                for ci, (t0, n) in enumerate(c.chunks):
                    if ci == 0 and last:
                        continue
                    v = 1 if ci == 0 else 0
                    k.dma(SP, cat_sb[:, :, 0:n], catT[:, t0:t0 + n].rearrange("(t p) n -> p t n", p=P), reads=[B("cat", t, ci) for t in range(NT)], writes=[B("cat_sb")])
                    for ct in range(NT):
                        w_, wbb = wload("wout", l, ct, NT)
                        ps_, pb = ps_next()
                        for kt in range(NT):
                            k.op(PE, lambda ps_=ps_, w_=w_, kt=kt: nc.tensor.matmul(ps_[:, 0:n], lhsT=w_[:, kt * P:(kt + 1) * P], rhs=cat_sb[:, kt, 0:n],
                                                                                  start=(kt == 0), stop=(kt == NT - 1)),
                                 reads=[wbb, B("cat_sb")], writes=[pb], pe_acc=(kt > 0), inc=(kt == NT - 1))
                        xr, xrb = xres[ct % 2], B("xres", ct % 2)
                        k.dma(SP, xr[:, 0:n], X[ct * P:(ct + 1) * P, t0:t0 + n], reads=[B(XN, ci, ct)], writes=[xrb])
                        xn, xnb = xnew[ct % 2], B("xnew", ct % 2)
                        k.op(DV, lambda xn=xn, ps_=ps_, xr=xr, ct=ct: nc.vector.scalar_tensor_tensor(out=xn[:, 0:n], in0=ps_[:, 0:n], scalar=mod(l, v, 2, ct), in1=xr[:, 0:n],
                                                                                                  op0=ALU.mult, op1=ALU.add), reads=[pb, xrb, MS], writes=[xnb])
                        k.dma(PO, X[ct * P:(ct + 1) * P, t0:t0 + n], xn[:, 0:n], reads=[xnb], writes=[B(XN, ci, ct)])
                        if ci == 1:
                            k.op(DV, lambda xn=xn, ct=ct: nc.vector.tensor_copy(out=xe_sb[:, 2 * ct:2 * ct + 1], in_=xn[:, 0:1]), reads=[xnb, B("xe_sb")], writes=[B("xe_sb")])
                        if ci == c.NBLK:
                            k.op(DV, lambda xn=xn, ct=ct: nc.vector.tensor_copy(out=xe_sb[:, 2 * ct + 1:2 * ct + 2], in_=xn[:, n - 1:n]), reads=[xnb, B("xe_sb")], writes=[B("xe_sb")])
                k.dma(PO, xeS[:, :], xe_sb[:], reads=[B("xe_sb")], writes=[B("xeS")])
            k.allgather(PAIRS, xeS[:, :], xeG[:, :], reads=[B("xeS")], writes=[B("xeG")])
            ck('E')
            Y, YN = nxt
            with Scope() as ss:
                wk = mk_norm_bufs(ss)
                tmp = wk[3]
                hsb = ss("hsb", [P, NT, 514], BF16)
                wload = mk_w(ss, max(NT, c.FT), nb=3)
                U = [ss(f"U{i}", [P, 514]) for i in range(2)]
                act_sb = ss("act_sb", [P, c.FT, 512], BF16)
                gsil = ss("gsil", [P, 512])
                xres = [ss(f"xres{i}", [P, 512]) for i in range(2)]
                xnew = [ss(f"xnew{i}", [P, 512]) for i in range(2)]

                def xepiece(r, e):
                    return (lambda gi, G: xeG[r * P:(r + 1) * P, :].rearrange("p (t e) -> p t e", e=2)[:, gi:gi + G, e:e + 1],
                            lambda gi, G: [B("xeG")], 1)
                for ci, (t0, n) in enumerate(c.chunks):
                    if ci == 0 and last:
                        continue
                    v = 1 if ci == 0 else 0
                    W = n + 2
                    first_lat, last_lat = ci == 1, ci == c.NBLK
                    if ci == 0:
                        left, right = xpiece(X, XN, ci, t0, t0 + 1), xpiece(X, XN, ci, t0 + n - 1, t0 + n)
                    else:
                        left = xepiece(0, 1) if first_lat else xpiece(X, XN, ci - 1, t0 - 1, t0)
                        right = xepiece(1, 0) if last_lat else xpiece(X, XN, ci + 1, t0 + n, t0 + n + 1)
                    norm_mod(ss, [left, xpiece(X, XN, ci, t0, t0 + n), right], W, l, v, 4, 3, hsb, wk)
                    HB = B("hsb")
                    hw = W // 2
                    for ft in range(c.FT):
                        for part in range(2):
                            ct = ft + part * c.FT
                            w_, wbb = wload("wup", l, ct, NT)
                            pA, pAb = ps_next()
                            pB, pBb = ps_next()
                            for (pp, ppb, a) in [(pA, pAb, 0), (pB, pBb, hw)]:
                                for kt in range(NT):
                                    k.op(PE, lambda pp=pp, w_=w_, kt=kt, a=a: nc.tensor.matmul(pp[:, 0:hw], lhsT=w_[:, kt * P:(kt + 1) * P], rhs=hsb[:, kt, a:a + hw],
                                                                                             start=(kt == 0), stop=(kt == NT - 1)),
                                         reads=[wbb, HB], writes=[ppb], pe_acc=(kt > 0), inc=(kt == NT - 1))
                            u_, ub = U[part], B("U", part)
                            k.op(AC, lambda u_=u_, pA=pA: nc.scalar.copy(out=u_[:, 0:hw], in_=pA[:, 0:hw]), reads=[pAb], writes=[ub])
                            k.op(AC, lambda u_=u_, pB=pB: nc.scalar.copy(out=u_[:, hw:W], in_=pB[:, 0:hw]), reads=[pBb, ub], writes=[ub])
                            for (colx, isl) in [(0, True), (W - 1, False)]:
                                if ci == 0:
                                    k.op(DV, lambda u_=u_, colx=colx: nc.vector.memset(u_[:, colx:colx + 1], 0.0), reads=[ub], writes=[ub])
                                elif (isl and first_lat) or ((not isl) and last_lat):
                                    fcol = 4 if isl else 5
                                    k.op(DV, lambda u_=u_, colx=colx, fcol=fcol: nc.vector.tensor_scalar(out=u_[:, colx:colx + 1], in0=u_[:, colx:colx + 1],
                                                                                                       scalar1=flg[:, fcol:fcol + 1], scalar2=None, op0=ALU.mult),
                                         reads=[ub, CB], writes=[ub])
                            cv, cvb = tmp()
                            k.op(DV, lambda cv=cv, u_=u_, ct=ct: nc.vector.tensor_scalar(out=cv[:, 0:n], in0=u_[:, 1:n + 1], scalar1=pvo("fdw", ct * 3 + 1), scalar2=pvo("fdb", ct),
                                                                                      op0=ALU.mult, op1=ALU.add), reads=[ub, B("pv")], writes=[cvb])
                            k.op(DV, lambda cv=cv, u_=u_, ct=ct: nc.vector.scalar_tensor_tensor(out=cv[:, 0:n], in0=u_[:, 0:n], scalar=pvo("fdw", ct * 3), in1=cv[:, 0:n],
                                                                                             op0=ALU.mult, op1=ALU.add), reads=[ub, B("pv"), cvb], writes=[cvb])
                            k.op(DV, lambda cv=cv, u_=u_, ct=ct: nc.vector.scalar_tensor_tensor(out=cv[:, 0:n], in0=u_[:, 2:n + 2], scalar=pvo("fdw", ct * 3 + 2), in1=cv[:, 0:n],
                                                                                             op0=ALU.mult, op1=ALU.add), reads=[ub, B("pv"), cvb], writes=[cvb])
                            if part == 0:
                                k.op(AC, lambda cv=cv: nc.scalar.activation(out=gsil[:, 0:n], in_=cv[:, 0:n], func=ACT.Silu), reads=[cvb], writes=[B("gsil")])
                            else:
                                k.op(DV, lambda cv=cv, ft=ft: nc.vector.tensor_tensor(out=act_sb[:, ft, 0:n], in0=cv[:, 0:n], in1=gsil[:, 0:n], op=ALU.mult),
                                     reads=[cvb, B("gsil"), B("act_sb")], writes=[B("act_sb")])
                    for ct in range(NT):
                        w_, wbb = wload("wdn", l, ct, c.FT)
                        ps_, pb = ps_next()
                        for kt in range(c.FT):
                            k.op(PE, lambda ps_=ps_, w_=w_, kt=kt: nc.tensor.matmul(ps_[:, 0:n], lhsT=w_[:, kt * P:(kt + 1) * P], rhs=act_sb[:, kt, 0:n],
                                                                                  start=(kt == 0), stop=(kt == c.FT - 1)),
                                 reads=[wbb, B("act_sb")], writes=[pb], pe_acc=(kt > 0), inc=(kt == c.FT - 1))
                        xr, xrb = xres[ct % 2], B("xres", ct % 2)
                        k.dma(SP, xr[:, 0:n], X[ct * P:(ct + 1) * P, t0:t0 + n], reads=[B(XN, ci, ct)], writes=[xrb])
                        xn, xnb = xnew[ct % 2], B("xnew", ct % 2)
                        k.op(DV, lambda xn=xn, ps_=ps_, xr=xr, ct=ct: nc.vector.scalar_tensor_tensor(out=xn[:, 0:n], in0=ps_[:, 0:n], scalar=mod(l, v, 5, ct), in1=xr[:, 0:n],
                                                                                                  op0=ALU.mult, op1=ALU.add), reads=[pb, xrb, MS], writes=[xnb])
                        k.dma(PO, Y[ct * P:(ct + 1) * P, t0:t0 + n], xn[:, 0:n], reads=[xnb], writes=[B(YN, ci, ct)])
            cur, nxt = nxt, cur

        X, XN = cur
        with Scope() as ss:
            wk = mk_norm_bufs(ss)
            tmp = wk[3]
            for ci in lat:
                t0, n = c.chunks[ci]

                def ydst(t, tm, tb, t0=t0, n=n):
                    o_, ob_ = tmp()
                    k.op(DV, lambda: nc.vector.tensor_scalar(out=o_[:, 0:n], in0=tm[:, 0:n], scalar1=fin_g[:, t:t + 1], scalar2=None, op0=ALU.mult),
                         reads=[tb, CB], writes=[ob_])
                    k.dma(PO, yout[t * P:(t + 1) * P, t0 - CTX:t0 - CTX + n], o_[:, 0:n], reads=[ob_], writes=[B("yout")])
                norm_mod(ss, [xpiece(X, XN, ci, t0, t0 + n)], n, 0, 0, 0, 0, None, wk, ydst=ydst)
        k.barrier()
    return nc


def _fm(vec, nt):
    return np.ascontiguousarray(vec.reshape(nt, P).T)


def _tile_major(w):
    K_, N_ = w.shape
    return np.ascontiguousarray(w.reshape(K_ // P, P, N_ // P, P).transpose(2, 1, 0, 3)).reshape(-1)


def rope_tables(cfg, half):
    c = cfg
    t = np.arange(c.SL, dtype=np.int64) + half * c.SL
    row = (t // c.GW).astype(np.float32)
    col = (t % c.GW).astype(np.float32)
    hh = P // 2
    inv = (np.float32(10000.0) ** (-np.arange(0, hh, 2, dtype=np.float32) / np.float32(hh))).astype(np.float32)
    ang = np.concatenate([row[:, None] * inv, col[:, None] * inv], axis=-1).astype(np.float32)
    cs, sn = np.cos(ang).astype(np.float32), np.sin(ang).astype(np.float32)
    C = np.ones((P, c.T), np.float32)
    S = np.zeros((P, c.T), np.float32)
    C[:, c.CTX:] = np.repeat(cs.T, 2, axis=0)
    S[:, c.CTX:] = np.repeat(sn.T, 2, axis=0)
    return C, S


def nbr_mask(cfg, half):
    c = cfg
    base = half * c.RL
    m = np.full((c.NBLK, 2, 64, 8, 8, 64), NEG, np.float32)
    wq = np.arange(64)
    cstart = np.clip(wq - 8, 0, 64 - 16)
    wk = np.arange(64)
    colok = (wk[:, None] >= cstart[None, :]) & (wk[:, None] < cstart[None, :] + 16)
    for bi in range(c.NBLK):
        for a in range(8):
            r = base + 8 * bi + a
            r0 = int(np.clip(r - 4, 0, c.ROWS - 8))
            for j in range(8):
                for b in range(2):
                    rk = base + 8 * bi + 2 * j + b - 4
                    if r0 <= rk < r0 + 8:
                        m[bi, b, :, j, a, :] = np.where(colok, 0.0, NEG)
    return m.reshape(c.NBLK * P, 8 * 512).astype(ml_dtypes.bfloat16)


def prep_inputs(cfg, inp):
    c = cfg
    L, D, NT = c.L, c.D, c.NT
    f = lambda a: np.asarray(a, dtype=np.float32)
    x, cc, ctx, c_ctx = f(inp["x"]), f(inp["c"]), f(inp["ctx"]), f(inp["c_ctx"])
    consts = np.zeros((P, 3 * P), np.float32)
    consts[:, 0:P] = np.eye(P, dtype=np.float32)
    for i in range(P // 2):
        consts[2 * i + 1, P + 2 * i] = -1.0
        consts[2 * i, P + 2 * i + 1] = 1.0
    consts[:, 2 * P:] = 1.0
    c5 = np.concatenate([cc, c_ctx[None]], 0)
    c5T = np.ascontiguousarray(c5.reshape(5, NT, P).transpose(2, 1, 0)).reshape(P, NT * 5)
    pvec = np.zeros((L, P, c.NP), np.float32)
    for l in range(L):
        pvec[l, :, c.pv["n1g"]:c.pv["n1g"] + NT] = _fm(f(inp["norm1_g"])[l], NT)
        pvec[l, :, c.pv["n2g"]:c.pv["n2g"] + NT] = _fm(f(inp["norm2_g"])[l], NT)
        pvec[l, :, c.pv["qg"]] = f(inp["a_qn_g"])[l]
        pvec[l, :, c.pv["kg"]] = f(inp["a_kn_g"])[l]
        bd = f(inp["b_dw_w"])[l]
        pvec[l, :, c.pv["bdw"]:c.pv["bdw"] + c.BT * 31] = bd.reshape(31, c.BT, P).transpose(2, 1, 0).reshape(P, c.BT * 31)
        for nm, key in [("bdb", "b_dw_b"), ("blg", "b_ln_g"), ("blb", "b_ln_b"), ("bpb", "b_pw_b")]:
            pvec[l, :, c.pv[nm]:c.pv[nm] + c.BT] = _fm(f(inp[key])[l], c.BT)
        fd = f(inp["ffn_dw_w"])[l]
        pvec[l, :, c.pv["fdw"]:c.pv["fdw"] + 2 * c.FT * 3] = fd.reshape(3, 2 * c.FT, P).transpose(2, 1, 0).reshape(P, 2 * c.FT * 3)
        pvec[l, :, c.pv["fdb"]:c.pv["fdb"] + 2 * c.FT] = _fm(f(inp["ffn_dw_b"])[l], 2 * c.FT)
    pvec = pvec.reshape(L * P, c.NP)
    fing = _fm(f(inp["final_g"]), NT)
    rpb = f(inp["c_rpb"])
    wk, wq = np.arange(64)[:, None], np.arange(64)[None, :]
    cidx = np.clip(wk - wq + 15, 0, 30)
    tab = np.zeros((L, c.CH, 64, 23, 64), np.float32)
    for dp in range(23):
        dl = 11 - dp
        if -7 <= dl <= 7:
            tab[:, :, :, dp, :] = rpb[:, :, dl + 7, :][:, :, cidx]
    rpbT = tab.reshape(L * c.CH * 64, 23 * 64)
    wflat = {}
    for n, key in [("win", "w_in"), ("wout", "w_out"), ("wup", "ffn_w_up"), ("wdn", "ffn_w_down"), ("wpw", "b_pw_w")]:
        w = f(inp[key])
        rows = w.shape[1] * w.shape[2] // 8 // 1024
        blk = wblk(rows)
        wflat[n] = np.stack([_tile_major(w[l]).reshape(rows // blk, 8, blk * 1024).transpose(1, 0, 2).reshape(8, -1) for l in range(L)], 0)
    ada_w, ada_b = f(inp["ada_w"]), f(inp["ada_b"])
    maps = []
    for r in range(8):
        b, half = r // 2, r % 2
        m = {}
        xT = np.empty((D, c.T), np.float32)
        xT[:, :c.CTX] = ctx[b].T
        xT[:, c.CTX:] = x[b, half * c.SL:(half + 1) * c.SL].T
        m["xin"] = xT
        m["c5T"] = c5T
        m["ada_s"] = np.ascontiguousarray(ada_w[:, :, r * c.MC:(r + 1) * c.MC]).reshape(L * D, c.MC)
        m["adab5"] = np.ascontiguousarray(np.broadcast_to(ada_b[None, :, r * c.MC:(r + 1) * c.MC], (5, L, c.MC))).reshape(5, L * c.MC)
        fl = np.zeros((P, 8), np.float32)
        fl[:, b] = 1.0
        fl[:, 4] = 1.0 if half == 1 else 0.0
        fl[:, 5] = 1.0 if half == 0 else 0.0
        m["flags"] = fl
        m["pvec"] = pvec
        m["fing"] = fing
        C_, S_ = rope_tables(c, half)
        m["cosT"], m["sinT"] = C_, S_
        m["consts"] = consts
        m["nmask"] = nbr_mask(c, half)
        m["rpbT"] = rpbT
        for n in wflat:
            m[n + "_s"] = np.ascontiguousarray(wflat[n][:, r, :]).reshape(-1, 1024)
        maps.append(m)
    return maps


_NC_CACHE = {}


def run(cfg, inp):
    key = (cfg.D, cfg.SEQ, cfg.L)
    if key not in _NC_CACHE:
        _NC_CACHE[key] = build(cfg)
    nc = _NC_CACHE[key]
    maps = prep_inputs(cfg, inp)
    res = run_bass_kernel_spmd(nc, maps, core_ids=list(range(8)))
    out = np.empty((cfg.B, cfg.SEQ, cfg.D), np.float32)
    for r in range(8):
        b, half = r // 2, r % 2
        out[b, half * cfg.SL:(half + 1) * cfg.SL, :] = res.results[r]["yout"].T
    return out


def kernel(**inputs):
    return run(Cfg(), inputs)
```

```python
import numpy as np
import ml_dtypes
from contextlib import ExitStack
import concourse.bass as bass
import concourse.mybir as mybir
from concourse.bass_utils import run_bass_kernel_spmd

F32 = mybir.dt.float32
BF16 = mybir.dt.bfloat16
ACT = mybir.ActivationFunctionType
ALU = mybir.AluOpType
NEG = -1e30
EPS = 1e-6
P = 128


class _Stop(Exception):
    pass


class Cfg:
    stop = None

    def __init__(s, D=4096, SEQ=4096, L=4):
        s.D, s.SEQ, s.L = D, SEQ, L
        s.B, s.CTX, s.GW = 4, 256, 64
        s.NT = D // P
        NH = D // P
        s.AH = 3 * NH // 8
        s.AKV = s.AH // 3
        s.CH = 3 * NH // 8
        s.BCH = D - (s.AH + s.CH) * P
        s.BT = s.BCH // P
        s.DFF = 11 * D // 8
        s.FT = s.DFF // P
        s.INW = s.AH * P + 2 * s.AKV * P + 2 * s.BCH + 3 * s.CH * P
        s.CTI = s.INW // P
        s.SL = SEQ // 2
        s.T = s.CTX + s.SL
        s.ROWS = SEQ // s.GW
        s.RL = s.SL // s.GW
        s.NBLK = s.SL // 512
        s.MC = 6 * D // 8
        s.o_qa = 0
        s.o_ka = s.o_qa + s.AH
        s.o_va = s.o_ka + s.AKV
        s.o_ba = s.o_va + s.AKV
        s.o_bg = s.o_ba + s.BT
        s.o_qn = s.o_bg + s.BT
        s.o_kn = s.o_qn + s.CH
        s.o_vn = s.o_kn + s.CH
        o = 0
        s.pv = {}
        for name, n in [("n1g", s.NT), ("n2g", s.NT), ("qg", 1), ("kg", 1), ("bdw", s.BT * 31), ("bdb", s.BT),
                        ("blg", s.BT), ("blb", s.BT), ("bpb", s.BT), ("fdw", 2 * s.FT * 3), ("fdb", 2 * s.FT)]:
            s.pv[name] = o
            o += n
        s.NP = o
        s.chunks = [(0, s.CTX)] + [(s.CTX + 512 * i, 512) for i in range(s.NBLK)]


class Sem:
    def __init__(s, h):
        s.h, s.count = h, 0


class Buf:
    def __init__(s, name):
        s.name, s.w, s.r = name, None, []


class Eng:
    def __init__(s, e, sems):
        s.e, s.sems, s.si, s.seen = e, sems, 0, {}
        s.sem = sems[0]

    def rotate(s):
        if s.sem.count >= 30000:
            s.si += 1
            s.sem = s.sems[s.si]


class K:
    def __init__(s, nc, stack, n_eng_sems=10, n_dma_sems=20):
        s.nc = nc
        mk = lambda nm: Sem(stack.enter_context(nc.semaphore(nm)))
        s.pe = Eng(nc.tensor, [mk(f"pe{i}") for i in range(5)])
        s.act = Eng(nc.scalar, [mk(f"ac{i}") for i in range(4)])
        s.dve = Eng(nc.vector, [mk(f"dv{i}") for i in range(6)])
        s.pool = Eng(nc.gpsimd, [mk(f"po{i}") for i in range(2)])
        s.sp = Eng(nc.sync, [mk("spx")])
        s.dsem = {id(s.sp): [mk(f"ds{i}") for i in range(16)],
                  id(s.pool): [mk(f"dp{i}") for i in range(28)]}
        s.dsi = {k: 0 for k in s.dsem}
        s.ccsem = [mk(f"cc{i}") for i in range(4)]
        s.cci = 0
        s.bufs = {}

    def B(s, *key):
        if key not in s.bufs:
            s.bufs[key] = Buf(str(key))
        return s.bufs[key]

    def _wait(s, eng, reads, writes, pe_acc=False):
        deps = {}

        def add(tok):
            if tok is None:
                return
            sem, val = tok
            if deps.get(id(sem), (None, 0))[1] < val:
                deps[id(sem)] = (sem, val)
        for b in reads:
            add(b.w)
        for b in writes:
            if not (pe_acc and b.w is not None and b.w[0] in eng.sems):
                add(b.w)
            for t in b.r:
                add(t)
        for sem, val in deps.values():
            if pe_acc and sem in eng.sems:
                continue
            if eng.seen.get(id(sem), 0) < val:
                eng.e.wait_ge(sem.h, val)
                eng.seen[id(sem)] = val

    def _done(s, tok, reads, writes):
        for b in reads:
            b.r.append(tok)
            if len(b.r) > 64:
                b.r = b.r[-64:] if False else b.r
        for b in writes:
            b.w, b.r = tok, []

    def op(s, eng, fn, reads=(), writes=(), inc=True, pe_acc=False):
        s._wait(eng, reads, writes, pe_acc)
        ins = fn()
        if inc:
            eng.sem.count += 1
            ins.then_inc(eng.sem.h, 1)
            tok = (eng.sem, eng.sem.count)
            s._done(tok, reads, writes)
            eng.rotate()
        else:
            tok = (eng.sem, eng.sem.count + 1)
            s._done(tok, reads, writes)
        return ins

    def dma(s, eng, out, in_, reads=(), writes=(), slow=False):
        s._wait(eng, reads, writes)
        pool = s.dsem[id(eng)]
        sem = pool[s.dsi[id(eng)] % len(pool)]
        s.dsi[id(eng)] += 1
        if sem.count and eng.seen.get(id(sem), 0) < sem.count:
            eng.e.wait_ge(sem.h, sem.count)
            eng.seen[id(sem)] = sem.count
        ins = eng.e.dma_start(out=out, in_=in_, allow_slow_non_contiguous=True) if slow else eng.e.dma_start(out=out, in_=in_)
        sem.count += 16
        ins.then_inc(sem.h, 16)
        s._done((sem, sem.count), reads, writes)

    def allgather(s, groups, in_ap, out_ap, reads=(), writes=()):
        eng = s.pool
        s._wait(eng, reads, writes)
        sem = s.ccsem[0]
        ins = eng.e.collective_compute("AllGather", ALU.bypass, replica_groups=groups,
                                       ins=[in_ap.opt()], outs=[out_ap.opt()])
        sem.count += 1
        ins.then_inc(sem.h)
        s._done((sem, sem.count), reads, writes)

    def wait_all(s, eng, bufs):
        s._wait(eng, bufs, ())

    def barrier(s):
        sems = []
        for e in (s.pe, s.act, s.dve, s.pool, s.sp):
            sems += e.sems
        for v in s.dsem.values():
            sems += v
        sems += s.ccsem
        for e in (s.pe, s.act, s.dve, s.pool, s.sp):
            for sem in sems:
                if sem.count and e.seen.get(id(sem), 0) < sem.count:
                    e.e.wait_ge(sem.h, sem.count)
                    e.seen[id(sem)] = sem.count


PAIRS = [[0, 1], [2, 3], [4, 5], [6, 7]]
ALL8 = [list(range(8))]
QUADS = [[0, 1, 2, 3], [4, 5, 6, 7]]
P4 = [[0, 4], [1, 5], [2, 6], [3, 7]]


def wblk(rows):
    b = min(128, rows)
    while rows % b:
        b -= 1
    return b


def build(cfg):
    try:
        return _build(cfg)
    except _Stop as e:
        return e.args[0]


def _build(cfg):
    c = cfg
    D, NT, T, SL, CTX, L = c.D, c.NT, c.T, c.SL, c.CTX, c.L
    nc = bass.Bass("TRN2", target_bir_lowering=False)
    din = lambda n, sh, dt=F32: nc.dram_tensor(n, sh, dt, kind="ExternalInput")
    dsc = lambda n, sh, dt=F32: nc.dram_tensor(n, sh, dt)
    xin = din("xin", [D, T])
    c5T = din("c5T", [P, NT * 5])
    ada_s = din("ada_s", [L * D, c.MC])
    adab5 = din("adab5", [5, L * c.MC])
    flags = din("flags", [P, 8])
    pvec = din("pvec", [L * P, c.NP])
    fing = din("fing", [P, NT])
    cosT = din("cosT", [P, T])
    sinT = din("sinT", [P, T])
    consts = din("consts", [P, 3 * P])
    nmask = din("nmask", [c.NBLK * P, 8 * 512], BF16)
    rpbT = din("rpbT", [L * c.CH * 64, 23 * 64])
    wspec = {"win": (D, c.INW), "wout": (D, D), "wup": (D, 2 * c.DFF), "wdn": (c.DFF, D), "wpw": (c.BCH, c.BCH)}
    wsh, wbf, wfull, wquad = {}, {}, {}, {}
    for n, (kk, nn) in wspec.items():
        rows = kk * nn // 8 // 1024
        wsh[n] = din(n + "_s", [L * rows, 1024])
        wbf[n] = [dsc(f"{n}_b{l}", [rows, 1024], BF16) for l in range(L)]
        wfull[n] = [dsc(f"{n}_f{l}", [8 * rows, 1024], BF16) for l in range(L)]
        wquad[n] = [dsc(f"{n}_q{l}", [4 * rows, 1024], BF16) for l in range(L)]
    yout = nc.dram_tensor("yout", [D, SL], F32, kind="ExternalOutput")
    xA = dsc("xA", [D, T]); xB_ = dsc("xB", [D, T])
    qaT = dsc("qaT", [c.AH * P, T], BF16)
    kaT = dsc("kaT", [c.AKV * P, T], BF16)
    va = dsc("va", [c.AKV * T, P], BF16)
    qnT = dsc("qnT", [c.CH * P, T], BF16)
    knT = dsc("knT", [c.CH * P, T], BF16)
    vn = dsc("vn", [c.CH * T, P], BF16)
    ZW = 15 + CTX + 15 + 15 + SL + 15
    zp = dsc("zp", [c.BCH, ZW])
    catT = dsc("catT", [D, T], BF16)
    kaS = dsc("kaS", [c.AKV * P, SL], BF16); kaG = dsc("kaG", [2 * c.AKV * P, SL], BF16)
    vaS = dsc("vaS", [c.AKV * SL, P], BF16); vaG = dsc("vaG", [2 * c.AKV * SL, P], BF16)
    knS = dsc("knS", [c.CH * P, 512], BF16); knG = dsc("knG", [2 * c.CH * P, 512], BF16)
    vnS = dsc("vnS", [c.CH * 512, P], BF16); vnG = dsc("vnG", [2 * c.CH * 512, P], BF16)
    zS = dsc("zS", [c.BCH, 32]); zG = dsc("zG", [2 * c.BCH, 32])
    xeS = dsc("xeS", [P, 2 * NT]); xeG = dsc("xeG", [2 * P, 2 * NT])
    modS = dsc("modS", [5, L * c.MC]); modG = dsc("modG", [8 * 5, L * c.MC]); modQ = dsc("modQ", [4 * 5, L * c.MC])

    with ExitStack() as st:
        k = K(nc, st)
        PE, AC, DV, PO, SP = k.pe, k.act, k.dve, k.pool, k.sp
        B = k.B

        uid = [0]

        class Scope:
            def __enter__(s):
                s.st = ExitStack()
                s.st.__enter__()
                uid[0] += 1
                u = uid[0]
                return lambda n, sh, dt=F32: s.st.enter_context(nc.sbuf_tensor(f"{n}_u{u}", sh, dt))

            def __exit__(s, *a):
                k.barrier()
                return s.st.__exit__(*a)

        sb = lambda n, sh, dt=F32: st.enter_context(nc.sbuf_tensor(n, sh, dt))
        psb = [st.enter_context(nc.psum_tensor(f"ps{i}", [P, 512], F32)) for i in range(7)]
        pst = st.enter_context(nc.psum_tensor("pst", [P, 4 * P], BF16))
        rot_i = [0]

        def ps_next():
            i = rot_i[0] % 4
            rot_i[0] += 1
            return psb[i], B("ps", i)
        PS_O, PS_DEN, PS_AUX = (psb[4], B("ps", 4)), (psb[5], B("ps", 5)), (psb[6], B("ps", 6))
        PST = (pst, B("pst"))
        ident_f = sb("ident_f", [P, P]); rot_f = sb("rot_f", [P, P]); ones_f = sb("ones_f", [P, P])
        ident_b = sb("ident_b", [P, P], BF16); ones_b = sb("ones_b", [P, P], BF16)
        flg = sb("flg", [P, 8]); fin_g = sb("fin_g", [P, NT])
        zero_f = sb("zero_f", [P, 16])
        msb = sb("msb", [P, L, 2, 6 * NT])
        pv = sb("pv", [P, L, c.NP])
        qgs = sb("qgs", [P, L])
        CB = B("consts")
        k.dma(SP, ident_f[:], consts[:, 0:P], writes=[CB])
        k.dma(SP, rot_f[:], consts[:, P:2 * P], writes=[CB])
        k.dma(SP, ones_f[:], consts[:, 2 * P:3 * P], writes=[CB])
        k.dma(PO, ident_b[:], consts[:, 0:P], writes=[CB])
        k.dma(PO, ones_b[:], consts[:, 2 * P:3 * P], writes=[CB])
        k.dma(SP, flg[:], flags[:, :], writes=[CB])
        k.dma(SP, fin_g[:], fing[:, :], writes=[CB])
        k.op(DV, lambda: nc.vector.memset(zero_f[:], 0.0), writes=[CB])

        def wpieces(l):
            for n, (kk, nn) in wspec.items():
                rows = kk * nn // 8 // 1024
                blk = wblk(rows)
                for pi in range(rows // blk):
                    yield n, rows, blk, pi
        def weight_setup(l):
            for n, rows, blk, pi in wpieces(l):
                r0 = pi * blk
                k.dma(PO, wbf[n][l][r0:r0 + blk, :], wsh[n][l * rows + r0:l * rows + r0 + blk, :], writes=[B("wbfp", n, l, pi)])
            for nm in wspec:
                for n, rows, blk, pi in wpieces(l):
                    if n != nm:
                        continue
                    r0, q0 = pi * blk, pi * 4 * blk
                    k.allgather(QUADS, wbf[n][l][r0:r0 + blk, :], wquad[n][l][q0:q0 + 4 * blk, :],
                                reads=[B("wbfp", n, l, pi)], writes=[B("wq", n, l, pi)])
                for n, rows, blk, pi in wpieces(l):
                    if n != nm:
                        continue
                    q0 = pi * 4 * blk
                    k.allgather(P4, wquad[n][l][q0:q0 + 4 * blk, :], wfull[n][l][2 * q0:2 * q0 + 8 * blk, :],
                                reads=[B("wq", n, l, pi)], writes=[B("wf", n, l)])
        weight_setup(0)

        def ck(name):
            if c.stop == name:
                k.barrier()
                raise _Stop(nc)
        ck('weights')

        def wtile(n, l, ct, KT):
            v = wfull[n][l].ap().rearrange("r j -> (r j)").rearrange("(c p q) -> c p q", p=P, q=KT * P)
            return v[ct]

        ng = 1
        while c.MC % ng or c.MC // ng > 512:
            ng += 1
        gw = c.MC // ng
        assert ng <= 6
        NJ = c.MC // P
        NR = 8 * NJ
        with Scope() as ss:
            c5 = ss("c5", [P, NT * 5])
            adat = [ss(f"adat{i}", [P, c.MC]) for i in range(2)]
            modrow = ss("modrow", [5, c.MC]); adabs = ss("adabs", [5, c.MC])
            modT = ss("modT", [P, L, 5, NR])
            mrow = [ss(f"mrow{i}", [P, P]) for i in range(2)]
            k.dma(SP, c5[:], c5T[:, :], writes=[B("c5")])
            k.op(AC, lambda: nc.scalar.activation(out=c5[:], in_=c5[:], func=ACT.Silu), reads=[B("c5")], writes=[B("c5")])
            for l in range(L):
                k.dma(SP, adabs[:], adab5[:, l * c.MC:(l + 1) * c.MC], writes=[B("adabs")])
                for kt in range(NT):
                    at, ab = adat[kt % 2], B("adat", kt % 2)
                    k.dma(SP, at[:], ada_s[l * D + kt * P:l * D + (kt + 1) * P, :], writes=[ab])
                    for g in range(ng):
                        psg, pb = psb[g], B("ps", g)
                        k.op(PE, lambda psg=psg, at=at, g=g, kt=kt: nc.tensor.matmul(
                            psg[0:5, 0:gw], lhsT=c5[:, kt * 5:(kt + 1) * 5], rhs=at[:, g * gw:(g + 1) * gw],
                            start=(kt == 0), stop=(kt == NT - 1)),
                            reads=[ab, B("c5")], writes=[pb], pe_acc=(kt > 0))
                for g in range(ng):
                    psg, pb = psb[g], B("ps", g)
                    k.op(DV, lambda psg=psg, g=g: nc.vector.tensor_tensor(
                        out=modrow[:, g * gw:(g + 1) * gw], in0=psg[0:5, 0:gw], in1=adabs[:, g * gw:(g + 1) * gw], op=ALU.add),
                        reads=[pb, B("adabs")], writes=[B("modrow")])
                k.dma(PO, modS[:, l * c.MC:(l + 1) * c.MC], modrow[:], reads=[B("modrow")], writes=[B("modS")])
            k.allgather(QUADS, modS[:, :], modQ[:, :], reads=[B("modS")], writes=[B("modQ")])
            k.allgather(P4, modQ[:, :], modG[:, :], reads=[B("modQ")], writes=[B("modG")])
            cnt = 0
            mg = modG.ap().rearrange("(k r) (l j q) -> r l k j q", r=5, l=L, q=P)
            for l in range(L):
                for r in range(5):
                    for h0 in range(0, NR, P):
                        nrow = min(P, NR - h0)
                        mr, mb = mrow[cnt % 2], B("mrow", cnt % 2)
                        cnt += 1
                        for kk_ in range(8):
                            lo, hi = max(h0, kk_ * NJ), min(h0 + nrow, (kk_ + 1) * NJ)
                            if lo < hi:
                                k.dma(SP, mr[lo - h0:hi - h0, :], mg[r, l, kk_, lo - kk_ * NJ:hi - kk_ * NJ, :],
                                      reads=[B("modG")], writes=[mb])
                        k.op(PE, lambda mr=mr, nrow=nrow: nc.tensor.transpose(PS_AUX[0][:, 0:nrow], mr[0:nrow, :], ident_f[0:nrow, 0:nrow]),
                             reads=[mb, CB], writes=[PS_AUX[1]])
                        k.op(DV, lambda l=l, r=r, h0=h0, nrow=nrow: nc.vector.tensor_copy(out=modT[:, l, r, h0:h0 + nrow], in_=PS_AUX[0][:, 0:nrow]),
                             reads=[PS_AUX[1]], writes=[B("modT")])
            for l in range(L):
                k.dma(SP, pv[:, l, :], pvec[l * P:(l + 1) * P, :], writes=[B("pv")])
            for l in range(L):
                k.op(DV, lambda l=l: nc.vector.tensor_scalar(out=msb[:, l, 0, :], in0=modT[:, l, 0, :], scalar1=flg[:, 0:1], scalar2=None, op0=ALU.mult),
                     reads=[B("modT"), CB], writes=[B("msb")])
                for b in range(1, 4):
                    k.op(DV, lambda l=l, b=b: nc.vector.scalar_tensor_tensor(out=msb[:, l, 0, :], in0=modT[:, l, b, :], scalar=flg[:, b:b + 1],
                                                                            in1=msb[:, l, 0, :], op0=ALU.mult, op1=ALU.add),
                         reads=[B("modT"), B("msb"), CB], writes=[B("msb")])
                k.op(DV, lambda l=l: nc.vector.tensor_copy(out=msb[:, l, 1, :], in_=modT[:, l, 4, :]), reads=[B("modT"), B("msb")], writes=[B("msb")])
                for v in range(2):
                    for (mi, gname) in [(1, "n1g"), (4, "n2g")]:
                        k.op(DV, lambda l=l, v=v, mi=mi, gname=gname: nc.vector.scalar_tensor_tensor(
                            out=msb[:, l, v, mi * NT:(mi + 1) * NT], in0=msb[:, l, v, mi * NT:(mi + 1) * NT], scalar=1.0,
                            in1=pv[:, l, c.pv[gname]:c.pv[gname] + NT], op0=ALU.add, op1=ALU.mult),
                            reads=[B("msb"), B("pv")], writes=[B("msb")])
                k.op(DV, lambda l=l: nc.vector.tensor_scalar(out=qgs[:, l:l + 1], in0=pv[:, l, c.pv["qg"]:c.pv["qg"] + 1], scalar1=float(P) ** -0.5,
                                                            scalar2=None, op0=ALU.mult), reads=[B("pv")], writes=[B("qgs")])
        ck('mods')
        MS = B("msb")

        def mod(l, v, mi, t):
            return msb[:, l, v, mi * NT + t:mi * NT + t + 1]

        nch = len(c.chunks)
        lat = list(range(1, nch))
        XT = lambda nm, ci, t: B(nm, ci, t)
        XC = lambda nm, ci: [B(nm, ci, t) for t in range(NT)]
        for ci, (t0, n) in enumerate(c.chunks):
            k.dma(PO, xA[:, t0:t0 + n], xin[:, t0:t0 + n], writes=XC("xA", ci))
        for j in range(c.BT):
            k.dma(PO, zp[j * P:(j + 1) * P, 0:15], zero_f[:, 0:15], reads=[CB], writes=[B("zp", j, "pad")])
            k.dma(PO, zp[j * P:(j + 1) * P, 15 + CTX:15 + CTX + 15], zero_f[:, 0:15], reads=[CB], writes=[B("zp", j, "pad")])

        def xpiece(xt_, nm, ci, a, b):
            return (lambda gi, G: xt_[:, a:b].rearrange("(t p) n -> p t n", p=P)[:, gi:gi + G, :],
                    lambda gi, G: [B(nm, ci, t) for t in range(gi, gi + G)], b - a)

        def norm_mod(ss, pieces, W, l, v, mA, mS, hdst, wk, ydst=None):
            xt, sq, rstd, tmp = wk
            G = 2
            halves = [(0, W)] if W <= 512 else [(0, W // 2), (W // 2, W)]

            def load(gi):
                xb_, xbb = xt[(gi // G) % 2], B("xt", (gi // G) % 2)
                col = 0
                for (apf, bf, w) in pieces:
                    k.dma(SP, xb_[:, :, col:col + w], apf(gi, G), reads=bf(gi, G), writes=[xbb], slow=(w == 1))
                    col += w
                return xb_, xbb
            for gi in range(0, NT, G):
                xb_, xbb = load(gi)
                for tt in range(G):
                    t = gi + tt
                    s_, sbb = sq[t % 2], B("sq", t % 2)
                    k.op(AC, lambda xb_=xb_, tt=tt, s_=s_: nc.scalar.activation(out=s_[:, 0:W], in_=xb_[:, tt, 0:W], func=ACT.Square),
                         reads=[xbb], writes=[sbb])
                    for hi, (a, b) in enumerate(halves):
                        pp = [PS_AUX, PS_DEN][hi]
                        k.op(PE, lambda pp=pp, s_=s_, a=a, b=b, t=t: nc.tensor.matmul(pp[0][:, 0:b - a], lhsT=ones_b[:], rhs=s_[:, a:b],
                                                                                  start=(t == 0), stop=(t == NT - 1)),
                             reads=[sbb, CB], writes=[pp[1]], pe_acc=(t > 0))
            for hi, (a, b) in enumerate(halves):
                pp = [PS_AUX, PS_DEN][hi]
                k.op(DV, lambda pp=pp, a=a, b=b: nc.vector.tensor_scalar(out=rstd[:, a:b], in0=pp[0][:, 0:b - a], scalar1=1.0 / D, scalar2=EPS,
                                                                       op0=ALU.mult, op1=ALU.add), reads=[pp[1], B("rstd")], writes=[B("rstd")])
            k.op(AC, lambda: nc.scalar.sqrt(out=rstd[:, 0:W], in_=rstd[:, 0:W]), reads=[B("rstd")], writes=[B("rstd")])
            k.op(DV, lambda: nc.vector.reciprocal(out=rstd[:, 0:W], in_=rstd[:, 0:W]), reads=[B("rstd")], writes=[B("rstd")])
            for gi in range(0, NT, G):
                xb_, xbb = load(gi)
                for tt in range(G):
                    t = gi + tt
                    tm, tb = tmp()
                    k.op(DV, lambda xb_=xb_, tt=tt, tm=tm: nc.vector.tensor_tensor(out=tm[:, 0:W], in0=xb_[:, tt, 0:W], in1=rstd[:, 0:W], op=ALU.mult),
                         reads=[xbb, B("rstd")], writes=[tb])
                    if ydst is None:
                        k.op(AC, lambda tm=tm, t=t: nc.scalar.activation(out=hdst[:, t, 0:W], in_=tm[:, 0:W], func=ACT.Identity,
                                                                       bias=mod(l, v, mS, t), scale=mod(l, v, mA, t)),
                             reads=[tb, MS], writes=[B("hsb")])
                    else:
                        ydst(t, tm, tb)

        def mk_norm_bufs(ss):
            xt = [ss(f"xt{i}", [P, 2, 514]) for i in range(2)]
            sq = [ss(f"sq{i}", [P, 514], BF16) for i in range(2)]
            rstd = ss("rstd", [P, 514])
            tmpf = [ss(f"tmpf{i}", [P, 514]) for i in range(3)]
            tc_ = [0]

            def tmp():
                i = tc_[0] % 3
                tc_[0] += 1
                return tmpf[i], B("tmpf", i)
            return (xt, sq, rstd, tmp)

        def mk_w(ss, KT, nb=4):
            wb_ = [ss(f"wb{i}", [P, KT * P], BF16) for i in range(nb)]
            wr = [0]

            def wload(n, l, ct, KT_):
                i = wr[0] % nb
                wr[0] += 1
                k.dma(SP, wb_[i][:, 0:KT_ * P], wtile(n, l, ct, KT_), reads=[B("wf", n, l)], writes=[B("wb", i)])
                return wb_[i], B("wb", i)
            return wload

        def mk_ob(ss):
            ob = [ss(f"ob{i}", [P, 512], BF16) for i in range(3)]
            oc = [0]

            def obuf():
                i = oc[0] % 3
                oc[0] += 1
                return ob[i], B("ob", i)
            return obuf

        cur, nxt = (xA, "xA"), (xB_, "xB")
        for l in range(L):
            last = (l == L - 1)
            X, XN = cur
            pvo = lambda name, j=0: pv[:, l, c.pv[name] + j:c.pv[name] + j + 1]
            with Scope() as ss:
                wk = mk_norm_bufs(ss)
                tmp = wk[3]
                hsb = ss("hsb", [P, NT, 514], BF16)
                wload = mk_w(ss, NT)
                obuf = mk_ob(ss)
                vt = [ss(f"vt{i}", [P, 4 * P], BF16) for i in range(2)]
                sg = ss("sg", [P, c.BT, 512])
                cos_sb = ss("cos_sb", [P, T]); sin_sb = ss("sin_sb", [P, T])
                k.dma(SP, cos_sb[:], cosT[:, :], writes=[B("rope")])
                k.dma(SP, sin_sb[:], sinT[:, :], writes=[B("rope")])
                RB = B("rope")
                xcnt = 0
                for ci, (t0, n) in enumerate(c.chunks):
                    v = 1 if ci == 0 else 0
                    norm_mod(ss, [xpiece(X, XN, ci, t0, t0 + n)], n, l, v, 1, 0, hsb, wk)
                    HB = B("hsb")
                    if l == 0 and ci == 0:
                        ck('n1')
                    order = list(range(c.o_qa, c.o_ba)) + [x_ for j in range(c.BT) for x_ in (c.o_bg + j, c.o_ba + j)] + list(range(c.o_qn, c.CTI))
                    for ct in order:
                        if ci == 0 and last and (ct < c.o_ka or c.o_ba <= ct < c.o_kn):
                            continue
                        w_, wbb = wload("win", l, ct, NT)
                        ck('w0')
                        ps_, pb = ps_next()
                        for kt in range(NT):
                            k.op(PE, lambda ps_=ps_, w_=w_, kt=kt: nc.tensor.matmul(ps_[:, 0:n], lhsT=w_[:, kt * P:(kt + 1) * P], rhs=hsb[:, kt, 0:n],
                                                                                  start=(kt == 0), stop=(kt == NT - 1)),
                                 reads=[wbb, HB], writes=[pb], pe_acc=(kt > 0), inc=(kt == NT - 1))
                        ck('mm0')
                        if ct < c.o_va:
                            isq = ct < c.o_ka
                            qf, qfb = tmp()
                            k.op(DV, lambda qf=qf, ps_=ps_: nc.vector.tensor_copy(out=qf[:, 0:n], in_=ps_[:, 0:n]), reads=[pb], writes=[qfb])
                            ck('h0')
                            s2, s2b = tmp()
                            k.op(DV, lambda s2=s2, qf=qf: nc.vector.tensor_tensor(out=s2[:, 0:n], in0=qf[:, 0:n], in1=qf[:, 0:n], op=ALU.mult), reads=[qfb], writes=[s2b])
                            ck('h1')
                            k.op(PE, lambda s2=s2: nc.tensor.matmul(PS_AUX[0][:, 0:n], lhsT=ones_f[:], rhs=s2[:, 0:n], start=True, stop=True),
                                 reads=[s2b, CB], writes=[PS_AUX[1]])
                            ck('h2')
                            k.op(DV, lambda s2=s2: nc.vector.tensor_scalar(out=s2[:, 0:n], in0=PS_AUX[0][:, 0:n], scalar1=1.0 / P, scalar2=EPS, op0=ALU.mult, op1=ALU.add),
                                 reads=[PS_AUX[1], s2b], writes=[s2b])
                            ck('h3')
                            k.op(AC, lambda s2=s2: nc.scalar.sqrt(out=s2[:, 0:n], in_=s2[:, 0:n]), reads=[s2b], writes=[s2b])
                            k.op(DV, lambda s2=s2: nc.vector.reciprocal(out=s2[:, 0:n], in_=s2[:, 0:n]), reads=[s2b], writes=[s2b])
                            ck('hn')
                            gain = qgs[:, l:l + 1] if isq else pvo("kg")
                            k.op(DV, lambda qf=qf, s2=s2, gain=gain: nc.vector.scalar_tensor_tensor(out=qf[:, 0:n], in0=qf[:, 0:n], scalar=gain, in1=s2[:, 0:n],
                                                                                                 op0=ALU.mult, op1=ALU.mult),
                                 reads=[qfb, s2b, B("qgs"), B("pv")], writes=[qfb])
                            k.op(PE, lambda qf=qf: nc.tensor.matmul(PS_AUX[0][:, 0:n], lhsT=rot_f[:], rhs=qf[:, 0:n], start=True, stop=True),
                                 reads=[qfb, CB], writes=[PS_AUX[1]])
                            k.op(DV, lambda s2=s2: nc.vector.tensor_tensor(out=s2[:, 0:n], in0=PS_AUX[0][:, 0:n], in1=sin_sb[:, t0:t0 + n], op=ALU.mult),
                                 reads=[PS_AUX[1], RB, s2b], writes=[s2b])
                            k.op(DV, lambda qf=qf: nc.vector.tensor_tensor(out=qf[:, 0:n], in0=qf[:, 0:n], in1=cos_sb[:, t0:t0 + n], op=ALU.mult),
                                 reads=[qfb, RB], writes=[qfb])
                            o_, obb = obuf()
                            k.op(DV, lambda o_=o_, qf=qf, s2=s2: nc.vector.tensor_tensor(out=o_[:, 0:n], in0=qf[:, 0:n], in1=s2[:, 0:n], op=ALU.add),
                                 reads=[qfb, s2b], writes=[obb])
                            ck('rope')
                            if isq:
                                k.dma(PO, qaT[ct * P:(ct + 1) * P, t0:t0 + n], o_[:, 0:n], reads=[obb], writes=[B("qa", ct, ci)])
                            else:
                                h_ = ct - c.o_ka
                                k.dma(PO, kaT[h_ * P:(h_ + 1) * P, t0:t0 + n], o_[:, 0:n], reads=[obb], writes=[B("ka", h_, ci)])
                                if ci > 0:
                                    k.dma(PO, kaS[h_ * P:(h_ + 1) * P, t0 - CTX:t0 - CTX + n], o_[:, 0:n], reads=[obb], writes=[B("kaS")])
                        elif ct < c.o_ba or ct >= c.o_vn:
                            isa = ct < c.o_ba
                            h_ = ct - (c.o_va if isa else c.o_vn)
                            o_, obb = obuf()
                            k.op(AC, lambda o_=o_, ps_=ps_: nc.scalar.copy(out=o_[:, 0:n], in_=ps_[:, 0:n]), reads=[pb], writes=[obb])
                            nb = n // P
                            for bi in range(nb):
                                k.op(PE, lambda o_=o_, bi=bi: nc.tensor.transpose(PST[0][:, bi * P:(bi + 1) * P], o_[:, bi * P:(bi + 1) * P], ident_b[:]),
                                     reads=[obb, CB], writes=[PST[1]], pe_acc=(bi > 0), inc=(bi == nb - 1))
                            vt_, vtb = vt[xcnt % 2], B("vt", xcnt % 2)
                            xcnt += 1
                            k.op(DV, lambda vt_=vt_: nc.vector.tensor_copy(out=vt_[:, 0:n], in_=PST[0][:, 0:n]), reads=[PST[1]], writes=[vtb])
                            dstT = va if isa else vn
                            v3 = lambda ap: ap.rearrange("(b p) d -> p b d", p=P)
                            s3 = lambda lo, hi: vt_[:, lo:hi].rearrange("p (b d) -> p b d", d=P)
                            k.dma(PO, v3(dstT[h_ * T + t0:h_ * T + t0 + n, :]), s3(0, n), reads=[vtb], writes=[B("va" if isa else "vn", h_, ci)])
                            if isa and ci > 0:
                                k.dma(PO, v3(vaS[h_ * SL + t0 - CTX:h_ * SL + t0 - CTX + n, :]), s3(0, n), reads=[vtb], writes=[B("vaS")])
                            if (not isa) and ci == 1:
                                k.dma(PO, v3(vnS[h_ * 512:h_ * 512 + 256, :]), s3(0, 256), reads=[vtb], writes=[B("vnS")])
                            if (not isa) and ci == c.NBLK:
                                k.dma(PO, v3(vnS[h_ * 512 + 256:h_ * 512 + 512, :]), s3(n - 256, n), reads=[vtb], writes=[B("vnS")])
                        elif ct < c.o_qn:
                            if ct >= c.o_bg:
                                j = ct - c.o_bg
                                k.op(AC, lambda ps_=ps_, j=j: nc.scalar.activation(out=sg[:, j, 0:n], in_=ps_[:, 0:n], func=ACT.Sigmoid), reads=[pb], writes=[B("sg", j)])
                            else:
                                j = ct - c.o_ba
                                z_, zb = tmp()
                                k.op(DV, lambda z_=z_, ps_=ps_, j=j: nc.vector.tensor_tensor(out=z_[:, 0:n], in0=ps_[:, 0:n], in1=sg[:, j, 0:n], op=ALU.mult),
                                     reads=[pb, B("sg", j)], writes=[zb])
                                zo = 15 if ci == 0 else (15 + CTX + 15 + 15 + t0 - CTX)
                                k.dma(PO, zp[j * P:(j + 1) * P, zo:zo + n], z_[:, 0:n], reads=[zb], writes=[B("zp", j, ci)])
                                if ci == 1:
                                    k.dma(PO, zS[j * P:(j + 1) * P, 0:16], z_[:, 0:16], reads=[zb], writes=[B("zS")])
                                if ci == c.NBLK:
                                    k.dma(PO, zS[j * P:(j + 1) * P, 16:32], z_[:, n - 16:n], reads=[zb], writes=[B("zS")])
                        else:
                            isq = ct < c.o_kn
                            h_ = ct - (c.o_qn if isq else c.o_kn)
                            o_, obb = obuf()
                            if isq:
                                k.op(AC, lambda o_=o_, ps_=ps_: nc.scalar.mul(out=o_[:, 0:n], in_=ps_[:, 0:n], mul=float(P) ** -0.5), reads=[pb], writes=[obb])
                                k.dma(PO, qnT[h_ * P:(h_ + 1) * P, t0:t0 + n], o_[:, 0:n], reads=[obb], writes=[B("qn", h_, ci)])
                            else:
                                k.op(AC, lambda o_=o_, ps_=ps_: nc.scalar.copy(out=o_[:, 0:n], in_=ps_[:, 0:n]), reads=[pb], writes=[obb])
                                k.dma(PO, knT[h_ * P:(h_ + 1) * P, t0:t0 + n], o_[:, 0:n], reads=[obb], writes=[B("kn", h_, ci)])
                                if ci == 1:
                                    k.dma(PO, knS[h_ * P:(h_ + 1) * P, 0:256], o_[:, 0:256], reads=[obb], writes=[B("knS")])
                                if ci == c.NBLK:
                                    k.dma(PO, knS[h_ * P:(h_ + 1) * P, 256:512], o_[:, n - 256:n], reads=[obb], writes=[B("knS")])
                        if l == 0 and ci == 0:
                            ck(f'ct{ct}')
            ck('AB')
            for h_ in range(c.AKV):
                k.allgather(PAIRS, kaS[h_ * P:(h_ + 1) * P, :], kaG[2 * h_ * P:2 * (h_ + 1) * P, :], reads=[B("kaS")], writes=[B("kaG")])
                k.allgather(PAIRS, vaS[h_ * SL:(h_ + 1) * SL, :], vaG[2 * h_ * SL:2 * (h_ + 1) * SL, :], reads=[B("vaS")], writes=[B("vaG")])
            for h_ in range(c.CH):
                k.allgather(PAIRS, knS[h_ * P:(h_ + 1) * P, :], knG[2 * h_ * P:2 * (h_ + 1) * P, :], reads=[B("knS")], writes=[B("knG")])
                k.allgather(PAIRS, vnS[h_ * 512:(h_ + 1) * 512, :], vnG[2 * h_ * 512:2 * (h_ + 1) * 512, :], reads=[B("vnS")], writes=[B("vnG")])
            k.allgather(PAIRS, zS[:, :], zG[:, :], reads=[B("zS")], writes=[B("zG")])
            zl = 15 + CTX + 15
            with Scope() as ss:
                halo_sb = ss("halo_sb", [P, 2, 16])
                for j in range(c.BT):
                    k.dma(SP, halo_sb[:, 0, :], zG[j * P:(j + 1) * P, 16:32], reads=[B("zG")], writes=[B("halo")])
                    k.dma(SP, halo_sb[:, 1, :], zG[c.BCH + j * P:c.BCH + (j + 1) * P, 0:16], reads=[B("zG")], writes=[B("halo")])
                    k.op(DV, lambda: nc.vector.tensor_scalar(out=halo_sb[:, 0, :], in0=halo_sb[:, 0, :], scalar1=flg[:, 4:5], scalar2=None, op0=ALU.mult),
                         reads=[B("halo"), CB], writes=[B("halo")])
                    k.op(DV, lambda: nc.vector.tensor_scalar(out=halo_sb[:, 1, :], in0=halo_sb[:, 1, :], scalar1=flg[:, 5:6], scalar2=None, op0=ALU.mult),
                         reads=[B("halo"), CB], writes=[B("halo")])
                    k.dma(PO, zp[j * P:(j + 1) * P, zl:zl + 15], halo_sb[:, 0, 1:16], reads=[B("halo")], writes=[B("zp", j, "h0")])
                    k.dma(PO, zp[j * P:(j + 1) * P, zl + 15 + SL:zl + 30 + SL], halo_sb[:, 1, 0:15], reads=[B("halo")], writes=[B("zp", j, "h1")])
            ck('C')
            if l + 1 < L:
                weight_setup(l + 1)
            KMAX = CTX + c.SEQ
            with Scope() as ss:
                kT_sb = ss("kT_sb", [P, KMAX], BF16)
                v_sb = ss("v_sb", [P, KMAX // P, P], BF16)
                q_sb = [ss(f"q_sb{i}", [P, 512], BF16) for i in range(2)]
                p_sb = [ss(f"p_sb{i}", [P, 512], BF16) for i in range(3)]
                bias_sb = ss("bias_sb", [P, 8, 512], BF16)
                mask_sb = ss("mask_sb", [P, c.NBLK, 8 * 512], BF16)
                rden = ss("rden", [P, 512])
                obuf = mk_ob(ss)
                for bi in range(c.NBLK):
                    k.dma(SP, mask_sb[:, bi, :], nmask[bi * P:(bi + 1) * P, :], writes=[B("mask_sb")])
                cn = [0, 0]

                def attend(qsrc, qbufs, nq, ktiles, dst_ap, dst_bufs, kb):
                    qs, qb = q_sb[cn[0] % 2], B("q_sb", cn[0] % 2)
                    cn[0] += 1
                    k.dma(SP, qs[:, 0:nq], qsrc, reads=qbufs, writes=[qb])
                    nk = len(ktiles)

                    def qk(i):
                        ko, vi, extra = ktiles[i]
                        ps_, pb = ps_next()
                        k.op(PE, lambda ps_=ps_, ko=ko: nc.tensor.matmul(ps_[:, 0:nq], lhsT=kT_sb[:, ko:ko + P], rhs=qs[:, 0:nq], start=True, stop=(extra is None)),
                             reads=[kb, qb], writes=[pb], inc=(extra is None))
                        if extra is not None:
                            bj, mblk, mj = extra
                            k.op(PE, lambda ps_=ps_, bj=bj: nc.tensor.matmul(ps_[:, 0:nq], lhsT=ident_b[:], rhs=bias_sb[:, bj, 0:nq], start=False, stop=False),
                                 reads=[B("bias_sb"), CB], writes=[pb], pe_acc=True, inc=False)
                            k.op(PE, lambda ps_=ps_, mblk=mblk, mj=mj: nc.tensor.matmul(ps_[:, 0:nq], lhsT=ident_b[:], rhs=mask_sb[:, mblk, mj * 512:mj * 512 + nq],
                                                                                      start=False, stop=True),
                                 reads=[B("mask_sb"), CB], writes=[pb], pe_acc=True)
                        return ps_, pb
                    cur_s = qk(0)
                    for i, (ko, vi, extra) in enumerate(ktiles):
                        nxt_s = qk(i + 1) if i + 1 < nk else None
                        ps_, pb = cur_s
                        pt, ptb = p_sb[cn[1] % 3], B("p_sb", cn[1] % 3)
                        cn[1] += 1
                        k.op(AC, lambda ps_=ps_, pt=pt: nc.scalar.activation(out=pt[:, 0:nq], in_=ps_[:, 0:nq], func=ACT.Exp), reads=[pb], writes=[ptb])
                        k.op(PE, lambda pt=pt, vi=vi, i=i: nc.tensor.matmul(PS_O[0][:, 0:nq], lhsT=v_sb[:, vi, :], rhs=pt[:, 0:nq], start=(i == 0), stop=(i == nk - 1)),
                             reads=[ptb, kb], writes=[PS_O[1]], pe_acc=(i > 0), inc=False)
                        k.op(PE, lambda pt=pt, i=i: nc.tensor.matmul(PS_DEN[0][:, 0:nq], lhsT=ones_b[:], rhs=pt[:, 0:nq], start=(i == 0), stop=(i == nk - 1)),
                             reads=[ptb, CB], writes=[PS_DEN[1]], pe_acc=(i > 0))
                        cur_s = nxt_s
                    k.op(DV, lambda: nc.vector.reciprocal(out=rden[:, 0:nq], in_=PS_DEN[0][:, 0:nq]), reads=[PS_DEN[1]], writes=[B("rden")])
                    o_, obb = obuf()
                    k.op(DV, lambda o_=o_: nc.vector.tensor_tensor(out=o_[:, 0:nq], in0=PS_O[0][:, 0:nq], in1=rden[:, 0:nq], op=ALU.mult),
                         reads=[PS_O[1], B("rden")], writes=[obb])
                    k.dma(PO, dst_ap, o_[:, 0:nq], reads=[obb], writes=dst_bufs)

                KB = B("kv_sb")
                v3 = lambda ap: ap.rearrange("(b p) d -> p b d", p=P)
                for kv in range(c.AKV):
                    k.dma(SP, kT_sb[:, 0:CTX], kaT[kv * P:(kv + 1) * P, 0:CTX], reads=[B("ka", kv, 0)], writes=[KB])
                    k.dma(SP, kT_sb[:, CTX:CTX + SL], kaG[(2 * kv) * P:(2 * kv + 1) * P, :], reads=[B("kaG")], writes=[KB])
                    k.dma(SP, kT_sb[:, CTX + SL:CTX + 2 * SL], kaG[(2 * kv + 1) * P:(2 * kv + 2) * P, :], reads=[B("kaG")], writes=[KB])
                    k.dma(SP, v_sb[:, 0:CTX // P, :], v3(va[kv * T:kv * T + CTX, :]), reads=[B("va", kv, 0)], writes=[KB])
                    for hf in range(2):
                        base = (2 * kv + hf) * SL
                        k.dma(SP, v_sb[:, (CTX + hf * SL) // P:(CTX + (hf + 1) * SL) // P, :], v3(vaG[base:base + SL, :]), reads=[B("vaG")], writes=[KB])
                    for g in range(3):
                        h_ = kv * 3 + g
                        for ci in lat:
                            t0, n = c.chunks[ci]
                            kts = [(i * P, i, None) for i in range(KMAX // P)]
                            attend(qaT[h_ * P:(h_ + 1) * P, t0:t0 + n], [B("qa", h_, ci)], n, kts,
                                   catT[h_ * P:(h_ + 1) * P, t0:t0 + n], [B("cat", h_, ci)], KB)
                        if not last:
                            kts = [(i * P, i, None) for i in range(CTX // P)]
                            attend(qaT[h_ * P:(h_ + 1) * P, 0:CTX], [B("qa", h_, 0)], CTX, kts,
                                   catT[h_ * P:(h_ + 1) * P, 0:CTX], [B("cat", h_, 0)], KB)
                for h_ in range(c.CH):
                    k.dma(SP, kT_sb[:, 0:CTX], knT[h_ * P:(h_ + 1) * P, 0:CTX], reads=[B("kn", h_, 0)], writes=[KB])
                    k.dma(SP, kT_sb[:, CTX:CTX + 256], knG[(2 * h_) * P:(2 * h_ + 1) * P, 256:512], reads=[B("knG")], writes=[KB])
                    k.dma(SP, kT_sb[:, CTX + 256:CTX + 256 + SL], knT[h_ * P:(h_ + 1) * P, CTX:T], reads=[B("kn", h_, ci) for ci in lat], writes=[KB])
                    k.dma(SP, kT_sb[:, CTX + 256 + SL:CTX + 512 + SL], knG[(2 * h_ + 1) * P:(2 * h_ + 2) * P, 0:256], reads=[B("knG")], writes=[KB])
                    k.dma(SP, v_sb[:, 0:2, :], v3(vn[h_ * T:h_ * T + CTX, :]), reads=[B("vn", h_, 0)], writes=[KB])
                    k.dma(SP, v_sb[:, 2:4, :], v3(vnG[(2 * h_) * 512 + 256:(2 * h_) * 512 + 512, :]), reads=[B("vnG")], writes=[KB])
                    k.dma(SP, v_sb[:, 4:4 + SL // P, :], v3(vn[h_ * T + CTX:h_ * T + T, :]), reads=[B("vn", h_, ci) for ci in lat], writes=[KB])
                    k.dma(SP, v_sb[:, 4 + SL // P:6 + SL // P, :], v3(vnG[(2 * h_ + 1) * 512:(2 * h_ + 1) * 512 + 256, :]), reads=[B("vnG")], writes=[KB])
                    tab = rpbT[(l * c.CH + h_) * 64:(l * c.CH + h_ + 1) * 64, :]
                    for j in range(8):
                        for b in range(2):
                            d0 = 2 * j + b
                            k.dma(PO, bias_sb[b * 64:(b + 1) * 64, j, :], tab[:, (15 - d0) * 64:(15 - d0 + 8) * 64], writes=[B("bias_sb")])
                    for bi, ci in enumerate(lat):
                        t0, n = c.chunks[ci]
                        kts = [(i * P, i, None) for i in range(2)]
                        for j in range(8):
                            eo = (8 * bi + 2 * j) * 64
                            kts.append((CTX + eo, 2 + eo // P, (j, bi, j)))
                        cr = c.AH + c.BT + h_
                        attend(qnT[h_ * P:(h_ + 1) * P, t0:t0 + n], [B("qn", h_, ci)], n, kts,
                               catT[cr * P:(cr + 1) * P, t0:t0 + n], [B("cat", cr, ci)], KB)
                    if not last:
                        kts = [(i * P, i, None) for i in range(2)]
                        cr = c.AH + c.BT + h_
                        attend(qnT[h_ * P:(h_ + 1) * P, 0:CTX], [B("qn", h_, 0)], CTX, kts,
                               catT[cr * P:(cr + 1) * P, 0:CTX], [B("cat", cr, 0)], KB)
            ck('D12')
            with Scope() as ss:
                zwin = [ss(f"zwin{i}", [P, 512 + 30]) for i in range(2)]
                yconv = ss("yconv", [P, c.BT, 512])
                ysq = ss("ysq", [P, 512]); mean = ss("mean", [P, 512]); var = ss("var", [P, 512])
                zz = ss("zz", [P, c.BT, 512], BF16)
                wload = mk_w(ss, c.BT)
                obuf = mk_ob(ss)
                for ci, (t0, n) in enumerate(c.chunks):
                    if ci == 0 and last:
                        continue
                    zo = 0 if ci == 0 else (15 + CTX + 15 + t0 - CTX)
                    for j in range(c.BT):
                        zi = (ci * c.BT + j) % 2
                        zw, zwb = zwin[zi], B("zwin", zi)
                        deps = [B("zp", j, "pad"), B("zp", j, "h0"), B("zp", j, "h1")] + [B("zp", j, cj) for cj in range(nch)]
                        k.dma(SP, zw[:, 0:n + 30], zp[j * P:(j + 1) * P, zo:zo + n + 30], reads=deps, writes=[zwb])
                        YB = B("yconv", j)
                        k.op(DV, lambda zw=zw, j=j: nc.vector.tensor_scalar(out=yconv[:, j, 0:n], in0=zw[:, 0:n], scalar1=pvo("bdw", j * 31), scalar2=pvo("bdb", j),
                                                                          op0=ALU.mult, op1=ALU.add), reads=[zwb, B("pv")], writes=[YB])
                        for tap in range(1, 31):
                            k.op(DV, lambda zw=zw, j=j, tap=tap: nc.vector.scalar_tensor_tensor(out=yconv[:, j, 0:n], in0=zw[:, tap:tap + n], scalar=pvo("bdw", j * 31 + tap),
                                                                                             in1=yconv[:, j, 0:n], op0=ALU.mult, op1=ALU.add),
                                 reads=[zwb, B("pv"), YB], writes=[YB])
                        k.op(PE, lambda j=j: nc.tensor.matmul(PS_AUX[0][:, 0:n], lhsT=ones_f[:], rhs=yconv[:, j, 0:n], start=(j == 0), stop=(j == c.BT - 1)),
                             reads=[YB, CB], writes=[PS_AUX[1]], pe_acc=(j > 0))
                        k.op(DV, lambda j=j: nc.vector.tensor_tensor(out=ysq[:, 0:n], in0=yconv[:, j, 0:n], in1=yconv[:, j, 0:n], op=ALU.mult), reads=[YB, B("ysq")], writes=[B("ysq")])
                        k.op(PE, lambda j=j: nc.tensor.matmul(PS_DEN[0][:, 0:n], lhsT=ones_f[:], rhs=ysq[:, 0:n], start=(j == 0), stop=(j == c.BT - 1)),
                             reads=[B("ysq"), CB], writes=[PS_DEN[1]], pe_acc=(j > 0))
                    k.op(DV, lambda: nc.vector.tensor_scalar(out=mean[:, 0:n], in0=PS_AUX[0][:, 0:n], scalar1=1.0 / c.BCH, scalar2=None, op0=ALU.mult),
                         reads=[PS_AUX[1]], writes=[B("mean")])
                    k.op(DV, lambda: nc.vector.tensor_scalar(out=var[:, 0:n], in0=PS_DEN[0][:, 0:n], scalar1=1.0 / c.BCH, scalar2=EPS, op0=ALU.mult, op1=ALU.add),
                         reads=[PS_DEN[1]], writes=[B("var")])
                    k.op(DV, lambda: nc.vector.tensor_tensor(out=ysq[:, 0:n], in0=mean[:, 0:n], in1=mean[:, 0:n], op=ALU.mult), reads=[B("mean"), B("ysq")], writes=[B("ysq")])
                    k.op(DV, lambda: nc.vector.tensor_tensor(out=var[:, 0:n], in0=var[:, 0:n], in1=ysq[:, 0:n], op=ALU.subtract), reads=[B("var"), B("ysq")], writes=[B("var")])
                    k.op(AC, lambda: nc.scalar.sqrt(out=var[:, 0:n], in_=var[:, 0:n]), reads=[B("var")], writes=[B("var")])
                    k.op(DV, lambda: nc.vector.reciprocal(out=var[:, 0:n], in_=var[:, 0:n]), reads=[B("var")], writes=[B("var")])
                    for j in range(c.BT):
                        YB = B("yconv", j)
                        k.op(DV, lambda j=j: nc.vector.tensor_tensor(out=yconv[:, j, 0:n], in0=yconv[:, j, 0:n], in1=mean[:, 0:n], op=ALU.subtract),
                             reads=[YB, B("mean")], writes=[YB])
                        k.op(DV, lambda j=j: nc.vector.tensor_tensor(out=yconv[:, j, 0:n], in0=yconv[:, j, 0:n], in1=var[:, 0:n], op=ALU.mult),
                             reads=[YB, B("var")], writes=[YB])
                        k.op(DV, lambda j=j: nc.vector.tensor_scalar(out=yconv[:, j, 0:n], in0=yconv[:, j, 0:n], scalar1=pvo("blg", j), scalar2=pvo("blb", j),
                                                                   op0=ALU.mult, op1=ALU.add), reads=[YB, B("pv")], writes=[YB])
                        k.op(AC, lambda j=j: nc.scalar.activation(out=zz[:, j, 0:n], in_=yconv[:, j, 0:n], func=ACT.Silu), reads=[YB, B("zz")], writes=[B("zz")])
                    for j in range(c.BT):
                        w_, wbb = wload("wpw", l, j, c.BT)
                        ps_, pb = ps_next()
                        for kt in range(c.BT):
                            k.op(PE, lambda ps_=ps_, w_=w_, kt=kt: nc.tensor.matmul(ps_[:, 0:n], lhsT=w_[:, kt * P:(kt + 1) * P], rhs=zz[:, kt, 0:n],
                                                                                  start=(kt == 0), stop=(kt == c.BT - 1)),
                                 reads=[wbb, B("zz")], writes=[pb], pe_acc=(kt > 0), inc=(kt == c.BT - 1))
                        o_, obb = obuf()
                        k.op(AC, lambda o_=o_, ps_=ps_, j=j: nc.scalar.activation(out=o_[:, 0:n], in_=ps_[:, 0:n], func=ACT.Identity, bias=pvo("bpb", j), scale=1.0),
                             reads=[pb, B("pv")], writes=[obb])
                        k.dma(PO, catT[(c.AH + j) * P:(c.AH + j + 1) * P, t0:t0 + n], o_[:, 0:n], reads=[obb], writes=[B("cat", c.AH + j, ci)])
            ck('D3')
            with Scope() as ss:
                cat_sb = ss("cat_sb", [P, NT, 512], BF16)
                wload = mk_w(ss, NT)
                xres = [ss(f"xres{i}", [P, 512]) for i in range(2)]
                xnew = [ss(f"xnew{i}", [P, 512]) for i in range(2)]
                xe_sb = ss("xe_sb", [P, 2 * NT])
                for ci, (t0, n) in enumerate(c.chunks):
                    if ci == 0 and last:
                        continue
                    v = 1 if ci == 0 else 0
                    k.dma(SP, cat_sb[:, :, 0:n], catT[:, t0:t0 + n].rearrange("(t p) n -> p t n", p=P), reads=[B("cat", t, ci) for t in range(NT)], writes=[B("cat_sb")])
                    for ct in range(NT):
                        w_, wbb = wload("wout", l, ct, NT)
                        ps_, pb = ps_next()
                        for kt in range(NT):
                            k.op(PE, lambda ps_=ps_, w_=w_, kt=kt: nc.tensor.matmul(ps_[:, 0:n], lhsT=w_[:, kt * P:(kt + 1) * P], rhs=cat_sb[:, kt, 0:n],
                                                                                  start=(kt == 0), stop=(kt == NT - 1)),
                                 reads=[wbb, B("cat_sb")], writes=[pb], pe_acc=(kt > 0), inc=(kt == NT - 1))
                        xr, xrb = xres[ct % 2], B("xres", ct % 2)
                        k.dma(SP, xr[:, 0:n], X[ct * P:(ct + 1) * P, t0:t0 + n], reads=[B(XN, ci, ct)], writes=[xrb])
                        xn, xnb = xnew[ct % 2], B("xnew", ct % 2)
                        k.op(DV, lambda xn=xn, ps_=ps_, xr=xr, ct=ct: nc.vector.scalar_tensor_tensor(out=xn[:, 0:n], in0=ps_[:, 0:n], scalar=mod(l, v, 2, ct), in1=xr[:, 0:n],
                                                                                                  op0=ALU.mult, op1=ALU.add), reads=[pb, xrb, MS], writes=[xnb])
                        k.dma(PO, X[ct * P:(ct + 1) * P, t0:t0 + n], xn[:, 0:n], reads=[xnb], writes=[B(XN, ci, ct)])
                        if ci == 1:
                            k.op(DV, lambda xn=xn, ct=ct: nc.vector.tensor_copy(out=xe_sb[:, 2 * ct:2 * ct + 1], in_=xn[:, 0:1]), reads=[xnb, B("xe_sb")], writes=[B("xe_sb")])
                        if ci == c.NBLK:
                            k.op(DV, lambda xn=xn, ct=ct: nc.vector.tensor_copy(out=xe_sb[:, 2 * ct + 1:2 * ct + 2], in_=xn[:, n - 1:n]), reads=[xnb, B("xe_sb")], writes=[B("xe_sb")])
                k.dma(PO, xeS[:, :], xe_sb[:], reads=[B("xe_sb")], writes=[B("xeS")])
            k.allgather(PAIRS, xeS[:, :], xeG[:, :], reads=[B("xeS")], writes=[B("xeG")])
            ck('E')
            Y, YN = nxt
            with Scope() as ss:
                wk = mk_norm_bufs(ss)
                tmp = wk[3]
                hsb = ss("hsb", [P, NT, 514], BF16)
                wload = mk_w(ss, max(NT, c.FT), nb=3)
                U = [ss(f"U{i}", [P, 514]) for i in range(2)]
                act_sb = ss("act_sb", [P, c.FT, 512], BF16)
                gsil = ss("gsil", [P, 512])
                xres = [ss(f"xres{i}", [P, 512]) for i in range(2)]
                xnew = [ss(f"xnew{i}", [P, 512]) for i in range(2)]

                def xepiece(r, e):
                    return (lambda gi, G: xeG[r * P:(r + 1) * P, :].rearrange("p (t e) -> p t e", e=2)[:, gi:gi + G, e:e + 1],
                            lambda gi, G: [B("xeG")], 1)
                for ci, (t0, n) in enumerate(c.chunks):
                    if ci == 0 and last:
                        continue
                    v = 1 if ci == 0 else 0
                    W = n + 2
                    first_lat, last_lat = ci == 1, ci == c.NBLK
                    if ci == 0:
                        left, right = xpiece(X, XN, ci, t0, t0 + 1), xpiece(X, XN, ci, t0 + n - 1, t0 + n)
                    else:
                        left = xepiece(0, 1) if first_lat else xpiece(X, XN, ci - 1, t0 - 1, t0)
                        right = xepiece(1, 0) if last_lat else xpiece(X, XN, ci + 1, t0 + n, t0 + n + 1)
                    norm_mod(ss, [left, xpiece(X, XN, ci, t0, t0 + n), right], W, l, v, 4, 3, hsb, wk)
                    HB = B("hsb")
                    hw = W // 2
                    for ft in range(c.FT):
                        for part in range(2):
                            ct = ft + part * c.FT
                            w_, wbb = wload("wup", l, ct, NT)
                            pA, pAb = ps_next()
                            pB, pBb = ps_next()
                            for (pp, ppb, a) in [(pA, pAb, 0), (pB, pBb, hw)]:
                                for kt in range(NT):
                                    k.op(PE, lambda pp=pp, w_=w_, kt=kt, a=a: nc.tensor.matmul(pp[:, 0:hw], lhsT=w_[:, kt * P:(kt + 1) * P], rhs=hsb[:, kt, a:a + hw],
                                                                                             start=(kt == 0), stop=(kt == NT - 1)),
                                         reads=[wbb, HB], writes=[ppb], pe_acc=(kt > 0), inc=(kt == NT - 1))
                            u_, ub = U[part], B("U", part)
                            k.op(AC, lambda u_=u_, pA=pA: nc.scalar.copy(out=u_[:, 0:hw], in_=pA[:, 0:hw]), reads=[pAb], writes=[ub])
                            k.op(AC, lambda u_=u_, pB=pB: nc.scalar.copy(out=u_[:, hw:W], in_=pB[:, 0:hw]), reads=[pBb, ub], writes=[ub])
                            for (colx, isl) in [(0, True), (W - 1, False)]:
                                if ci == 0:
                                    k.op(DV, lambda u_=u_, colx=colx: nc.vector.memset(u_[:, colx:colx + 1], 0.0), reads=[ub], writes=[ub])
                                elif (isl and first_lat) or ((not isl) and last_lat):
                                    fcol = 4 if isl else 5
                                    k.op(DV, lambda u_=u_, colx=colx, fcol=fcol: nc.vector.tensor_scalar(out=u_[:, colx:colx + 1], in0=u_[:, colx:colx + 1],
                                                                                                       scalar1=flg[:, fcol:fcol + 1], scalar2=None, op0=ALU.mult),
                                         reads=[ub, CB], writes=[ub])
                            cv, cvb = tmp()
                            k.op(DV, lambda cv=cv, u_=u_, ct=ct: nc.vector.tensor_scalar(out=cv[:, 0:n], in0=u_[:, 1:n + 1], scalar1=pvo("fdw", ct * 3 + 1), scalar2=pvo("fdb", ct),
                                                                                      op0=ALU.mult, op1=ALU.add), reads=[ub, B("pv")], writes=[cvb])
                            k.op(DV, lambda cv=cv, u_=u_, ct=ct: nc.vector.scalar_tensor_tensor(out=cv[:, 0:n], in0=u_[:, 0:n], scalar=pvo("fdw", ct * 3), in1=cv[:, 0:n],
                                                                                             op0=ALU.mult, op1=ALU.add), reads=[ub, B("pv"), cvb], writes=[cvb])
                            k.op(DV, lambda cv=cv, u_=u_, ct=ct: nc.vector.scalar_tensor_tensor(out=cv[:, 0:n], in0=u_[:, 2:n + 2], scalar=pvo("fdw", ct * 3 + 2), in1=cv[:, 0:n],
                                                                                             op0=ALU.mult, op1=ALU.add), reads=[ub, B("pv"), cvb], writes=[cvb])
                            if part == 0:
                                k.op(AC, lambda cv=cv: nc.scalar.activation(out=gsil[:, 0:n], in_=cv[:, 0:n], func=ACT.Silu), reads=[cvb], writes=[B("gsil")])
                            else:
                                k.op(DV, lambda cv=cv, ft=ft: nc.vector.tensor_tensor(out=act_sb[:, ft, 0:n], in0=cv[:, 0:n], in1=gsil[:, 0:n], op=ALU.mult),
                                     reads=[cvb, B("gsil"), B("act_sb")], writes=[B("act_sb")])
                    for ct in range(NT):
                        w_, wbb = wload("wdn", l, ct, c.FT)
                        ps_, pb = ps_next()
                        for kt in range(c.FT):
                            k.op(PE, lambda ps_=ps_, w_=w_, kt=kt: nc.tensor.matmul(ps_[:, 0:n], lhsT=w_[:, kt * P:(kt + 1) * P], rhs=act_sb[:, kt, 0:n],
                                                                                  start=(kt == 0), stop=(kt == c.FT - 1)),
                                 reads=[wbb, B("act_sb")], writes=[pb], pe_acc=(kt > 0), inc=(kt == c.FT - 1))
                        xr, xrb = xres[ct % 2], B("xres", ct % 2)
                        k.dma(SP, xr[:, 0:n], X[ct * P:(ct + 1) * P, t0:t0 + n], reads=[B(XN, ci, ct)], writes=[xrb])
                        xn, xnb = xnew[ct % 2], B("xnew", ct % 2)
                        k.op(DV, lambda xn=xn, ps_=ps_, xr=xr, ct=ct: nc.vector.scalar_tensor_tensor(out=xn[:, 0:n], in0=ps_[:, 0:n], scalar=mod(l, v, 5, ct), in1=xr[:, 0:n],
                                                                                                  op0=ALU.mult, op1=ALU.add), reads=[pb, xrb, MS], writes=[xnb])
                        k.dma(PO, Y[ct * P:(ct + 1) * P, t0:t0 + n], xn[:, 0:n], reads=[xnb], writes=[B(YN, ci, ct)])
            cur, nxt = nxt, cur

        X, XN = cur
        with Scope() as ss:
            wk = mk_norm_bufs(ss)
            tmp = wk[3]
            for ci in lat:
                t0, n = c.chunks[ci]

                def ydst(t, tm, tb, t0=t0, n=n):
                    o_, ob_ = tmp()
                    k.op(DV, lambda: nc.vector.tensor_scalar(out=o_[:, 0:n], in0=tm[:, 0:n], scalar1=fin_g[:, t:t + 1], scalar2=None, op0=ALU.mult),
                         reads=[tb, CB], writes=[ob_])
                    k.dma(PO, yout[t * P:(t + 1) * P, t0 - CTX:t0 - CTX + n], o_[:, 0:n], reads=[ob_], writes=[B("yout")])
                norm_mod(ss, [xpiece(X, XN, ci, t0, t0 + n)], n, 0, 0, 0, 0, None, wk, ydst=ydst)
        k.barrier()
    return nc


def _fm(vec, nt):
    return np.ascontiguousarray(vec.reshape(nt, P).T)


def _tile_major(w):
    K_, N_ = w.shape
    return np.ascontiguousarray(w.reshape(K_ // P, P, N_ // P, P).transpose(2, 1, 0, 3)).reshape(-1)


def rope_tables(cfg, half):
    c = cfg
    t = np.arange(c.SL, dtype=np.int64) + half * c.SL
    row = (t // c.GW).astype(np.float32)
    col = (t % c.GW).astype(np.float32)
    hh = P // 2
    inv = (np.float32(10000.0) ** (-np.arange(0, hh, 2, dtype=np.float32) / np.float32(hh))).astype(np.float32)
    ang = np.concatenate([row[:, None] * inv, col[:, None] * inv], axis=-1).astype(np.float32)
    cs, sn = np.cos(ang).astype(np.float32), np.sin(ang).astype(np.float32)
    C = np.ones((P, c.T), np.float32)
    S = np.zeros((P, c.T), np.float32)
    C[:, c.CTX:] = np.repeat(cs.T, 2, axis=0)
    S[:, c.CTX:] = np.repeat(sn.T, 2, axis=0)
    return C, S


def nbr_mask(cfg, half):
    c = cfg
    base = half * c.RL
    m = np.full((c.NBLK, 2, 64, 8, 8, 64), NEG, np.float32)
    wq = np.arange(64)
    cstart = np.clip(wq - 8, 0, 64 - 16)
    wk = np.arange(64)
    colok = (wk[:, None] >= cstart[None, :]) & (wk[:, None] < cstart[None, :] + 16)
    for bi in range(c.NBLK):
        for a in range(8):
            r = base + 8 * bi + a
            r0 = int(np.clip(r - 4, 0, c.ROWS - 8))
            for j in range(8):
                for b in range(2):
                    rk = base + 8 * bi + 2 * j + b - 4
                    if r0 <= rk < r0 + 8:
                        m[bi, b, :, j, a, :] = np.where(colok, 0.0, NEG)
    return m.reshape(c.NBLK * P, 8 * 512).astype(ml_dtypes.bfloat16)


def prep_inputs(cfg, inp):
    c = cfg
    L, D, NT = c.L, c.D, c.NT
    f = lambda a: np.asarray(a, dtype=np.float32)
    x, cc, ctx, c_ctx = f(inp["x"]), f(inp["c"]), f(inp["ctx"]), f(inp["c_ctx"])
    consts = np.zeros((P, 3 * P), np.float32)
    consts[:, 0:P] = np.eye(P, dtype=np.float32)
    for i in range(P // 2):
        consts[2 * i + 1, P + 2 * i] = -1.0
        consts[2 * i, P + 2 * i + 1] = 1.0
    consts[:, 2 * P:] = 1.0
    c5 = np.concatenate([cc, c_ctx[None]], 0)
    c5T = np.ascontiguousarray(c5.reshape(5, NT, P).transpose(2, 1, 0)).reshape(P, NT * 5)
    pvec = np.zeros((L, P, c.NP), np.float32)
    for l in range(L):
        pvec[l, :, c.pv["n1g"]:c.pv["n1g"] + NT] = _fm(f(inp["norm1_g"])[l], NT)
        pvec[l, :, c.pv["n2g"]:c.pv["n2g"] + NT] = _fm(f(inp["norm2_g"])[l], NT)
        pvec[l, :, c.pv["qg"]] = f(inp["a_qn_g"])[l]
        pvec[l, :, c.pv["kg"]] = f(inp["a_kn_g"])[l]
        bd = f(inp["b_dw_w"])[l]
        pvec[l, :, c.pv["bdw"]:c.pv["bdw"] + c.BT * 31] = bd.reshape(31, c.BT, P).transpose(2, 1, 0).reshape(P, c.BT * 31)
        for nm, key in [("bdb", "b_dw_b"), ("blg", "b_ln_g"), ("blb", "b_ln_b"), ("bpb", "b_pw_b")]:
            pvec[l, :, c.pv[nm]:c.pv[nm] + c.BT] = _fm(f(inp[key])[l], c.BT)
        fd = f(inp["ffn_dw_w"])[l]
        pvec[l, :, c.pv["fdw"]:c.pv["fdw"] + 2 * c.FT * 3] = fd.reshape(3, 2 * c.FT, P).transpose(2, 1, 0).reshape(P, 2 * c.FT * 3)
        pvec[l, :, c.pv["fdb"]:c.pv["fdb"] + 2 * c.FT] = _fm(f(inp["ffn_dw_b"])[l], 2 * c.FT)
    pvec = pvec.reshape(L * P, c.NP)
    fing = _fm(f(inp["final_g"]), NT)
    rpb = f(inp["c_rpb"])
    wk, wq = np.arange(64)[:, None], np.arange(64)[None, :]
    cidx = np.clip(wk - wq + 15, 0, 30)
    tab = np.zeros((L, c.CH, 64, 23, 64), np.float32)
    for dp in range(23):
        dl = 11 - dp
        if -7 <= dl <= 7:
            tab[:, :, :, dp, :] = rpb[:, :, dl + 7, :][:, :, cidx]
    rpbT = tab.reshape(L * c.CH * 64, 23 * 64)
    wflat = {}
    for n, key in [("win", "w_in"), ("wout", "w_out"), ("wup", "ffn_w_up"), ("wdn", "ffn_w_down"), ("wpw", "b_pw_w")]:
        w = f(inp[key])
        rows = w.shape[1] * w.shape[2] // 8 // 1024
        blk = wblk(rows)
        wflat[n] = np.stack([_tile_major(w[l]).reshape(rows // blk, 8, blk * 1024).transpose(1, 0, 2).reshape(8, -1) for l in range(L)], 0)
    ada_w, ada_b = f(inp["ada_w"]), f(inp["ada_b"])
    maps = []
    for r in range(8):
        b, half = r // 2, r % 2
        m = {}
        xT = np.empty((D, c.T), np.float32)
        xT[:, :c.CTX] = ctx[b].T
        xT[:, c.CTX:] = x[b, half * c.SL:(half + 1) * c.SL].T
        m["xin"] = xT
        m["c5T"] = c5T
        m["ada_s"] = np.ascontiguousarray(ada_w[:, :, r * c.MC:(r + 1) * c.MC]).reshape(L * D, c.MC)
        m["adab5"] = np.ascontiguousarray(np.broadcast_to(ada_b[None, :, r * c.MC:(r + 1) * c.MC], (5, L, c.MC))).reshape(5, L * c.MC)
        fl = np.zeros((P, 8), np.float32)
        fl[:, b] = 1.0
        fl[:, 4] = 1.0 if half == 1 else 0.0
        fl[:, 5] = 1.0 if half == 0 else 0.0
        m["flags"] = fl
        m["pvec"] = pvec
        m["fing"] = fing
        C_, S_ = rope_tables(c, half)
        m["cosT"], m["sinT"] = C_, S_
        m["consts"] = consts
        m["nmask"] = nbr_mask(c, half)
        m["rpbT"] = rpbT
        for n in wflat:
            m[n + "_s"] = np.ascontiguousarray(wflat[n][:, r, :]).reshape(-1, 1024)
        maps.append(m)
    return maps


_NC_CACHE = {}


def run(cfg, inp):
    key = (cfg.D, cfg.SEQ, cfg.L)
    if key not in _NC_CACHE:
        _NC_CACHE[key] = build(cfg)
    nc = _NC_CACHE[key]
    maps = prep_inputs(cfg, inp)
    res = run_bass_kernel_spmd(nc, maps, core_ids=list(range(8)))
    out = np.empty((cfg.B, cfg.SEQ, cfg.D), np.float32)
    for r in range(8):
        b, half = r // 2, r % 2
        out[b, half * cfg.SL:(half + 1) * cfg.SL, :] = res.results[r]["yout"].T
    return out


def kernel(**inputs):
    return run(Cfg(), inputs)
```

```python
import numpy as np
import ml_dtypes
from contextlib import ExitStack
import concourse.bass as bass
import concourse.mybir as mybir
from concourse.bass_utils import run_bass_kernel_spmd

F32 = mybir.dt.float32
BF16 = mybir.dt.bfloat16
ACT = mybir.ActivationFunctionType
ALU = mybir.AluOpType
NEG = -1e30
EPS = 1e-6
P = 128


class _Stop(Exception):
    pass


class Cfg:
    stop = None

    def __init__(s, D=4096, SEQ=4096, L=4):
        s.D, s.SEQ, s.L = D, SEQ, L
        s.B, s.CTX, s.GW = 4, 256, 64
        s.NT = D // P
        NH = D // P
        s.AH = 3 * NH // 8
        s.AKV = s.AH // 3
        s.CH = 3 * NH // 8
        s.BCH = D - (s.AH + s.CH) * P
        s.BT = s.BCH // P
        s.DFF = 11 * D // 8
        s.FT = s.DFF // P
        s.INW = s.AH * P + 2 * s.AKV * P + 2 * s.BCH + 3 * s.CH * P
        s.CTI = s.INW // P
        s.SL = SEQ // 2
        s.T = s.CTX + s.SL
        s.ROWS = SEQ // s.GW
        s.RL = s.SL // s.GW
        s.NBLK = s.SL // 512
        s.MC = 6 * D // 8
        s.o_qa = 0
        s.o_ka = s.o_qa + s.AH
        s.o_va = s.o_ka + s.AKV
        s.o_ba = s.o_va + s.AKV
        s.o_bg = s.o_ba + s.BT
        s.o_qn = s.o_bg + s.BT
        s.o_kn = s.o_qn + s.CH
        s.o_vn = s.o_kn + s.CH
        o = 0
        s.pv = {}
        for name, n in [("n1g", s.NT), ("n2g", s.NT), ("qg", 1), ("kg", 1), ("bdw", s.BT * 31), ("bdb", s.BT),
                        ("blg", s.BT), ("blb", s.BT), ("bpb", s.BT), ("fdw", 2 * s.FT * 3), ("fdb", 2 * s.FT)]:
            s.pv[name] = o
            o += n
        s.NP = o
        s.chunks = [(0, s.CTX)] + [(s.CTX + 512 * i, 512) for i in range(s.NBLK)]


class Sem:
    def __init__(s, h):
        s.h, s.count = h, 0


class Buf:
    def __init__(s, name):
        s.name, s.w, s.r = name, None, []


class Eng:
    def __init__(s, e, sems):
        s.e, s.sems, s.si, s.seen = e, sems, 0, {}
        s.sem = sems[0]

    def rotate(s):
        if s.sem.count >= 30000:
            s.si += 1
            s.sem = s.sems[s.si]


class K:
    def __init__(s, nc, stack, n_eng_sems=10, n_dma_sems=20):
        s.nc = nc
        mk = lambda nm: Sem(stack.enter_context(nc.semaphore(nm)))
        s.pe = Eng(nc.tensor, [mk(f"pe{i}") for i in range(5)])
        s.act = Eng(nc.scalar, [mk(f"ac{i}") for i in range(4)])
        s.dve = Eng(nc.vector, [mk(f"dv{i}") for i in range(6)])
        s.pool = Eng(nc.gpsimd, [mk(f"po{i}") for i in range(2)])
        s.sp = Eng(nc.sync, [mk("spx")])
        s.dsem = {id(s.sp): [mk(f"ds{i}") for i in range(16)],
                  id(s.pool): [mk(f"dp{i}") for i in range(28)]}
        s.dsi = {k: 0 for k in s.dsem}
        s.ccsem = [mk(f"cc{i}") for i in range(4)]
        s.cci = 0
        s.bufs = {}

    def B(s, *key):
        if key not in s.bufs:
            s.bufs[key] = Buf(str(key))
        return s.bufs[key]

    def _wait(s, eng, reads, writes, pe_acc=False):
        deps = {}

        def add(tok):
            if tok is None:
                return
            sem, val = tok
            if deps.get(id(sem), (None, 0))[1] < val:
                deps[id(sem)] = (sem, val)
        for b in reads:
            add(b.w)
        for b in writes:
            if not (pe_acc and b.w is not None and b.w[0] in eng.sems):
                add(b.w)
            for t in b.r:
                add(t)
        for sem, val in deps.values():
            if pe_acc and sem in eng.sems:
                continue
            if eng.seen.get(id(sem), 0) < val:
                eng.e.wait_ge(sem.h, val)
                eng.seen[id(sem)] = val

    def _done(s, tok, reads, writes):
        for b in reads:
            b.r.append(tok)
            if len(b.r) > 64:
                b.r = b.r[-64:] if False else b.r
        for b in writes:
            b.w, b.r = tok, []

    def op(s, eng, fn, reads=(), writes=(), inc=True, pe_acc=False):
        s._wait(eng, reads, writes, pe_acc)
        ins = fn()
        if inc:
            eng.sem.count += 1
            ins.then_inc(eng.sem.h, 1)
            tok = (eng.sem, eng.sem.count)
            s._done(tok, reads, writes)
            eng.rotate()
        else:
            tok = (eng.sem, eng.sem.count + 1)
            s._done(tok, reads, writes)
        return ins

    def dma(s, eng, out, in_, reads=(), writes=(), slow=False):
        s._wait(eng, reads, writes)
        pool = s.dsem[id(eng)]
        sem = pool[s.dsi[id(eng)] % len(pool)]
        s.dsi[id(eng)] += 1
        if sem.count and eng.seen.get(id(sem), 0) < sem.count:
            eng.e.wait_ge(sem.h, sem.count)
            eng.seen[id(sem)] = sem.count
        ins = eng.e.dma_start(out=out, in_=in_, allow_slow_non_contiguous=True) if slow else eng.e.dma_start(out=out, in_=in_)
        sem.count += 16
        ins.then_inc(sem.h, 16)
        s._done((sem, sem.count), reads, writes)

    def allgather(s, groups, in_ap, out_ap, reads=(), writes=()):
        eng = s.pool
        s._wait(eng, reads, writes)
        sem = s.ccsem[0]
        ins = eng.e.collective_compute("AllGather", ALU.bypass, replica_groups=groups,
                                       ins=[in_ap.opt()], outs=[out_ap.opt()])
        sem.count += 1
        ins.then_inc(sem.h)
        s._done((sem, sem.count), reads, writes)

    def wait_all(s, eng, bufs):
        s._wait(eng, bufs, ())

    def barrier(s):
        sems = []
        for e in (s.pe, s.act, s.dve, s.pool, s.sp):
            sems += e.sems
        for v in s.dsem.values():
            sems += v
        sems += s.ccsem
        for e in (s.pe, s.act, s.dve, s.pool, s.sp):
            for sem in sems:
                if sem.count and e.seen.get(id(sem), 0) < sem.count:
                    e.e.wait_ge(sem.h, sem.count)
                    e.seen[id(sem)] = sem.count


PAIRS = [[0, 1], [2, 3], [4, 5], [6, 7]]
ALL8 = [list(range(8))]
QUADS = [[0, 1, 2, 3], [4, 5, 6, 7]]
P4 = [[0, 4], [1, 5], [2, 6], [3, 7]]


def wblk(rows):
    b = min(128, rows)
    while rows % b:
        b -= 1
    return b


def build(cfg):
    try:
        return _build(cfg)
    except _Stop as e:
        return e.args[0]


def _build(cfg):
    c = cfg
    D, NT, T, SL, CTX, L = c.D, c.NT, c.T, c.SL, c.CTX, c.L
    nc = bass.Bass("TRN2", target_bir_lowering=False)
    din = lambda n, sh, dt=F32: nc.dram_tensor(n, sh, dt, kind="ExternalInput")
    dsc = lambda n, sh, dt=F32: nc.dram_tensor(n, sh, dt)
    xin = din("xin", [D, T])
    c5T = din("c5T", [P, NT * 5])
    ada_s = din("ada_s", [L * D, c.MC])
    adab5 = din("adab5", [5, L * c.MC])
    flags = din("flags", [P, 8])
    pvec = din("pvec", [L * P, c.NP])
    fing = din("fing", [P, NT])
    cosT = din("cosT", [P, T])
    sinT = din("sinT", [P, T])
    consts = din("consts", [P, 3 * P])
    nmask = din("nmask", [c.NBLK * P, 8 * 512], BF16)
    rpbT = din("rpbT", [L * c.CH * 64, 23 * 64])
    wspec = {"win": (D, c.INW), "wout": (D, D), "wup": (D, 2 * c.DFF), "wdn": (c.DFF, D), "wpw": (c.BCH, c.BCH)}
    wsh, wbf, wfull, wquad = {}, {}, {}, {}
    for n, (kk, nn) in wspec.items():
        rows = kk * nn // 8 // 1024
        wsh[n] = din(n + "_s", [L * rows, 1024])
        wbf[n] = [dsc(f"{n}_b{l}", [rows, 1024], BF16) for l in range(L)]
        wfull[n] = [dsc(f"{n}_f{l}", [8 * rows, 1024], BF16) for l in range(L)]
        wquad[n] = [dsc(f"{n}_q{l}", [4 * rows, 1024], BF16) for l in range(L)]
    yout = nc.dram_tensor("yout", [D, SL], F32, kind="ExternalOutput")
    xA = dsc("xA", [D, T]); xB_ = dsc("xB", [D, T])
    qaT = dsc("qaT", [c.AH * P, T], BF16)
    kaT = dsc("kaT", [c.AKV * P, T], BF16)
    va = dsc("va", [c.AKV * T, P], BF16)
    qnT = dsc("qnT", [c.CH * P, T], BF16)
    knT = dsc("knT", [c.CH * P, T], BF16)
    vn = dsc("vn", [c.CH * T, P], BF16)
    ZW = 15 + CTX + 15 + 15 + SL + 15
    zp = dsc("zp", [c.BCH, ZW])
    catT = dsc("catT", [D, T], BF16)
    kaS = dsc("kaS", [c.AKV * P, SL], BF16); kaG = dsc("kaG", [2 * c.AKV * P, SL], BF16)
    vaS = dsc("vaS", [c.AKV * SL, P], BF16); vaG = dsc("vaG", [2 * c.AKV * SL, P], BF16)
    knS = dsc("knS", [c.CH * P, 512], BF16); knG = dsc("knG", [2 * c.CH * P, 512], BF16)
    vnS = dsc("vnS", [c.CH * 512, P], BF16); vnG = dsc("vnG", [2 * c.CH * 512, P], BF16)
    zS = dsc("zS", [c.BCH, 32]); zG = dsc("zG", [2 * c.BCH, 32])
    xeS = dsc("xeS", [P, 2 * NT]); xeG = dsc("xeG", [2 * P, 2 * NT])
    modS = dsc("modS", [5, L * c.MC]); modG = dsc("modG", [8 * 5, L * c.MC]); modQ = dsc("modQ", [4 * 5, L * c.MC])

    with ExitStack() as st:
        k = K(nc, st)
        PE, AC, DV, PO, SP = k.pe, k.act, k.dve, k.pool, k.sp
        B = k.B

        uid = [0]

        class Scope:
            def __enter__(s):
                s.st = ExitStack()
                s.st.__enter__()
                uid[0] += 1
                u = uid[0]
                return lambda n, sh, dt=F32: s.st.enter_context(nc.sbuf_tensor(f"{n}_u{u}", sh, dt))

            def __exit__(s, *a):
                k.barrier()
                return s.st.__exit__(*a)

        sb = lambda n, sh, dt=F32: st.enter_context(nc.sbuf_tensor(n, sh, dt))
        psb = [st.enter_context(nc.psum_tensor(f"ps{i}", [P, 512], F32)) for i in range(7)]
        pst = st.enter_context(nc.psum_tensor("pst", [P, 4 * P], BF16))
        rot_i = [0]

        def ps_next():
            i = rot_i[0] % 4
            rot_i[0] += 1
            return psb[i], B("ps", i)
        PS_O, PS_DEN, PS_AUX = (psb[4], B("ps", 4)), (psb[5], B("ps", 5)), (psb[6], B("ps", 6))
        PST = (pst, B("pst"))
        ident_f = sb("ident_f", [P, P]); rot_f = sb("rot_f", [P, P]); ones_f = sb("ones_f", [P, P])
        ident_b = sb("ident_b", [P, P], BF16); ones_b = sb("ones_b", [P, P], BF16)
        flg = sb("flg", [P, 8]); fin_g = sb("fin_g", [P, NT])
        zero_f = sb("zero_f", [P, 16])
        msb = sb("msb", [P, L, 2, 6 * NT])
        pv = sb("pv", [P, L, c.NP])
        qgs = sb("qgs", [P, L])
        CB = B("consts")
        k.dma(SP, ident_f[:], consts[:, 0:P], writes=[CB])
        k.dma(SP, rot_f[:], consts[:, P:2 * P], writes=[CB])
        k.dma(SP, ones_f[:], consts[:, 2 * P:3 * P], writes=[CB])
        k.dma(PO, ident_b[:], consts[:, 0:P], writes=[CB])
        k.dma(PO, ones_b[:], consts[:, 2 * P:3 * P], writes=[CB])
        k.dma(SP, flg[:], flags[:, :], writes=[CB])
        k.dma(SP, fin_g[:], fing[:, :], writes=[CB])
        k.op(DV, lambda: nc.vector.memset(zero_f[:], 0.0), writes=[CB])

        def wpieces(l):
            for n, (kk, nn) in wspec.items():
                rows = kk * nn // 8 // 1024
                blk = wblk(rows)
                for pi in range(rows // blk):
                    yield n, rows, blk, pi
        def weight_setup(l):
            for n, rows, blk, pi in wpieces(l):
                r0 = pi * blk
                k.dma(PO, wbf[n][l][r0:r0 + blk, :], wsh[n][l * rows + r0:l * rows + r0 + blk, :], writes=[B("wbfp", n, l, pi)])
            for nm in wspec:
                for n, rows, blk, pi in wpieces(l):
                    if n != nm:
                        continue
                    r0, q0 = pi * blk, pi * 4 * blk
                    k.allgather(QUADS, wbf[n][l][r0:r0 + blk, :], wquad[n][l][q0:q0 + 4 * blk, :],
                                reads=[B("wbfp", n, l, pi)], writes=[B("wq", n, l, pi)])
                for n, rows, blk, pi in wpieces(l):
                    if n != nm:
                        continue
                    q0 = pi * 4 * blk
                    k.allgather(P4, wquad[n][l][q0:q0 + 4 * blk, :], wfull[n][l][2 * q0:2 * q0 + 8 * blk, :],
                                reads=[B("wq", n, l, pi)], writes=[B("wf", n, l)])
        weight_setup(0)

        def ck(name):
            if c.stop == name:
                k.barrier()
                raise _Stop(nc)
        ck('weights')

        def wtile(n, l, ct, KT):
            v = wfull[n][l].ap().rearrange("r j -> (r j)").rearrange("(c p q) -> c p q", p=P, q=KT * P)
            return v[ct]

        ng = 1
        while c.MC % ng or c.MC // ng > 512:
            ng += 1
        gw = c.MC // ng
        assert ng <= 6
        NJ = c.MC // P
        NR = 8 * NJ
        with Scope() as ss:
            c5 = ss("c5", [P, NT * 5])
            adat = [ss(f"adat{i}", [P, c.MC]) for i in range(2)]
            modrow = ss("modrow", [5, c.MC]); adabs = ss("adabs", [5, c.MC])
            modT = ss("modT", [P, L, 5, NR])
            mrow = [ss(f"mrow{i}", [P, P]) for i in range(2)]
            k.dma(SP, c5[:], c5T[:, :], writes=[B("c5")])
            k.op(AC, lambda: nc.scalar.activation(out=c5[:], in_=c5[:], func=ACT.Silu), reads=[B("c5")], writes=[B("c5")])
            for l in range(L):
                k.dma(SP, adabs[:], adab5[:, l * c.MC:(l + 1) * c.MC], writes=[B("adabs")])
                for kt in range(NT):
                    at, ab = adat[kt % 2], B("adat", kt % 2)
                    k.dma(SP, at[:], ada_s[l * D + kt * P:l * D + (kt + 1) * P, :], writes=[ab])
                    for g in range(ng):
                        psg, pb = psb[g], B("ps", g)
                        k.op(PE, lambda psg=psg, at=at, g=g, kt=kt: nc.tensor.matmul(
                            psg[0:5, 0:gw], lhsT=c5[:, kt * 5:(kt + 1) * 5], rhs=at[:, g * gw:(g + 1) * gw],
                            start=(kt == 0), stop=(kt == NT - 1)),
                            reads=[ab, B("c5")], writes=[pb], pe_acc=(kt > 0))
                for g in range(ng):
                    psg, pb = psb[g], B("ps", g)
                    k.op(DV, lambda psg=psg, g=g: nc.vector.tensor_tensor(
                        out=modrow[:, g * gw:(g + 1) * gw], in0=psg[0:5, 0:gw], in1=adabs[:, g * gw:(g + 1) * gw], op=ALU.add),
                        reads=[pb, B("adabs")], writes=[B("modrow")])
                k.dma(PO, modS[:, l * c.MC:(l + 1) * c.MC], modrow[:], reads=[B("modrow")], writes=[B("modS")])
            k.allgather(QUADS, modS[:, :], modQ[:, :], reads=[B("modS")], writes=[B("modQ")])
            k.allgather(P4, modQ[:, :], modG[:, :], reads=[B("modQ")], writes=[B("modG")])
            cnt = 0
            mg = modG.ap().rearrange("(k r) (l j q) -> r l k j q", r=5, l=L, q=P)
            for l in range(L):
                for r in range(5):
                    for h0 in range(0, NR, P):
                        nrow = min(P, NR - h0)
                        mr, mb = mrow[cnt % 2], B("mrow", cnt % 2)
                        cnt += 1
                        for kk_ in range(8):
                            lo, hi = max(h0, kk_ * NJ), min(h0 + nrow, (kk_ + 1) * NJ)
                            if lo < hi:
                                k.dma(SP, mr[lo - h0:hi - h0, :], mg[r, l, kk_, lo - kk_ * NJ:hi - kk_ * NJ, :],
                                      reads=[B("modG")], writes=[mb])
                        k.op(PE, lambda mr=mr, nrow=nrow: nc.tensor.transpose(PS_AUX[0][:, 0:nrow], mr[0:nrow, :], ident_f[0:nrow, 0:nrow]),
                             reads=[mb, CB], writes=[PS_AUX[1]])
                        k.op(DV, lambda l=l, r=r, h0=h0, nrow=nrow: nc.vector.tensor_copy(out=modT[:, l, r, h0:h0 + nrow], in_=PS_AUX[0][:, 0:nrow]),
                             reads=[PS_AUX[1]], writes=[B("modT")])
            for l in range(L):
                k.dma(SP, pv[:, l, :], pvec[l * P:(l + 1) * P, :], writes=[B("pv")])
            for l in range(L):
                k.op(DV, lambda l=l: nc.vector.tensor_scalar(out=msb[:, l, 0, :], in0=modT[:, l, 0, :], scalar1=flg[:, 0:1], scalar2=None, op0=ALU.mult),
                     reads=[B("modT"), CB], writes=[B("msb")])
                for b in range(1, 4):
                    k.op(DV, lambda l=l, b=b: nc.vector.scalar_tensor_tensor(out=msb[:, l, 0, :], in0=modT[:, l, b, :], scalar=flg[:, b:b + 1],
                                                                            in1=msb[:, l, 0, :], op0=ALU.mult, op1=ALU.add),
                         reads=[B("modT"), B("msb"), CB], writes=[B("msb")])
                k.op(DV, lambda l=l: nc.vector.tensor_copy(out=msb[:, l, 1, :], in_=modT[:, l, 4, :]), reads=[B("modT"), B("msb")], writes=[B("msb")])
                for v in range(2):
                    for (mi, gname) in [(1, "n1g"), (4, "n2g")]:
                        k.op(DV, lambda l=l, v=v, mi=mi, gname=gname: nc.vector.scalar_tensor_tensor(
                            out=msb[:, l, v, mi * NT:(mi + 1) * NT], in0=msb[:, l, v, mi * NT:(mi + 1) * NT], scalar=1.0,
                            in1=pv[:, l, c.pv[gname]:c.pv[gname] + NT], op0=ALU.add, op1=ALU.mult),
                            reads=[B("msb"), B("pv")], writes=[B("msb")])
                k.op(DV, lambda l=l: nc.vector.tensor_scalar(out=qgs[:, l:l + 1], in0=pv[:, l, c.pv["qg"]:c.pv["qg"] + 1], scalar1=float(P) ** -0.5,
                                                            scalar2=None, op0=ALU.mult), reads=[B("pv")], writes=[B("qgs")])
        ck('mods')
        MS = B("msb")

        def mod(l, v, mi, t):
            return msb[:, l, v, mi * NT + t:mi * NT + t + 1]

        nch = len(c.chunks)
        lat = list(range(1, nch))
        XT = lambda nm, ci, t: B(nm, ci, t)
        XC = lambda nm, ci: [B(nm, ci, t) for t in range(NT)]
        for ci, (t0, n) in enumerate(c.chunks):
            k.dma(PO, xA[:, t0:t0 + n], xin[:, t0:t0 + n], writes=XC("xA", ci))
        for j in range(c.BT):
            k.dma(PO, zp[j * P:(j + 1) * P, 0:15], zero_f[:, 0:15], reads=[CB], writes=[B("zp", j, "pad")])
            k.dma(PO, zp[j * P:(j + 1) * P, 15 + CTX:15 + CTX + 15], zero_f[:, 0:15], reads=[CB], writes=[B("zp", j, "pad")])

        def xpiece(xt_, nm, ci, a, b):
            return (lambda gi, G: xt_[:, a:b].rearrange("(t p) n -> p t n", p=P)[:, gi:gi + G, :],
                    lambda gi, G: [B(nm, ci, t) for t in range(gi, gi + G)], b - a)

        def norm_mod(ss, pieces, W, l, v, mA, mS, hdst, wk, ydst=None):
            xt, sq, rstd, tmp = wk
            G = 2
            halves = [(0, W)] if W <= 512 else [(0, W // 2), (W // 2, W)]

            def load(gi):
                xb_, xbb = xt[(gi // G) % 2], B("xt", (gi // G) % 2)
                col = 0
                for (apf, bf, w) in pieces:
                    k.dma(SP, xb_[:, :, col:col + w], apf(gi, G), reads=bf(gi, G), writes=[xbb], slow=(w == 1))
                    col += w
                return xb_, xbb
            for gi in range(0, NT, G):
                xb_, xbb = load(gi)
                for tt in range(G):
                    t = gi + tt
                    s_, sbb = sq[t % 2], B("sq", t % 2)
                    k.op(AC, lambda xb_=xb_, tt=tt, s_=s_: nc.scalar.activation(out=s_[:, 0:W], in_=xb_[:, tt, 0:W], func=ACT.Square),
                         reads=[xbb], writes=[sbb])
                    for hi, (a, b) in enumerate(halves):
                        pp = [PS_AUX, PS_DEN][hi]
                        k.op(PE, lambda pp=pp, s_=s_, a=a, b=b, t=t: nc.tensor.matmul(pp[0][:, 0:b - a], lhsT=ones_b[:], rhs=s_[:, a:b],
                                                                                  start=(t == 0), stop=(t == NT - 1)),
                             reads=[sbb, CB], writes=[pp[1]], pe_acc=(t > 0))
            for hi, (a, b) in enumerate(halves):
                pp = [PS_AUX, PS_DEN][hi]
                k.op(DV, lambda pp=pp, a=a, b=b: nc.vector.tensor_scalar(out=rstd[:, a:b], in0=pp[0][:, 0:b - a], scalar1=1.0 / D, scalar2=EPS,
                                                                       op0=ALU.mult, op1=ALU.add), reads=[pp[1], B("rstd")], writes=[B("rstd")])
            k.op(AC, lambda: nc.scalar.sqrt(out=rstd[:, 0:W], in_=rstd[:, 0:W]), reads=[B("rstd")], writes=[B("rstd")])
            k.op(DV, lambda: nc.vector.reciprocal(out=rstd[:, 0:W], in_=rstd[:, 0:W]), reads=[B("rstd")], writes=[B("rstd")])
            for gi in range(0, NT, G):
                xb_, xbb = load(gi)
                for tt in range(G):
                    t = gi + tt
                    tm, tb = tmp()
                    k.op(DV, lambda xb_=xb_, tt=tt, tm=tm: nc.vector.tensor_tensor(out=tm[:, 0:W], in0=xb_[:, tt, 0:W], in1=rstd[:, 0:W], op=ALU.mult),
                         reads=[xbb, B("rstd")], writes=[tb])
                    if ydst is None:
                        k.op(AC, lambda tm=tm, t=t: nc.scalar.activation(out=hdst[:, t, 0:W], in_=tm[:, 0:W], func=ACT.Identity,
                                                                       bias=mod(l, v, mS, t), scale=mod(l, v, mA, t)),
                             reads=[tb, MS], writes=[B("hsb")])
                    else:
                        ydst(t, tm, tb)

        def mk_norm_bufs(ss):
            xt = [ss(f"xt{i}", [P, 2, 514]) for i in range(2)]
            sq = [ss(f"sq{i}", [P, 514], BF16) for i in range(2)]
            rstd = ss("rstd", [P, 514])
            tmpf = [ss(f"tmpf{i}", [P, 514]) for i in range(3)]
            tc_ = [0]

            def tmp():
                i = tc_[0] % 3
                tc_[0] += 1
                return tmpf[i], B("tmpf", i)
            return (xt, sq, rstd, tmp)

        def mk_w(ss, KT, nb=4):
            wb_ = [ss(f"wb{i}", [P, KT * P], BF16) for i in range(nb)]
            wr = [0]

            def wload(n, l, ct, KT_):
                i = wr[0] % nb
                wr[0] += 1
                k.dma(SP, wb_[i][:, 0:KT_ * P], wtile(n, l, ct, KT_), reads=[B("wf", n, l)], writes=[B("wb", i)])
                return wb_[i], B("wb", i)
            return wload

        def mk_ob(ss):
            ob = [ss(f"ob{i}", [P, 512], BF16) for i in range(3)]
            oc = [0]

            def obuf():
                i = oc[0] % 3
                oc[0] += 1
                return ob[i], B("ob", i)
            return obuf

        cur, nxt = (xA, "xA"), (xB_, "xB")
        for l in range(L):
            last = (l == L - 1)
            X, XN = cur
            pvo = lambda name, j=0: pv[:, l, c.pv[name] + j:c.pv[name] + j + 1]
            with Scope() as ss:
                wk = mk_norm_bufs(ss)
                tmp = wk[3]
                hsb = ss("hsb", [P, NT, 514], BF16)
                wload = mk_w(ss, NT)
                obuf = mk_ob(ss)
                vt = [ss(f"vt{i}", [P, 4 * P], BF16) for i in range(2)]
                sg = ss("sg", [P, c.BT, 512])
                cos_sb = ss("cos_sb", [P, T]); sin_sb = ss("sin_sb", [P, T])
                k.dma(SP, cos_sb[:], cosT[:, :], writes=[B("rope")])
                k.dma(SP, sin_sb[:], sinT[:, :], writes=[B("rope")])
                RB = B("rope")
                xcnt = 0
                for ci, (t0, n) in enumerate(c.chunks):
                    v = 1 if ci == 0 else 0
                    norm_mod(ss, [xpiece(X, XN, ci, t0, t0 + n)], n, l, v, 1, 0, hsb, wk)
                    HB = B("hsb")
                    if l == 0 and ci == 0:
                        ck('n1')
                    order = list(range(c.o_qa, c.o_ba)) + [x_ for j in range(c.BT) for x_ in (c.o_bg + j, c.o_ba + j)] + list(range(c.o_qn, c.CTI))
                    for ct in order:
                        if ci == 0 and last and (ct < c.o_ka or c.o_ba <= ct < c.o_kn):
                            continue
                        w_, wbb = wload("win", l, ct, NT)
                        ck('w0')
                        ps_, pb = ps_next()
                        for kt in range(NT):
                            k.op(PE, lambda ps_=ps_, w_=w_, kt=kt: nc.tensor.matmul(ps_[:, 0:n], lhsT=w_[:, kt * P:(kt + 1) * P], rhs=hsb[:, kt, 0:n],
                                                                                  start=(kt == 0), stop=(kt == NT - 1)),
                                 reads=[wbb, HB], writes=[pb], pe_acc=(kt > 0), inc=(kt == NT - 1))
                        ck('mm0')
                        if ct < c.o_va:
                            isq = ct < c.o_ka
                            qf, qfb = tmp()
                            k.op(DV, lambda qf=qf, ps_=ps_: nc.vector.tensor_copy(out=qf[:, 0:n], in_=ps_[:, 0:n]), reads=[pb], writes=[qfb])
                            ck('h0')
                            s2, s2b = tmp()
                            k.op(DV, lambda s2=s2, qf=qf: nc.vector.tensor_tensor(out=s2[:, 0:n], in0=qf[:, 0:n], in1=qf[:, 0:n], op=ALU.mult), reads=[qfb], writes=[s2b])
                            ck('h1')
                            k.op(PE, lambda s2=s2: nc.tensor.matmul(PS_AUX[0][:, 0:n], lhsT=ones_f[:], rhs=s2[:, 0:n], start=True, stop=True),
                                 reads=[s2b, CB], writes=[PS_AUX[1]])
                            ck('h2')
                            k.op(DV, lambda s2=s2: nc.vector.tensor_scalar(out=s2[:, 0:n], in0=PS_AUX[0][:, 0:n], scalar1=1.0 / P, scalar2=EPS, op0=ALU.mult, op1=ALU.add),
                                 reads=[PS_AUX[1], s2b], writes=[s2b])
                            ck('h3')
                            k.op(AC, lambda s2=s2: nc.scalar.sqrt(out=s2[:, 0:n], in_=s2[:, 0:n]), reads=[s2b], writes=[s2b])
                            k.op(DV, lambda s2=s2: nc.vector.reciprocal(out=s2[:, 0:n], in_=s2[:, 0:n]), reads=[s2b], writes=[s2b])
                            ck('hn')
                            gain = qgs[:, l:l + 1] if isq else pvo("kg")
                            k.op(DV, lambda qf=qf, s2=s2, gain=gain: nc.vector.scalar_tensor_tensor(out=qf[:, 0:n], in0=qf[:, 0:n], scalar=gain, in1=s2[:, 0:n],
                                                                                                 op0=ALU.mult, op1=ALU.mult),
                                 reads=[qfb, s2b, B("qgs"), B("pv")], writes=[qfb])
                            k.op(PE, lambda qf=qf: nc.tensor.matmul(PS_AUX[0][:, 0:n], lhsT=rot_f[:], rhs=qf[:, 0:n], start=True, stop=True),
                                 reads=[qfb, CB], writes=[PS_AUX[1]])
                            k.op(DV, lambda s2=s2: nc.vector.tensor_tensor(out=s2[:, 0:n], in0=PS_AUX[0][:, 0:n], in1=sin_sb[:, t0:t0 + n], op=ALU.mult),
                                 reads=[PS_AUX[1], RB, s2b], writes=[s2b])
                            k.op(DV, lambda qf=qf: nc.vector.tensor_tensor(out=qf[:, 0:n], in0=qf[:, 0:n], in1=cos_sb[:, t0:t0 + n], op=ALU.mult),
                                 reads=[qfb, RB], writes=[qfb])
                            o_, obb = obuf()
                            k.op(DV, lambda o_=o_, qf=qf, s2=s2: nc.vector.tensor_tensor(out=o_[:, 0:n], in0=qf[:, 0:n], in1=s2[:, 0:n], op=ALU.add),
                                 reads=[qfb, s2b], writes=[obb])
                            ck('rope')
                            if isq:
                                k.dma(PO, qaT[ct * P:(ct + 1) * P, t0:t0 + n], o_[:, 0:n], reads=[obb], writes=[B("qa", ct, ci)])
                            else:
                                h_ = ct - c.o_ka
                                k.dma(PO, kaT[h_ * P:(h_ + 1) * P, t0:t0 + n], o_[:, 0:n], reads=[obb], writes=[B("ka", h_, ci)])
                                if ci > 0:
                                    k.dma(PO, kaS[h_ * P:(h_ + 1) * P, t0 - CTX:t0 - CTX + n], o_[:, 0:n], reads=[obb], writes=[B("kaS")])
                        elif ct < c.o_ba or ct >= c.o_vn:
                            isa = ct < c.o_ba
                            h_ = ct - (c.o_va if isa else c.o_vn)
                            o_, obb = obuf()
                            k.op(AC, lambda o_=o_, ps_=ps_: nc.scalar.copy(out=o_[:, 0:n], in_=ps_[:, 0:n]), reads=[pb], writes=[obb])
                            nb = n // P
                            for bi in range(nb):
                                k.op(PE, lambda o_=o_, bi=bi: nc.tensor.transpose(PST[0][:, bi * P:(bi + 1) * P], o_[:, bi * P:(bi + 1) * P], ident_b[:]),
                                     reads=[obb, CB], writes=[PST[1]], pe_acc=(bi > 0), inc=(bi == nb - 1))
                            vt_, vtb = vt[xcnt % 2], B("vt", xcnt % 2)
                            xcnt += 1
                            k.op(DV, lambda vt_=vt_: nc.vector.tensor_copy(out=vt_[:, 0:n], in_=PST[0][:, 0:n]), reads=[PST[1]], writes=[vtb])
                            dstT = va if isa else vn
                            v3 = lambda ap: ap.rearrange("(b p) d -> p b d", p=P)
                            s3 = lambda lo, hi: vt_[:, lo:hi].rearrange("p (b d) -> p b d", d=P)
                            k.dma(PO, v3(dstT[h_ * T + t0:h_ * T + t0 + n, :]), s3(0, n), reads=[vtb], writes=[B("va" if isa else "vn", h_, ci)])
                            if isa and ci > 0:
                                k.dma(PO, v3(vaS[h_ * SL + t0 - CTX:h_ * SL + t0 - CTX + n, :]), s3(0, n), reads=[vtb], writes=[B("vaS")])
                            if (not isa) and ci == 1:
                                k.dma(PO, v3(vnS[h_ * 512:h_ * 512 + 256, :]), s3(0, 256), reads=[vtb], writes=[B("vnS")])
                            if (not isa) and ci == c.NBLK:
                                k.dma(PO, v3(vnS[h_ * 512 + 256:h_ * 512 + 512, :]), s3(n - 256, n), reads=[vtb], writes=[B("vnS")])
                        elif ct < c.o_qn:
                            if ct >= c.o_bg:
                                j = ct - c.o_bg
                                k.op(AC, lambda ps_=ps_, j=j: nc.scalar.activation(out=sg[:, j, 0:n], in_=ps_[:, 0:n], func=ACT.Sigmoid), reads=[pb], writes=[B("sg", j)])
                            else:
                                j = ct - c.o_ba
                                z_, zb = tmp()
                                k.op(DV, lambda z_=z_, ps_=ps_, j=j: nc.vector.tensor_tensor(out=z_[:, 0:n], in0=ps_[:, 0:n], in1=sg[:, j, 0:n], op=ALU.mult),
                                     reads=[pb, B("sg", j)], writes=[zb])
                                zo = 15 if ci == 0 else (15 + CTX + 15 + 15 + t0 - CTX)
                                k.dma(PO, zp[j * P:(j + 1) * P, zo:zo + n], z_[:, 0:n], reads=[zb], writes=[B("zp", j, ci)])
                                if ci == 1:
                                    k.dma(PO, zS[j * P:(j + 1) * P, 0:16], z_[:, 0:16], reads=[zb], writes=[B("zS")])
                                if ci == c.NBLK:
                                    k.dma(PO, zS[j * P:(j + 1) * P, 16:32], z_[:, n - 16:n], reads=[zb], writes=[B("zS")])
                        else:
                            isq = ct < c.o_kn
                            h_ = ct - (c.o_qn if isq else c.o_kn)
                            o_, obb = obuf()
                            if isq:
                                k.op(AC, lambda o_=o_, ps_=ps_: nc.scalar.mul(out=o_[:, 0:n], in_=ps_[:, 0:n], mul=float(P) ** -0.5), reads=[pb], writes=[obb])
                                k.dma(PO, qnT[h_ * P:(h_ + 1) * P, t0:t0 + n], o_[:, 0:n], reads=[obb], writes=[B("qn", h_, ci)])
                            else:
                                k.op(AC, lambda o_=o_, ps_=ps_: nc.scalar.copy(out=o_[:, 0:n], in_=ps_[:, 0:n]), reads=[pb], writes=[obb])
                                k.dma(PO, knT[h_ * P:(h_ + 1) * P, t0:t0 + n], o_[:, 0:n], reads=[obb], writes=[B("kn", h_, ci)])
                                if ci == 1:
                                    k.dma(PO, knS[h_ * P:(h_ + 1) * P, 0:256], o_[:, 0:256], reads=[obb], writes=[B("knS")])
                                if ci == c.NBLK:
                                    k.dma(PO, knS[h_ * P:(h_ + 1) * P, 256:512], o_[:, n - 256:n], reads=[obb], writes=[B("knS")])
                        if l == 0 and ci == 0:
                            ck(f'ct{ct}')
            ck('AB')
            for h_ in range(c.AKV):
                k.allgather(PAIRS, kaS[h_ * P:(h_ + 1) * P, :], kaG[2 * h_ * P:2 * (h_ + 1) * P, :], reads=[B("kaS")], writes=[B("kaG")])
                k.allgather(PAIRS, vaS[h_ * SL:(h_ + 1) * SL, :], vaG[2 * h_ * SL:2 * (h_ + 1) * SL, :], reads=[B("vaS")], writes=[B("vaG")])
            for h_ in range(c.CH):
                k.allgather(PAIRS, knS[h_ * P:(h_ + 1) * P, :], knG[2 * h_ * P:2 * (h_ + 1) * P, :], reads=[B("knS")], writes=[B("knG")])
                k.allgather(PAIRS, vnS[h_ * 512:(h_ + 1) * 512, :], vnG[2 * h_ * 512:2 * (h_ + 1) * 512, :], reads=[B("vnS")], writes=[B("vnG")])
            k.allgather(PAIRS, zS[:, :], zG[:, :], reads=[B("zS")], writes=[B("zG")])
            zl = 15 + CTX + 15
            with Scope() as ss:
                halo_sb = ss("halo_sb", [P, 2, 16])
                for j in range(c.BT):
                    k.dma(SP, halo_sb[:, 0, :], zG[j * P:(j + 1) * P, 16:32], reads=[B("zG")], writes=[B("halo")])
                    k.dma(SP, halo_sb[:, 1, :], zG[c.BCH + j * P:c.BCH + (j + 1) * P, 0:16], reads=[B("zG")], writes=[B("halo")])
                    k.op(DV, lambda: nc.vector.tensor_scalar(out=halo_sb[:, 0, :], in0=halo_sb[:, 0, :], scalar1=flg[:, 4:5], scalar2=None, op0=ALU.mult),
                         reads=[B("halo"), CB], writes=[B("halo")])
                    k.op(DV, lambda: nc.vector.tensor_scalar(out=halo_sb[:, 1, :], in0=halo_sb[:, 1, :], scalar1=flg[:, 5:6], scalar2=None, op0=ALU.mult),
                         reads=[B("halo"), CB], writes=[B("halo")])
                    k.dma(PO, zp[j * P:(j + 1) * P, zl:zl + 15], halo_sb[:, 0, 1:16], reads=[B("halo")], writes=[B("zp", j, "h0")])
                    k.dma(PO, zp[j * P:(j + 1) * P, zl + 15 + SL:zl + 30 + SL], halo_sb[:, 1, 0:15], reads=[B("halo")], writes=[B("zp", j, "h1")])
            ck('C')
            if l + 1 < L:
                weight_setup(l + 1)
            KMAX = CTX + c.SEQ
            with Scope() as ss:
                kT_sb = ss("kT_sb", [P, KMAX], BF16)
                v_sb = ss("v_sb", [P, KMAX // P, P], BF16)
                q_sb = [ss(f"q_sb{i}", [P, 512], BF16) for i in range(2)]
                p_sb = [ss(f"p_sb{i}", [P, 512], BF16) for i in range(3)]
                bias_sb = ss("bias_sb", [P, 8, 512], BF16)
                mask_sb = ss("mask_sb", [P, c.NBLK, 8 * 512], BF16)
                rden = ss("rden", [P, 512])
                obuf = mk_ob(ss)
                for bi in range(c.NBLK):
                    k.dma(SP, mask_sb[:, bi, :], nmask[bi * P:(bi + 1) * P, :], writes=[B("mask_sb")])
                cn = [0, 0]

                def attend(qsrc, qbufs, nq, ktiles, dst_ap, dst_bufs, kb):
                    qs, qb = q_sb[cn[0] % 2], B("q_sb", cn[0] % 2)
                    cn[0] += 1
                    k.dma(SP, qs[:, 0:nq], qsrc, reads=qbufs, writes=[qb])
                    nk = len(ktiles)

                    def qk(i):
                        ko, vi, extra = ktiles[i]
                        ps_, pb = ps_next()
                        k.op(PE, lambda ps_=ps_, ko=ko: nc.tensor.matmul(ps_[:, 0:nq], lhsT=kT_sb[:, ko:ko + P], rhs=qs[:, 0:nq], start=True, stop=(extra is None)),
                             reads=[kb, qb], writes=[pb], inc=(extra is None))
                        if extra is not None:
                            bj, mblk, mj = extra
                            k.op(PE, lambda ps_=ps_, bj=bj: nc.tensor.matmul(ps_[:, 0:nq], lhsT=ident_b[:], rhs=bias_sb[:, bj, 0:nq], start=False, stop=False),
                                 reads=[B("bias_sb"), CB], writes=[pb], pe_acc=True, inc=False)
                            k.op(PE, lambda ps_=ps_, mblk=mblk, mj=mj: nc.tensor.matmul(ps_[:, 0:nq], lhsT=ident_b[:], rhs=mask_sb[:, mblk, mj * 512:mj * 512 + nq],
                                                                                      start=False, stop=True),
                                 reads=[B("mask_sb"), CB], writes=[pb], pe_acc=True)
                        return ps_, pb
                    cur_s = qk(0)
                    for i, (ko, vi, extra) in enumerate(ktiles):
                        nxt_s = qk(i + 1) if i + 1 < nk else None
                        ps_, pb = cur_s
                        pt, ptb = p_sb[cn[1] % 3], B("p_sb", cn[1] % 3)
                        cn[1] += 1
                        k.op(AC, lambda ps_=ps_, pt=pt: nc.scalar.activation(out=pt[:, 0:nq], in_=ps_[:, 0:nq], func=ACT.Exp), reads=[pb], writes=[ptb])
                        k.op(PE, lambda pt=pt, vi=vi, i=i: nc.tensor.matmul(PS_O[0][:, 0:nq], lhsT=v_sb[:, vi, :], rhs=pt[:, 0:nq], start=(i == 0), stop=(i == nk - 1)),
                             reads=[ptb, kb], writes=[PS_O[1]], pe_acc=(i > 0), inc=False)
                        k.op(PE, lambda pt=pt, i=i: nc.tensor.matmul(PS_DEN[0][:, 0:nq], lhsT=ones_b[:], rhs=pt[:, 0:nq], start=(i == 0), stop=(i == nk - 1)),
                             reads=[ptb, CB], writes=[PS_DEN[1]], pe_acc=(i > 0))
                        cur_s = nxt_s
                    k.op(DV, lambda: nc.vector.reciprocal(out=rden[:, 0:nq], in_=PS_DEN[0][:, 0:nq]), reads=[PS_DEN[1]], writes=[B("rden")])
                    o_, obb = obuf()
                    k.op(DV, lambda o_=o_: nc.vector.tensor_tensor(out=o_[:, 0:nq], in0=PS_O[0][:, 0:nq], in1=rden[:, 0:nq], op=ALU.mult),
                         reads=[PS_O[1], B("rden")], writes=[obb])
                    k.dma(PO, dst_ap, o_[:, 0:nq], reads=[obb], writes=dst_bufs)

                KB = B("kv_sb")
                v3 = lambda ap: ap.rearrange("(b p) d -> p b d", p=P)
                for kv in range(c.AKV):
                    k.dma(SP, kT_sb[:, 0:CTX], kaT[kv * P:(kv + 1) * P, 0:CTX], reads=[B("ka", kv, 0)], writes=[KB])
                    k.dma(SP, kT_sb[:, CTX:CTX + SL], kaG[(2 * kv) * P:(2 * kv + 1) * P, :], reads=[B("kaG")], writes=[KB])
                    k.dma(SP, kT_sb[:, CTX + SL:CTX + 2 * SL], kaG[(2 * kv + 1) * P:(2 * kv + 2) * P, :], reads=[B("kaG")], writes=[KB])
                    k.dma(SP, v_sb[:, 0:CTX // P, :], v3(va[kv * T:kv * T + CTX, :]), reads=[B("va", kv, 0)], writes=[KB])
                    for hf in range(2):
                        base = (2 * kv + hf) * SL
                        k.dma(SP, v_sb[:, (CTX + hf * SL) // P:(CTX + (hf + 1) * SL) // P, :], v3(vaG[base:base + SL, :]), reads=[B("vaG")], writes=[KB])
                    for g in range(3):
                        h_ = kv * 3 + g
                        for ci in lat:
                            t0, n = c.chunks[ci]
                            kts = [(i * P, i, None) for i in range(KMAX // P)]
                            attend(qaT[h_ * P:(h_ + 1) * P, t0:t0 + n], [B("qa", h_, ci)], n, kts,
                                   catT[h_ * P:(h_ + 1) * P, t0:t0 + n], [B("cat", h_, ci)], KB)
                        if not last:
                            kts = [(i * P, i, None) for i in range(CTX // P)]
                            attend(qaT[h_ * P:(h_ + 1) * P, 0:CTX], [B("qa", h_, 0)], CTX, kts,
                                   catT[h_ * P:(h_ + 1) * P, 0:CTX], [B("cat", h_, 0)], KB)
                for h_ in range(c.CH):
                    k.dma(SP, kT_sb[:, 0:CTX], knT[h_ * P:(h_ + 1) * P, 0:CTX], reads=[B("kn", h_, 0)], writes=[KB])
                    k.dma(SP, kT_sb[:, CTX:CTX + 256], knG[(2 * h_) * P:(2 * h_ + 1) * P, 256:512], reads=[B("knG")], writes=[KB])
                    k.dma(SP, kT_sb[:, CTX + 256:CTX + 256 + SL], knT[h_ * P:(h_ + 1) * P, CTX:T], reads=[B("kn", h_, ci) for ci in lat], writes=[KB])
                    k.dma(SP, kT_sb[:, CTX + 256 + SL:CTX + 512 + SL], knG[(2 * h_ + 1) * P:(2 * h_ + 2) * P, 0:256], reads=[B("knG")], writes=[KB])
                    k.dma(SP, v_sb[:, 0:2, :], v3(vn[h_ * T:h_ * T + CTX, :]), reads=[B("vn", h_, 0)], writes=[KB])
                    k.dma(SP, v_sb[:, 2:4, :], v3(vnG[(2 * h_) * 512 + 256:(2 * h_) * 512 + 512, :]), reads=[B("vnG")], writes=[KB])
                    k.dma(SP, v_sb[:, 4:4 + SL // P, :], v3(vn[h_ * T + CTX:h_ * T + T, :]), reads=[B("vn", h_, ci) for ci in lat], writes=[KB])
                    k.dma(SP, v_sb[:, 4 + SL // P:6 + SL // P, :], v3(vnG[(2 * h_ + 1) * 512:(2 * h_ + 1) * 512 + 256, :]), reads=[B("vnG")], writes=[KB])
                    tab = rpbT[(l * c.CH + h_) * 64:(l * c.CH + h_ + 1) * 64, :]
                    for j in range(8):
                        for b in range(2):
                            d0 = 2 * j + b
                            k.dma(PO, bias_sb[b * 64:(b + 1) * 64, j, :], tab[:, (15 - d0) * 64:(15 - d0 + 8) * 64], writes=[B("bias_sb")])
                    for bi, ci in enumerate(lat):
                        t0, n = c.chunks[ci]
                        kts = [(i * P, i, None) for i in range(2)]
                        for j in range(8):
                            eo = (8 * bi + 2 * j) * 64
                            kts.append((CTX + eo, 2 + eo // P, (j, bi, j)))
                        cr = c.AH + c.BT + h_
                        attend(qnT[h_ * P:(h_ + 1) * P, t0:t0 + n], [B("qn", h_, ci)], n, kts,
                               catT[cr * P:(cr + 1) * P, t0:t0 + n], [B("cat", cr, ci)], KB)
                    if not last:
                        kts = [(i * P, i, None) for i in range(2)]
                        cr = c.AH + c.BT + h_
                        attend(qnT[h_ * P:(h_ + 1) * P, 0:CTX], [B("qn", h_, 0)], CTX, kts,
                               catT[cr * P:(cr + 1) * P, 0:CTX], [B("cat", cr, 0)], KB)
            ck('D12')
            with Scope() as ss:
                zwin = [ss(f"zwin{i}", [P, 512 + 30]) for i in range(2)]
                yconv = ss("yconv", [P, c.BT, 512])
                ysq = ss("ysq", [P, 512]); mean = ss("mean", [P, 512]); var = ss("var", [P, 512])
                zz = ss("zz", [P, c.BT, 512], BF16)
                wload = mk_w(ss, c.BT)
                obuf = mk_ob(ss)
                for ci, (t0, n) in enumerate(c.chunks):
                    if ci == 0 and last:
                        continue
                    zo = 0 if ci == 0 else (15 + CTX + 15 + t0 - CTX)
                    for j0 in range(0, c.BT, 2):
                        js = [j for j in (j0, j0 + 1) if j < c.BT]
                        zws = {}
                        for j in js:
                            zw, zwb = zwin[j % 2], B("zwin", j % 2)
                            deps = [B("zp", j, "pad"), B("zp", j, "h0"), B("zp", j, "h1")] + [B("zp", j, cj) for cj in range(nch)]
                            k.dma(SP, zw[:, 0:n + 30], zp[j * P:(j + 1) * P, zo:zo + n + 30], reads=deps, writes=[zwb])
                            zws[j] = (zw, zwb)
                        for tap in range(31):
                            for j in js:
                                zw, zwb = zws[j]
                                YB = B("yconv", j)
                                if tap == 0:
                                    k.op(DV, lambda zw=zw, j=j: nc.vector.tensor_scalar(out=yconv[:, j, 0:n], in0=zw[:, 0:n], scalar1=pvo("bdw", j * 31), scalar2=pvo("bdb", j),
                                                                                      op0=ALU.mult, op1=ALU.add), reads=[zwb, B("pv")], writes=[YB])
                                else:
                                    k.op(DV, lambda zw=zw, j=j, tap=tap: nc.vector.scalar_tensor_tensor(out=yconv[:, j, 0:n], in0=zw[:, tap:tap + n], scalar=pvo("bdw", j * 31 + tap),
                                                                                                     in1=yconv[:, j, 0:n], op0=ALU.mult, op1=ALU.add),
                                         reads=[zwb, B("pv"), YB], writes=[YB])
                        for j in js:
                            YB = B("yconv", j)
                            k.op(PE, lambda j=j: nc.tensor.matmul(PS_AUX[0][:, 0:n], lhsT=ones_f[:], rhs=yconv[:, j, 0:n], start=(j == 0), stop=(j == c.BT - 1)),
                                 reads=[YB, CB], writes=[PS_AUX[1]], pe_acc=(j > 0))
                            k.op(DV, lambda j=j: nc.vector.tensor_tensor(out=ysq[:, 0:n], in0=yconv[:, j, 0:n], in1=yconv[:, j, 0:n], op=ALU.mult), reads=[YB, B("ysq")], writes=[B("ysq")])
                            k.op(PE, lambda j=j: nc.tensor.matmul(PS_DEN[0][:, 0:n], lhsT=ones_f[:], rhs=ysq[:, 0:n], start=(j == 0), stop=(j == c.BT - 1)),
                                 reads=[B("ysq"), CB], writes=[PS_DEN[1]], pe_acc=(j > 0))
                    k.op(DV, lambda: nc.vector.tensor_scalar(out=mean[:, 0:n], in0=PS_AUX[0][:, 0:n], scalar1=1.0 / c.BCH, scalar2=None, op0=ALU.mult),
                         reads=[PS_AUX[1]], writes=[B("mean")])
                    k.op(DV, lambda: nc.vector.tensor_scalar(out=var[:, 0:n], in0=PS_DEN[0][:, 0:n], scalar1=1.0 / c.BCH, scalar2=EPS, op0=ALU.mult, op1=ALU.add),
                         reads=[PS_DEN[1]], writes=[B("var")])
                    k.op(DV, lambda: nc.vector.tensor_tensor(out=ysq[:, 0:n], in0=mean[:, 0:n], in1=mean[:, 0:n], op=ALU.mult), reads=[B("mean"), B("ysq")], writes=[B("ysq")])
                    k.op(DV, lambda: nc.vector.tensor_tensor(out=var[:, 0:n], in0=var[:, 0:n], in1=ysq[:, 0:n], op=ALU.subtract), reads=[B("var"), B("ysq")], writes=[B("var")])
                    k.op(AC, lambda: nc.scalar.sqrt(out=var[:, 0:n], in_=var[:, 0:n]), reads=[B("var")], writes=[B("var")])
                    k.op(DV, lambda: nc.vector.reciprocal(out=var[:, 0:n], in_=var[:, 0:n]), reads=[B("var")], writes=[B("var")])
                    for j in range(c.BT):
                        YB = B("yconv", j)
                        k.op(DV, lambda j=j: nc.vector.tensor_tensor(out=yconv[:, j, 0:n], in0=yconv[:, j, 0:n], in1=mean[:, 0:n], op=ALU.subtract),
                             reads=[YB, B("mean")], writes=[YB])
                        k.op(DV, lambda j=j: nc.vector.tensor_tensor(out=yconv[:, j, 0:n], in0=yconv[:, j, 0:n], in1=var[:, 0:n], op=ALU.mult),
                             reads=[YB, B("var")], writes=[YB])
                        k.op(DV, lambda j=j: nc.vector.tensor_scalar(out=yconv[:, j, 0:n], in0=yconv[:, j, 0:n], scalar1=pvo("blg", j), scalar2=pvo("blb", j),
                                                                   op0=ALU.mult, op1=ALU.add), reads=[YB, B("pv")], writes=[YB])
                        k.op(AC, lambda j=j: nc.scalar.activation(out=zz[:, j, 0:n], in_=yconv[:, j, 0:n], func=ACT.Silu), reads=[YB, B("zz")], writes=[B("zz")])
                    for j in range(c.BT):
                        w_, wbb = wload("wpw", l, j, c.BT)
                        ps_, pb = ps_next()
                        for kt in range(c.BT):
                            k.op(PE, lambda ps_=ps_, w_=w_, kt=kt: nc.tensor.matmul(ps_[:, 0:n], lhsT=w_[:, kt * P:(kt + 1) * P], rhs=zz[:, kt, 0:n],
                                                                                  start=(kt == 0), stop=(kt == c.BT - 1)),
                                 reads=[wbb, B("zz")], writes=[pb], pe_acc=(kt > 0), inc=(kt == c.BT - 1))
                        o_, obb = obuf()
                        k.op(AC, lambda o_=o_, ps_=ps_, j=j: nc.scalar.activation(out=o_[:, 0:n], in_=ps_[:, 0:n], func=ACT.Identity, bias=pvo("bpb", j), scale=1.0),
                             reads=[pb, B("pv")], writes=[obb])
                        k.dma(PO, catT[(c.AH + j) * P:(c.AH + j + 1) * P, t0:t0 + n], o_[:, 0:n], reads=[obb], writes=[B("cat", c.AH + j, ci)])
            ck('D3')
            with Scope() as ss:
                cat_sb = ss("cat_sb", [P, NT, 512], BF16)
                wload = mk_w(ss, NT)
                xres = [ss(f"xres{i}", [P, 512]) for i in range(2)]
                xnew = [ss(f"xnew{i}", [P, 512]) for i in range(2)]
                xe_sb = ss("xe_sb", [P, 2 * NT])
                for ci, (t0, n) in enumerate(c.chunks):
                    if ci == 0 and last:
                        continue
                    v = 1 if ci == 0 else 0
                    k.dma(SP, cat_sb[:, :, 0:n], catT[:, t0:t0 + n].rearrange("(t p) n -> p t n", p=P), reads=[B("cat", t, ci) for t in range(NT)], writes=[B("cat_sb")])
                    for ct in range(NT):
                        w_, wbb = wload("wout", l, ct, NT)
                        ps_, pb = ps_next()
                        for kt in range(NT):
                            k.op(PE, lambda ps_=ps_, w_=w_, kt=kt: nc.tensor.matmul(ps_[:, 0:n], lhsT=w_[:, kt * P:(kt + 1) * P], rhs=cat_sb[:, kt, 0:n],
                                                                                  start=(kt == 0), stop=(kt == NT - 1)),
                                 reads=[wbb, B("cat_sb")], writes=[pb], pe_acc=(kt > 0), inc=(kt == NT - 1))
                        xr, xrb = xres[ct % 2], B("xres", ct % 2)
                        k.dma(SP, xr[:, 0:n], X[ct * P:(ct + 1) * P, t0:t0 + n], reads=[B(XN, ci, ct)], writes=[xrb])
                        xn, xnb = xnew[ct % 2], B("xnew", ct % 2)
                        k.op(DV, lambda xn=xn, ps_=ps_, xr=xr, ct=ct: nc.vector.scalar_tensor_tensor(out=xn[:, 0:n], in0=ps_[:, 0:n], scalar=mod(l, v, 2, ct), in1=xr[:, 0:n],
                                                                                                  op0=ALU.mult, op1=ALU.add), reads=[pb, xrb, MS], writes=[xnb])
                        k.dma(PO, X[ct * P:(ct + 1) * P, t0:t0 + n], xn[:, 0:n], reads=[xnb], writes=[B(XN, ci, ct)])
                        if ci == 1:
                            k.op(DV, lambda xn=xn, ct=ct: nc.vector.tensor_copy(out=xe_sb[:, 2 * ct:2 * ct + 1], in_=xn[:, 0:1]), reads=[xnb, B("xe_sb")], writes=[B("xe_sb")])
                        if ci == c.NBLK:
                            k.op(DV, lambda xn=xn, ct=ct: nc.vector.tensor_copy(out=xe_sb[:, 2 * ct + 1:2 * ct + 2], in_=xn[:, n - 1:n]), reads=[xnb, B("xe_sb")], writes=[B("xe_sb")])
                k.dma(PO, xeS[:, :], xe_sb[:], reads=[B("xe_sb")], writes=[B("xeS")])
            k.allgather(PAIRS, xeS[:, :], xeG[:, :], reads=[B("xeS")], writes=[B("xeG")])
            ck('E')
            Y, YN = nxt
            with Scope() as ss:
                wk = mk_norm_bufs(ss)
                tmp = wk[3]
                hsb = ss("hsb", [P, NT, 514], BF16)
                wload = mk_w(ss, max(NT, c.FT), nb=3)
                U = [ss(f"U{i}", [P, 514]) for i in range(2)]
                act_sb = ss("act_sb", [P, c.FT, 512], BF16)
                gsil = ss("gsil", [P, 512])
                xres = [ss(f"xres{i}", [P, 512]) for i in range(2)]
                xnew = [ss(f"xnew{i}", [P, 512]) for i in range(2)]

                def xepiece(r, e):
                    return (lambda gi, G: xeG[r * P:(r + 1) * P, :].rearrange("p (t e) -> p t e", e=2)[:, gi:gi + G, e:e + 1],
                            lambda gi, G: [B("xeG")], 1)
                for ci, (t0, n) in enumerate(c.chunks):
                    if ci == 0 and last:
                        continue
                    v = 1 if ci == 0 else 0
                    W = n + 2
                    first_lat, last_lat = ci == 1, ci == c.NBLK
                    if ci == 0:
                        left, right = xpiece(X, XN, ci, t0, t0 + 1), xpiece(X, XN, ci, t0 + n - 1, t0 + n)
                    else:
                        left = xepiece(0, 1) if first_lat else xpiece(X, XN, ci - 1, t0 - 1, t0)
                        right = xepiece(1, 0) if last_lat else xpiece(X, XN, ci + 1, t0 + n, t0 + n + 1)
                    norm_mod(ss, [left, xpiece(X, XN, ci, t0, t0 + n), right], W, l, v, 4, 3, hsb, wk)
                    HB = B("hsb")
                    hw = W // 2
                    for ft in range(c.FT):
                        for part in range(2):
                            ct = ft + part * c.FT
                            w_, wbb = wload("wup", l, ct, NT)
                            pA, pAb = ps_next()
                            pB, pBb = ps_next()
                            for (pp, ppb, a) in [(pA, pAb, 0), (pB, pBb, hw)]:
                                for kt in range(NT):
                                    k.op(PE, lambda pp=pp, w_=w_, kt=kt, a=a: nc.tensor.matmul(pp[:, 0:hw], lhsT=w_[:, kt * P:(kt + 1) * P], rhs=hsb[:, kt, a:a + hw],
                                                                                             start=(kt == 0), stop=(kt == NT - 1)),
                                         reads=[wbb, HB], writes=[ppb], pe_acc=(kt > 0), inc=(kt == NT - 1))
                            u_, ub = U[part], B("U", part)
                            k.op(AC, lambda u_=u_, pA=pA: nc.scalar.copy(out=u_[:, 0:hw], in_=pA[:, 0:hw]), reads=[pAb], writes=[ub])
                            k.op(AC, lambda u_=u_, pB=pB: nc.scalar.copy(out=u_[:, hw:W], in_=pB[:, 0:hw]), reads=[pBb, ub], writes=[ub])
                            for (colx, isl) in [(0, True), (W - 1, False)]:
                                if ci == 0:
                                    k.op(DV, lambda u_=u_, colx=colx: nc.vector.memset(u_[:, colx:colx + 1], 0.0), reads=[ub], writes=[ub])
                                elif (isl and first_lat) or ((not isl) and last_lat):
                                    fcol = 4 if isl else 5
                                    k.op(DV, lambda u_=u_, colx=colx, fcol=fcol: nc.vector.tensor_scalar(out=u_[:, colx:colx + 1], in0=u_[:, colx:colx + 1],
                                                                                                       scalar1=flg[:, fcol:fcol + 1], scalar2=None, op0=ALU.mult),
                                         reads=[ub, CB], writes=[ub])
                            cv, cvb = tmp()
                            k.op(DV, lambda cv=cv, u_=u_, ct=ct: nc.vector.tensor_scalar(out=cv[:, 0:n], in0=u_[:, 1:n + 1], scalar1=pvo("fdw", ct * 3 + 1), scalar2=pvo("fdb", ct),
                                                                                      op0=ALU.mult, op1=ALU.add), reads=[ub, B("pv")], writes=[cvb])
                            k.op(DV, lambda cv=cv, u_=u_, ct=ct: nc.vector.scalar_tensor_tensor(out=cv[:, 0:n], in0=u_[:, 0:n], scalar=pvo("fdw", ct * 3), in1=cv[:, 0:n],
                                                                                             op0=ALU.mult, op1=ALU.add), reads=[ub, B("pv"), cvb], writes=[cvb])
                            k.op(DV, lambda cv=cv, u_=u_, ct=ct: nc.vector.scalar_tensor_tensor(out=cv[:, 0:n], in0=u_[:, 2:n + 2], scalar=pvo("fdw", ct * 3 + 2), in1=cv[:, 0:n],
                                                                                             op0=ALU.mult, op1=ALU.add), reads=[ub, B("pv"), cvb], writes=[cvb])
                            if part == 0:
                                k.op(AC, lambda cv=cv: nc.scalar.activation(out=gsil[:, 0:n], in_=cv[:, 0:n], func=ACT.Silu), reads=[cvb], writes=[B("gsil")])
                            else:
                                k.op(DV, lambda cv=cv, ft=ft: nc.vector.tensor_tensor(out=act_sb[:, ft, 0:n], in0=cv[:, 0:n], in1=gsil[:, 0:n], op=ALU.mult),
                                     reads=[cvb, B("gsil"), B("act_sb")], writes=[B("act_sb")])
                    for ct in range(NT):
                        w_, wbb = wload("wdn", l, ct, c.FT)
                        ps_, pb = ps_next()
                        for kt in range(c.FT):
                            k.op(PE, lambda ps_=ps_, w_=w_, kt=kt: nc.tensor.matmul(ps_[:, 0:n], lhsT=w_[:, kt * P:(kt + 1) * P], rhs=act_sb[:, kt, 0:n],
                                                                                  start=(kt == 0), stop=(kt == c.FT - 1)),
                                 reads=[wbb, B("act_sb")], writes=[pb], pe_acc=(kt > 0), inc=(kt == c.FT - 1))
                        xr, xrb = xres[ct % 2], B("xres", ct % 2)
                        k.dma(SP, xr[:, 0:n], X[ct * P:(ct + 1) * P, t0:t0 + n], reads=[B(XN, ci, ct)], writes=[xrb])
                        xn, xnb = xnew[ct % 2], B("xnew", ct % 2)
                        k.op(DV, lambda xn=xn, ps_=ps_, xr=xr, ct=ct: nc.vector.scalar_tensor_tensor(out=xn[:, 0:n], in0=ps_[:, 0:n], scalar=mod(l, v, 5, ct), in1=xr[:, 0:n],
                                                                                                  op0=ALU.mult, op1=ALU.add), reads=[pb, xrb, MS], writes=[xnb])
                        k.dma(PO, Y[ct * P:(ct + 1) * P, t0:t0 + n], xn[:, 0:n], reads=[xnb], writes=[B(YN, ci, ct)])
            cur, nxt = nxt, cur

        X, XN = cur
        with Scope() as ss:
            wk = mk_norm_bufs(ss)
            tmp = wk[3]
            for ci in lat:
                t0, n = c.chunks[ci]

                def ydst(t, tm, tb, t0=t0, n=n):
                    o_, ob_ = tmp()
                    k.op(DV, lambda: nc.vector.tensor_scalar(out=o_[:, 0:n], in0=tm[:, 0:n], scalar1=fin_g[:, t:t + 1], scalar2=None, op0=ALU.mult),
                         reads=[tb, CB], writes=[ob_])
                    k.dma(PO, yout[t * P:(t + 1) * P, t0 - CTX:t0 - CTX + n], o_[:, 0:n], reads=[ob_], writes=[B("yout")])
                norm_mod(ss, [xpiece(X, XN, ci, t0, t0 + n)], n, 0, 0, 0, 0, None, wk, ydst=ydst)
        k.barrier()
    return nc


def _fm(vec, nt):
    return np.ascontiguousarray(vec.reshape(nt, P).T)


def _tile_major(w):
    K_, N_ = w.shape
    return np.ascontiguousarray(w.reshape(K_ // P, P, N_ // P, P).transpose(2, 1, 0, 3)).reshape(-1)


def rope_tables(cfg, half):
    c = cfg
    t = np.arange(c.SL, dtype=np.int64) + half * c.SL
    row = (t // c.GW).astype(np.float32)
    col = (t % c.GW).astype(np.float32)
    hh = P // 2
    inv = (np.float32(10000.0) ** (-np.arange(0, hh, 2, dtype=np.float32) / np.float32(hh))).astype(np.float32)
    ang = np.concatenate([row[:, None] * inv, col[:, None] * inv], axis=-1).astype(np.float32)
    cs, sn = np.cos(ang).astype(np.float32), np.sin(ang).astype(np.float32)
    C = np.ones((P, c.T), np.float32)
    S = np.zeros((P, c.T), np.float32)
    C[:, c.CTX:] = np.repeat(cs.T, 2, axis=0)
    S[:, c.CTX:] = np.repeat(sn.T, 2, axis=0)
    return C, S


def nbr_mask(cfg, half):
    c = cfg
    base = half * c.RL
    m = np.full((c.NBLK, 2, 64, 8, 8, 64), NEG, np.float32)
    wq = np.arange(64)
    cstart = np.clip(wq - 8, 0, 64 - 16)
    wk = np.arange(64)
    colok = (wk[:, None] >= cstart[None, :]) & (wk[:, None] < cstart[None, :] + 16)
    for bi in range(c.NBLK):
        for a in range(8):
            r = base + 8 * bi + a
            r0 = int(np.clip(r - 4, 0, c.ROWS - 8))
            for j in range(8):
                for b in range(2):
                    rk = base + 8 * bi + 2 * j + b - 4
                    if r0 <= rk < r0 + 8:
                        m[bi, b, :, j, a, :] = np.where(colok, 0.0, NEG)
    return m.reshape(c.NBLK * P, 8 * 512).astype(ml_dtypes.bfloat16)


def prep_inputs(cfg, inp):
    c = cfg
    L, D, NT = c.L, c.D, c.NT
    f = lambda a: np.asarray(a, dtype=np.float32)
    x, cc, ctx, c_ctx = f(inp["x"]), f(inp["c"]), f(inp["ctx"]), f(inp["c_ctx"])
    consts = np.zeros((P, 3 * P), np.float32)
    consts[:, 0:P] = np.eye(P, dtype=np.float32)
    for i in range(P // 2):
        consts[2 * i + 1, P + 2 * i] = -1.0
        consts[2 * i, P + 2 * i + 1] = 1.0
    consts[:, 2 * P:] = 1.0
    c5 = np.concatenate([cc, c_ctx[None]], 0)
    c5T = np.ascontiguousarray(c5.reshape(5, NT, P).transpose(2, 1, 0)).reshape(P, NT * 5)
    pvec = np.zeros((L, P, c.NP), np.float32)
    for l in range(L):
        pvec[l, :, c.pv["n1g"]:c.pv["n1g"] + NT] = _fm(f(inp["norm1_g"])[l], NT)
        pvec[l, :, c.pv["n2g"]:c.pv["n2g"] + NT] = _fm(f(inp["norm2_g"])[l], NT)
        pvec[l, :, c.pv["qg"]] = f(inp["a_qn_g"])[l]
        pvec[l, :, c.pv["kg"]] = f(inp["a_kn_g"])[l]
        bd = f(inp["b_dw_w"])[l]
        pvec[l, :, c.pv["bdw"]:c.pv["bdw"] + c.BT * 31] = bd.reshape(31, c.BT, P).transpose(2, 1, 0).reshape(P, c.BT * 31)
        for nm, key in [("bdb", "b_dw_b"), ("blg", "b_ln_g"), ("blb", "b_ln_b"), ("bpb", "b_pw_b")]:
            pvec[l, :, c.pv[nm]:c.pv[nm] + c.BT] = _fm(f(inp[key])[l], c.BT)
        fd = f(inp["ffn_dw_w"])[l]
        pvec[l, :, c.pv["fdw"]:c.pv["fdw"] + 2 * c.FT * 3] = fd.reshape(3, 2 * c.FT, P).transpose(2, 1, 0).reshape(P, 2 * c.FT * 3)
        pvec[l, :, c.pv["fdb"]:c.pv["fdb"] + 2 * c.FT] = _fm(f(inp["ffn_dw_b"])[l], 2 * c.FT)
    pvec = pvec.reshape(L * P, c.NP)
    fing = _fm(f(inp["final_g"]), NT)
    rpb = f(inp["c_rpb"])
    wk, wq = np.arange(64)[:, None], np.arange(64)[None, :]
    cidx = np.clip(wk - wq + 15, 0, 30)
    tab = np.zeros((L, c.CH, 64, 23, 64), np.float32)
    for dp in range(23):
        dl = 11 - dp
        if -7 <= dl <= 7:
            tab[:, :, :, dp, :] = rpb[:, :, dl + 7, :][:, :, cidx]
    rpbT = tab.reshape(L * c.CH * 64, 23 * 64)
    wflat = {}
    for n, key in [("win", "w_in"), ("wout", "w_out"), ("wup", "ffn_w_up"), ("wdn", "ffn_w_down"), ("wpw", "b_pw_w")]:
        w = f(inp[key])
        rows = w.shape[1] * w.shape[2] // 8 // 1024
        blk = wblk(rows)
        wflat[n] = np.stack([_tile_major(w[l]).reshape(rows // blk, 8, blk * 1024).transpose(1, 0, 2).reshape(8, -1) for l in range(L)], 0)
    ada_w, ada_b = f(inp["ada_w"]), f(inp["ada_b"])
    maps = []
    for r in range(8):
        b, half = r // 2, r % 2
        m = {}
        xT = np.empty((D, c.T), np.float32)
        xT[:, :c.CTX] = ctx[b].T
        xT[:, c.CTX:] = x[b, half * c.SL:(half + 1) * c.SL].T
        m["xin"] = xT
        m["c5T"] = c5T
        m["ada_s"] = np.ascontiguousarray(ada_w[:, :, r * c.MC:(r + 1) * c.MC]).reshape(L * D, c.MC)
        m["adab5"] = np.ascontiguousarray(np.broadcast_to(ada_b[None, :, r * c.MC:(r + 1) * c.MC], (5, L, c.MC))).reshape(5, L * c.MC)
        fl = np.zeros((P, 8), np.float32)
        fl[:, b] = 1.0
        fl[:, 4] = 1.0 if half == 1 else 0.0
        fl[:, 5] = 1.0 if half == 0 else 0.0
        m["flags"] = fl
        m["pvec"] = pvec
        m["fing"] = fing
        C_, S_ = rope_tables(c, half)
        m["cosT"], m["sinT"] = C_, S_
        m["consts"] = consts
        m["nmask"] = nbr_mask(c, half)
        m["rpbT"] = rpbT
        for n in wflat:
            m[n + "_s"] = np.ascontiguousarray(wflat[n][:, r, :]).reshape(-1, 1024)
        maps.append(m)
    return maps


_NC_CACHE = {}


def run(cfg, inp):
    key = (cfg.D, cfg.SEQ, cfg.L)
    if key not in _NC_CACHE:
        _NC_CACHE[key] = build(cfg)
    nc = _NC_CACHE[key]
    maps = prep_inputs(cfg, inp)
    res = run_bass_kernel_spmd(nc, maps, core_ids=list(range(8)))
    out = np.empty((cfg.B, cfg.SEQ, cfg.D), np.float32)
    for r in range(8):
        b, half = r // 2, r % 2
        out[b, half * cfg.SL:(half + 1) * cfg.SL, :] = res.results[r]["yout"].T
    return out


def kernel(**inputs):
    return run(Cfg(), inputs)
```
